# Optimizing a Trainium2 kernel written in Bass

```python
import jax, jax.numpy as jnp
from jax import lax
import numpy as np

D_MODEL = 1024
BATCH = 4
SEQ = 4096
DEPTH = 2

GRID_W = 64
CTX_LEN = 256
EPS = 1e-6
N_BRANCH = 4
BRANCH_W = D_MODEL // 4
D_FF = 4 * D_MODEL

POOL_GROUPS = 4
POOL_GC = BRANCH_W // POOL_GROUPS
POOL_WINDOWS = (2, 4, 8, 16)
POOL_W = POOL_GROUPS * POOL_GC

MLA_HEADS = 4
MLA_Q_RANK = D_MODEL // 4
MLA_KV_RANK = D_MODEL // 8
MLA_NOPE = 64
MLA_ROPE = 32
MLA_V = 64
MLA_QK = MLA_NOPE + MLA_ROPE
ROPE_BASE = 10000.0
Q_BLOCK = 128

NA_HEADS = 4
NA_HD = 64
NA_WIN_R = 8
NA_WIN_C = 16

HG_HEADS = 4
HG_DK = 64
HG_DV = 64
HG_CHUNK = 64
LB_EPS = 1e-6

KV_SIZES = (MLA_KV_RANK + MLA_ROPE, NA_HEADS * NA_HD, NA_HEADS * NA_HD,
            HG_HEADS * HG_DK, HG_HEADS * HG_DK, HG_HEADS * HG_DV)
Q_SIZES = (POOL_W, MLA_Q_RANK, NA_HEADS * NA_HD, HG_HEADS * HG_DK, HG_HEADS * HG_DV,
           N_BRANCH * D_MODEL)
KV_COLS = sum(KV_SIZES)
D_IN = KV_COLS + sum(Q_SIZES)

kernel_name = 'hybrid_dit_pool_mla_natten_hgrn2'


def rmsnorm(x, g):
    xf = x.astype(jnp.float32)
    y = xf * lax.rsqrt(jnp.mean(xf * xf, axis=-1, keepdims=True) + EPS)
    return (y * g.astype(jnp.float32)).astype(x.dtype)


def modulate(x, g, shift, scale):
    return rmsnorm(x, g) * (1 + scale) + shift


def split_cols(u, sizes):
    return jnp.split(u, [int(s) for s in np.cumsum(sizes)[:-1]], axis=-1)


def split_heads(u, n_heads, g=None):
    B, L, _ = u.shape
    h = u.reshape(B, L, n_heads, -1)
    if g is not None:
        h = rmsnorm(h, g)
    return h.transpose(0, 2, 1, 3)


def merge_heads(o):
    B, H, L, d = o.shape
    return o.transpose(0, 2, 1, 3).reshape(B, L, H * d)


def axial_rope_tables(n_tok):
    t = jnp.arange(n_tok)
    pos = jnp.stack([t // GRID_W, t % GRID_W], axis=-1).astype(jnp.float32)
    nf = MLA_ROPE // 4
    inv = ROPE_BASE ** (-jnp.arange(nf, dtype=jnp.float32) / nf)
    ang = pos[:, :, None] * inv
    return jnp.cos(ang), jnp.sin(ang)


def apply_axial_rope(x, cos, sin):
    sh = x.shape
    xs = x.reshape(sh[:-1] + (2, 2, MLA_ROPE // 4)).astype(jnp.float32)
    x1, x2 = xs[..., 0, :], xs[..., 1, :]
    out = jnp.stack([x1 * cos - x2 * sin, x2 * cos + x1 * sin], axis=-2)
    return out.reshape(sh).astype(x.dtype)


def rotate_tail(x, rope):
    return jnp.concatenate([x[..., :MLA_NOPE], apply_axial_rope(x[..., MLA_NOPE:], rope[0], rope[1])], axis=-1)


def mla_queries(qa, q_norm, w_qb, gq, rope):
    B, L, _ = qa.shape
    q = (rmsnorm(qa, q_norm) @ w_qb).reshape(B, L, MLA_HEADS, MLA_QK)
    q = rmsnorm(q, gq).transpose(0, 2, 1, 3)
    if rope is not None:
        q = rotate_tail(q, rope)
    return q


def mla_keys_values(kva, kv_norm, w_kvb, gk, rope):
    B, L, _ = kva.shape
    ckv, k_rope = kva[..., :MLA_KV_RANK], kva[..., MLA_KV_RANK:]
    kv = (rmsnorm(ckv, kv_norm) @ w_kvb).reshape(B, L, MLA_HEADS, MLA_NOPE + MLA_V)
    k = jnp.concatenate([kv[..., :MLA_NOPE],
                         jnp.broadcast_to(k_rope[:, :, None, :], (B, L, MLA_HEADS, MLA_ROPE))], axis=-1)
    k = rmsnorm(k, gk).transpose(0, 2, 1, 3)
    v = kv[..., MLA_NOPE:].transpose(0, 2, 1, 3)
    if rope is not None:
        k = rotate_tail(k, rope)
    return k, v


def dense_attention(q, k, v, scale):
    B, H, L, dq = q.shape
    nb = L // Q_BLOCK
    qb = q.reshape(B, H, nb, Q_BLOCK, dq).transpose(2, 0, 1, 3, 4)

    def one_block(qi):
        s = jnp.einsum('bhqd,bhkd->bhqk', qi, k).astype(jnp.float32) * scale
        pr = jax.nn.softmax(s, axis=-1).astype(v.dtype)
        return jnp.einsum('bhqk,bhkd->bhqd', pr, v)

    o = lax.map(one_block, qb)
    return o.transpose(1, 2, 0, 3, 4).reshape(B, H, L, v.shape[-1])


def na_tables(rows, rpb):
    wr = min(NA_WIN_R, rows)
    r = jnp.arange(rows)
    col = jnp.arange(GRID_W)
    rs = jnp.clip(r - NA_WIN_R // 2, 0, rows - wr)
    cs = jnp.clip(col - NA_WIN_C // 2, 0, GRID_W - NA_WIN_C)
    key_r = rs[:, None] + jnp.arange(wr)
    key_c = cs[:, None] + jnp.arange(NA_WIN_C)
    idx = (key_r[:, None, :, None] * GRID_W + key_c[None, :, None, :]).reshape(rows, GRID_W, wr * NA_WIN_C)
    dr = key_r - r[:, None] + NA_WIN_R - 1
    dc = key_c - col[:, None] + NA_WIN_C - 1
    bias = rpb[:, dr[:, None, :, None], dc[None, :, None, :]].reshape(NA_HEADS, rows, GRID_W, wr * NA_WIN_C)
    return idx, bias.astype(jnp.float32)


def na_attention(q, k, v, k_ctx, v_ctx, rpb):
    B, H, L, d = q.shape
    rows = L // GRID_W
    idx, bias = na_tables(rows, rpb)
    n_loc = idx.shape[-1]
    scale = NA_HD ** -0.5
    qr = q.reshape(B, H, rows, GRID_W, d).transpose(2, 0, 1, 3, 4)

    def one_row(args):
        qi, ii, bi = args
        k_loc = k[:, :, ii]
        v_loc = v[:, :, ii]
        s_loc = jnp.einsum('bhqd,bhqnd->bhqn', qi, k_loc).astype(jnp.float32) * scale + bi
        s_ctx = jnp.einsum('bhqd,bhkd->bhqk', qi, k_ctx).astype(jnp.float32) * scale
        pr = jax.nn.softmax(jnp.concatenate([s_loc, s_ctx], axis=-1), axis=-1).astype(v.dtype)
        return (jnp.einsum('bhqn,bhqnd->bhqd', pr[..., :n_loc], v_loc)
                + jnp.einsum('bhqk,bhkd->bhqd', pr[..., n_loc:], v_ctx))

    o = lax.map(one_row, (qr, idx, bias.transpose(1, 0, 2, 3)))
    return o.transpose(1, 2, 0, 3, 4).reshape(B, H, L, d)


def pool_mix(u, w_grp, scale):
    B, L, _ = u.shape
    uf = u.astype(jnp.float32)
    csum = jnp.concatenate([jnp.zeros((B, 1, POOL_W), jnp.float32), jnp.cumsum(uf, axis=1)], axis=1)
    t = jnp.arange(L)
    groups = []
    for gi, w in enumerate(POOL_WINDOWS):
        lo = jnp.clip(t - w // 2, 0, L)
        hi = jnp.clip(t - w // 2 + w, 0, L)
        cg = csum[..., gi * POOL_GC:(gi + 1) * POOL_GC]
        cnt = (hi - lo).astype(jnp.float32)[None, :, None]
        groups.append((cg[:, hi] - cg[:, lo]) / cnt - uf[..., gi * POOL_GC:(gi + 1) * POOL_GC])
    p = jnp.stack(groups, axis=2)
    y = jnp.einsum('blgc,gcd->blgd', p, w_grp.astype(jnp.float32)).reshape(B, L, POOL_W)
    return (y * scale.astype(jnp.float32)).astype(u.dtype)


def hgrn_gates(z, lb):
    logf = jnp.logaddexp(jax.nn.log_sigmoid(z), jnp.log(lb) + jax.nn.log_sigmoid(-z))
    k = (1 - lb) * jax.nn.sigmoid(-z)
    return logf, k


def hgrn_scan(q, k, logf, i, s0):
    B, H, L, _ = k.shape
    nc = L // HG_CHUNK

    def chunks(a):
        return a.reshape(B, H, nc, HG_CHUNK, a.shape[-1]).transpose(2, 0, 1, 3, 4)

    lower_tri = jnp.tril(jnp.ones((HG_CHUNK, HG_CHUNK), dtype=bool))[:, :, None]

    def advance(S, b, kc, ic):
        bl = b[:, :, -1]
        return jnp.exp(bl)[..., None] * S + jnp.einsum('bhsd,bhsv->bhdv', kc * jnp.exp(bl[:, :, None] - b), ic)

    if q is None:
        def state_step(S, xs):
            kc, fc, ic = xs
            return advance(S, jnp.cumsum(fc, axis=2), kc, ic), None
        S_fin, _ = lax.scan(state_step, s0, (chunks(k), chunks(logf), chunks(i)))
        return None, S_fin

    def step(S, xs):
        qc, kc, fc, ic = xs
        b = jnp.cumsum(fc, axis=2)
        diff = b[:, :, :, None, :] - b[:, :, None, :, :]
        dec = jnp.where(lower_tri, jnp.exp(jnp.minimum(diff, 0.0)), 0.0)
        a = jnp.einsum('bhtd,bhsd,bhtsd->bhts', qc, kc, dec)
        o = jnp.einsum('bhts,bhsv->bhtv', a, ic) + jnp.einsum('bhtd,bhdv->bhtv', qc * jnp.exp(b), S)
        return advance(S, b, kc, ic), o

    S_fin, o = lax.scan(step, s0, (chunks(q), chunks(k), chunks(logf), chunks(i)))
    return o.transpose(1, 2, 0, 3, 4).reshape(B, H, L, -1), S_fin


def hgrn_inputs(kv, lbF, lbB):
    zF = split_heads(kv[3], HG_HEADS).astype(jnp.float32)
    zB = split_heads(kv[4], HG_HEADS).astype(jnp.float32)
    fF, kF = hgrn_gates(zF, lbF)
    fB, kB = hgrn_gates(zB, lbB)
    i = split_heads(kv[5], HG_HEADS).astype(jnp.float32)
    return fF, kF, fB, kB, i


def hgrn_readout(o, g_cols, g_norm):
    y = merge_heads(rmsnorm(o, g_norm))
    return (y * jax.nn.sigmoid(g_cols.astype(jnp.float32))).astype(g_cols.dtype)


def flip(a):
    return jnp.flip(a, axis=2)


def merge_branches(branches, gate_cols, w_br, w_o):
    B, L, _ = gate_cols.shape
    gates = jax.nn.sigmoid(gate_cols.reshape(B, L, N_BRANCH, D_MODEL))
    y = jnp.stack(branches, axis=2)
    proj = jnp.einsum('blnw,nwd->blnd', y, w_br)
    return jnp.einsum('bld,de->ble', jnp.sum(gates * proj, axis=2), w_o)


def sqrelu_mlp(h, w1, w2):
    return jnp.square(jax.nn.relu(h @ w1)) @ w2


def setup_inputs(seed: int = 0) -> dict:
    key = jax.random.key(seed)
    ks = jax.random.split(key, 32)
    D = D_MODEL

    def nrm(k, shape, s):
        return jax.random.normal(k, shape, jnp.float32) * s

    def gain(k, shape):
        return 1.0 + 0.05 * jax.random.normal(k, shape, jnp.float32)

    return {
        'x': nrm(ks[0], (BATCH, SEQ, D), 1.0),
        'c': nrm(ks[1], (BATCH, D), 1.0),
        'ctx': nrm(ks[2], (BATCH, CTX_LEN, D), 1.0),
        'c_ctx': nrm(ks[3], (D,), 1.0),
        'ada_w': nrm(ks[4], (DEPTH, D, 6 * D), D ** -0.5),
        'ada_b': nrm(ks[5], (DEPTH, 6 * D), 0.02),
        'norm1': gain(ks[6], (DEPTH, D)),
        'norm2': gain(ks[7], (DEPTH, D)),
        'w_in': nrm(ks[8], (DEPTH, D, D_IN), D ** -0.5),
        'pool_w': nrm(ks[9], (DEPTH, POOL_GROUPS, POOL_GC, POOL_GC), POOL_GC ** -0.5),
        'pool_scale': gain(ks[10], (DEPTH, POOL_W)),
        'mla_q_norm': gain(ks[11], (DEPTH, MLA_Q_RANK)),
        'mla_wq_b': nrm(ks[12], (DEPTH, MLA_Q_RANK, MLA_HEADS * MLA_QK), MLA_Q_RANK ** -0.5),
        'mla_kv_norm': gain(ks[13], (DEPTH, MLA_KV_RANK)),
        'mla_wkv_b': nrm(ks[14], (DEPTH, MLA_KV_RANK, MLA_HEADS * (MLA_NOPE + MLA_V)), MLA_KV_RANK ** -0.5),
        'mla_gq': gain(ks[15], (DEPTH, MLA_QK)),
        'mla_gk': gain(ks[16], (DEPTH, MLA_QK)),
        'na_gq': gain(ks[17], (DEPTH, NA_HD)),
        'na_gk': gain(ks[18], (DEPTH, NA_HD)),
        'na_rpb': nrm(ks[19], (DEPTH, NA_HEADS, 2 * NA_WIN_R - 1, 2 * NA_WIN_C - 1), 0.1),
        'hg_lb': nrm(ks[20], (DEPTH + 1, 2, HG_HEADS * HG_DK), 0.5),
        'hg_norm': gain(ks[21], (DEPTH, HG_DV)),
        'w_branch': nrm(ks[22], (DEPTH, N_BRANCH, BRANCH_W, D), BRANCH_W ** -0.5),
        'w_out': nrm(ks[23], (DEPTH, D, D), D ** -0.5),
        'w_ff1': nrm(ks[24], (DEPTH, D, D_FF), D ** -0.5),
        'w_ff2': nrm(ks[25], (DEPTH, D_FF, D), D_FF ** -0.5),
    }


def reference(x, c, ctx, c_ctx, ada_w, ada_b, norm1, norm2, w_in, pool_w, pool_scale,
              mla_q_norm, mla_wq_b, mla_kv_norm, mla_wkv_b, mla_gq, mla_gk,
              na_gq, na_gk, na_rpb, hg_lb, hg_norm, w_branch, w_out, w_ff1, w_ff2):
    B, L, D = x.shape
    rope = axial_rope_tables(L)
    lb_p = jax.nn.softmax(hg_lb.astype(jnp.float32), axis=0)
    lb_all = jnp.clip(jnp.cumsum(lb_p, axis=0)[:DEPTH], LB_EPS, 1.0 - LB_EPS)
    s_lat = jax.nn.silu(c)
    s_ctx = jax.nn.silu(c_ctx)
    mla_scale = MLA_QK ** -0.5
    hg_qscale = HG_DK ** -0.5
    xl, xc = x, ctx
    for l in range(DEPTH):
        ctx_out = l < DEPTH - 1
        mod_l = (s_lat @ ada_w[l] + ada_b[l])[:, None, :]
        sh1, sc1, g1, sh2, sc2, g2 = jnp.split(mod_l, 6, axis=-1)
        n_mod = 6 if ctx_out else 2
        mods_c = jnp.split(s_ctx @ ada_w[l][:, :n_mod * D] + ada_b[l][:n_mod * D], n_mod)
        lbF = lb_all[l, 0].reshape(HG_HEADS, 1, HG_DK)
        lbB = lb_all[l, 1].reshape(HG_HEADS, 1, HG_DK)

        hl = modulate(xl, norm1[l], sh1, sc1)
        hc = modulate(xc, norm1[l], mods_c[0], mods_c[1])
        ul = hl @ w_in[l]
        kv_l = split_cols(ul[..., :KV_COLS], KV_SIZES)
        q_l = split_cols(ul[..., KV_COLS:], Q_SIZES)
        uc = hc @ (w_in[l] if ctx_out else w_in[l][:, :KV_COLS])
        kv_c = split_cols(uc[..., :KV_COLS], KV_SIZES)

        k_mla_c, v_mla_c = mla_keys_values(kv_c[0], mla_kv_norm[l], mla_wkv_b[l], mla_gk[l], None)
        k_na_c = split_heads(kv_c[1], NA_HEADS, na_gk[l])
        v_na_c = split_heads(kv_c[2], NA_HEADS)
        fFc, kFc, fBc, kBc, ic = hgrn_inputs(kv_c, lbF, lbB)
        zeros = jnp.zeros((B, HG_HEADS, HG_DK, HG_DV), jnp.float32)
        if ctx_out:
            q_c = split_cols(uc[..., KV_COLS:], Q_SIZES)
            qc_hg = split_heads(q_c[3], HG_HEADS).astype(jnp.float32) * hg_qscale
            oFc, SFc = hgrn_scan(qc_hg, kFc, fFc, ic, zeros)
            oBc, SBc = hgrn_scan(flip(qc_hg), flip(kBc), flip(fBc), flip(ic), zeros)
        else:
            _, SFc = hgrn_scan(None, kFc, fFc, ic, zeros)
            _, SBc = hgrn_scan(None, flip(kBc), flip(fBc), flip(ic), zeros)

        a_l = pool_mix(q_l[0], pool_w[l], pool_scale[l])
        q_mla = mla_queries(q_l[1], mla_q_norm[l], mla_wq_b[l], mla_gq[l], rope)
        k_mla, v_mla = mla_keys_values(kv_l[0], mla_kv_norm[l], mla_wkv_b[l], mla_gk[l], rope)
        b_l = merge_heads(dense_attention(q_mla, jnp.concatenate([k_mla, k_mla_c], axis=2),
                                          jnp.concatenate([v_mla, v_mla_c], axis=2), mla_scale))
        q_na = split_heads(q_l[2], NA_HEADS, na_gq[l])
        k_na = split_heads(kv_l[1], NA_HEADS, na_gk[l])
        v_na = split_heads(kv_l[2], NA_HEADS)
        c_l = merge_heads(na_attention(q_na, k_na, v_na, k_na_c, v_na_c, na_rpb[l]))
        fFl, kFl, fBl, kBl, il = hgrn_inputs(kv_l, lbF, lbB)
        ql_hg = split_heads(q_l[3], HG_HEADS).astype(jnp.float32) * hg_qscale
        oFl, _ = hgrn_scan(ql_hg, kFl, fFl, il, SFc)
        oBl, _ = hgrn_scan(flip(ql_hg), flip(kBl), flip(fBl), flip(il), SBc)
        d_l = hgrn_readout(oFl + flip(oBl), q_l[4], hg_norm[l])
        y_l = merge_branches([a_l, b_l, c_l, d_l], q_l[5], w_branch[l], w_out[l])

        if ctx_out:
            a_c = pool_mix(q_c[0], pool_w[l], pool_scale[l])
            q_mla_c = mla_queries(q_c[1], mla_q_norm[l], mla_wq_b[l], mla_gq[l], None)
            b_c = merge_heads(dense_attention(q_mla_c, k_mla_c, v_mla_c, mla_scale))
            q_na_c = split_heads(q_c[2], NA_HEADS, na_gq[l])
            c_c = merge_heads(dense_attention(q_na_c, k_na_c, v_na_c, NA_HD ** -0.5))
            d_c = hgrn_readout(oFc + flip(oBc), q_c[4], hg_norm[l])
            y_c = merge_branches([a_c, b_c, c_c, d_c], q_c[5], w_branch[l], w_out[l])
            xc = xc + mods_c[2] * y_c
            xc = xc + mods_c[5] * sqrelu_mlp(modulate(xc, norm2[l], mods_c[3], mods_c[4]), w_ff1[l], w_ff2[l])

        xl = xl + g1 * y_l
        xl = xl + g2 * sqrelu_mlp(modulate(xl, norm2[l], sh2, sc2), w_ff1[l], w_ff2[l])
    return xl
```

```python
import numpy as np
from contextlib import ExitStack
import concourse.bass as bass
import concourse.mybir as mb
from concourse.bass_utils import run_bass_kernel_spmd

F32 = mb.dt.float32
BF = mb.dt.bfloat16
AF = mb.ActivationFunctionType
ALU = mb.AluOpType
AX = mb.AxisListType

D = 1024
T = 2048
TC = 256
TA = T + TC
NT = TA // 128
L = 4096
DEPTH = 2
NCOL = 2720
EPS = 1e-6
NEG = -30000.0


class _Op:
    __slots__ = ("eng", "fn", "kind", "deps", "needs_inc", "token", "waits", "ring_wait", "carry")

    def __init__(self, eng, fn, kind):
        self.eng = eng
        self.fn = fn
        self.kind = kind
        self.deps = []
        self.needs_inc = False
        self.token = None
        self.waits = []
        self.ring_wait = None
        self.carry = False


class Prog:
    ENGS = ("pe", "act", "dve", "pool", "sp")
    NRING = 8

    def __init__(self, nc, es):
        self.nc = nc
        self.h = {"pe": nc.tensor, "act": nc.scalar, "dve": nc.vector, "pool": nc.gpsimd, "sp": nc.sync}
        self.sem = {e: es.enter_context(nc.semaphore("s_" + e)) for e in self.ENGS}
        self.cnt = {e: 0 for e in self.ENGS}
        self.ring = {q: [es.enter_context(nc.semaphore("r_%s%d" % (q, i))) for i in range(self.NRING)]
                     for q in ("sp", "pool", "act")}
        self.ringcnt = {q: [0] * self.NRING for q in self.ring}
        self.ringlast = {q: [None] * self.NRING for q in self.ring}
        self.ndma = {q: 0 for q in self.ring}
        self.ccsem = es.enter_context(nc.semaphore("s_cc"))
        self.cccnt = 0
        self.waited = {e: {} for e in self.ENGS}
        self.ops = []
        self.lastw = {}
        self.readers = {}
        self.pending_dma = []
        self.nstage = 0

    def _record(self, op, reads, writes):
        deps = []
        for k in reads:
            for w in self.lastw.get(k, ()):
                if w.eng == op.eng == "pe" and w.kind == "c" and op.kind == "c":
                    continue
                deps.append(w)
        for k in writes:
            for w in self.lastw.get(k, ()):
                if w.kind == "c" and op.kind == "c" and w.eng == op.eng == "pe":
                    continue
                deps.append(w)
            for r in self.readers.get(k, ()):
                if r.kind == "c" and op.kind == "c" and r.eng == op.eng == "pe":
                    continue
                deps.append(r)
        seen = set()
        for d in deps:
            if id(d) not in seen and d is not op:
                seen.add(id(d))
                d.needs_inc = True
                op.deps.append(d)
        for k in reads:
            self.readers.setdefault(k, []).append(op)
        for k in writes:
            self.lastw[k] = [op]
            self.readers[k] = []
        self.ops.append(op)
        return op

    def c(self, eng, fn, reads=(), writes=()):
        return self._record(_Op(eng, fn, "c"), reads, writes)

    def dma(self, q, out, in_, reads=(), writes=(), carry=False):
        op = _Op(q, lambda e: e.dma_start(out=out, in_=in_), "d")
        op.needs_inc = True
        op.carry = carry
        if not carry:
            self.pending_dma.append(op)
        return self._record(op, reads, writes)

    def cc(self, fn, reads=(), writes=()):
        op = _Op("pool", fn, "cc")
        op.needs_inc = True
        self.pending_dma.append(op)
        return self._record(op, reads, writes)

    def record_streams(self, fns):
        lists = []
        for fn in fns:
            saved, self.ops = self.ops, []
            fn()
            lists.append(self.ops)
            self.ops = saved
        idx = [0] * len(lists)
        n = max(len(x) for x in lists) if lists else 0
        for t in range(n):
            for i, lst in enumerate(lists):
                upto = ((t + 1) * len(lst) + n - 1) // n
                while idx[i] < min(upto, len(lst)):
                    self.ops.append(lst[idx[i]])
                    idx[i] += 1
        for i, lst in enumerate(lists):
            self.ops.extend(lst[idx[i]:])

    def flush(self):
        per = {e: [] for e in self.ENGS}
        for op in self.ops:
            e = op.eng
            waits = []
            for d in op.deps:
                s, v = d.token
                key = id(s)
                if self.waited[e].get(key, 0) < v:
                    self.waited[e][key] = v
                    waits.append((s, v))
            if op.kind == "d":
                r = self.ndma[e] % self.NRING
                self.ndma[e] += 1
                prev = self.ringlast[e][r]
                if prev is not None:
                    s, v = prev
                    if self.waited[e].get(id(s), 0) < v:
                        self.waited[e][id(s)] = v
                        waits.append((s, v))
                self.ringcnt[e][r] += 16
                op.token = (self.ring[e][r], self.ringcnt[e][r])
                self.ringlast[e][r] = op.token
            elif op.kind == "cc":
                self.cccnt += 1
                op.token = (self.ccsem, self.cccnt)
            elif op.needs_inc:
                self.cnt[e] += 1
                op.token = (self.sem[e], self.cnt[e])
            op.waits = waits
            per[e].append(op)
        tail = {e: [] for e in self.ENGS}
        for op in self.pending_dma:
            s, v = op.token
            e = op.eng
            if self.waited[e].get(id(s), 0) < v:
                self.waited[e][id(s)] = v
                tail[e].append((s, v))
        sem, ringsem, ccsem = self.sem, self.ring, self.ccsem

        def body(e):
            def run(h):
                for op in per[e]:
                    for s, v in op.waits:
                        h.wait_ge(s, v)
                    ins = op.fn(h)
                    if op.kind == "d":
                        ins.then_inc(op.token[0], 16)
                    elif op.kind == "cc":
                        ins.then_inc(op.token[0], 1)
                    elif op.needs_inc:
                        ins.then_inc(op.token[0], 1)
                for s, v in tail[e]:
                    h.wait_ge(s, v)
            return run

        with self.nc.Block() as blk:
            blk.tensor(body("pe"))
            blk.scalar(body("act"))
            blk.vector(body("dve"))
            blk.gpsimd(body("pool"))
            blk.sync(body("sp"))
        self.ops = []
        self.lastw = {kk: ws for kk, ws in self.lastw.items() if all(w.carry for w in ws)}
        self.readers = {}
        self.pending_dma = []
        self.nstage += 1


def rsqrt(P, out, in_, mul, add, rk, wk):
    P.c("act", lambda e: e.activation(out=out, in_=in_, func=AF.Ln, scale=mul, bias=add), reads=rk, writes=wk)
    P.c("act", lambda e: e.activation(out=out, in_=out, func=AF.Exp, scale=-0.5), reads=wk, writes=wk)


class Pool_:
    N = [0]

    def __init__(self, nc, es):
        self.nc, self.es = nc, es

    def sb(self, shape, dt=F32, name=None):
        Pool_.N[0] += 1
        return self.es.enter_context(self.nc.sbuf_tensor((name or "t") + "_%d" % Pool_.N[0], list(shape), dt))

    def ps(self, shape, dt=F32, name=None):
        Pool_.N[0] += 1
        return self.es.enter_context(self.nc.psum_tensor((name or "p") + "_%d" % Pool_.N[0], list(shape), dt))

    def get(self, key, shape, dt=F32, psum=False):
        c = self.__dict__.setdefault("_cache", {})
        if key not in c:
            c[key] = self.ps(shape, dt, key) if psum else self.sb(shape, dt, key)
        return c[key]


class K:
    pass


def _dram_in(nc, name, shape, dt=F32):
    return nc.dram_tensor(name, list(shape), dt, kind="ExternalInput").ap()


def build(dbg=(), upto=None, ncores=8, skip=()):
    nc = bass.Bass("TRN2", target_bir_lowering=False)
    es = ExitStack()
    k = K()
    k.nc, k.es = nc, es
    k.P = Prog(nc, es)
    k.dbg = set(dbg)
    k.upto = upto
    k.groups = [[2 * i, 2 * i + 1] for i in range(ncores // 2)]
    shapes = dict([
        ("x", (T, D)), ("ctx", (TC, D)), ("cT", (128, 8, 2)),
        ("ada_w", (DEPTH, D, 6 * D)), ("ada_bT", (128, DEPTH, 48)), ("normT", (128, DEPTH, 2, 8)),
        ("w_in", (DEPTH, D, 6816)), ("w_branch", (DEPTH, 4, 256, D)), ("w_out", (DEPTH, D, D)),
        ("w_ff1", (DEPTH, D, 4 * D)), ("w_ff2", (DEPTH, 4 * D, D)),
        ("pool_w", (DEPTH, 4, 64, 64)), ("pool_scT", (128, DEPTH, 2)), ("pinv", (128, 2, T)), ("pinvc", (128, 2, TC)),
        ("hm", (128, 2)),
        ("mla_wq_b", (DEPTH, 256, 384)), ("mla_wkv_b", (DEPTH, 128, 512)),
        ("gains", (128, DEPTH, 800)),
        ("ropeq", (T, 32)), ("ropek", (L, 32)),
        ("nab", (DEPTH, 16, 128, 6, 4, 128)),
        ("hg_lbT", (128, 3, 2, 2)),
    ])

    class LazyIn(dict):
        def __missing__(self, name):
            v = _dram_in(nc, name, shapes[name])
            self[name] = v
            return v
    I = LazyIn()
    k.I = I
    k.out = nc.dram_tensor("out", [T, D], F32, kind="ExternalOutput").ap()

    def scratch(name, shape, dt=F32):
        kind = "ExternalOutput" if name in k.dbg else "Internal"
        return nc.dram_tensor(name, list(shape), dt, kind=kind).ap()
    k.scratch = scratch
    pp = Pool_(nc, es)
    k.pp = pp
    k.modT = pp.sb([128, DEPTH, 48, 2], F32, "modT")
    k.AM = pp.sb([128, DEPTH, 2, 8, 2], F32, "AM")
    k.ident = pp.sb([128, 128], BF, "ident")
    k.identf = pp.sb([128, 128], F32, "identf")
    k.gains = pp.sb([128, DEPTH, 800], F32, "gains")
    k.hm = pp.sb([128, 2], F32, "hm")
    k.HTd = nc.dram_tensor("HTd", [D, TA], BF, kind="Internal").ap()
    k.U = scratch("U", (TA, 1696))
    k.UT = scratch("UT", (1024, TA))
    k.MODROW = scratch("MODROW", (DEPTH, 96, 128))
    k.XM = [scratch("XM%d" % l, (TA, D)) for l in range(DEPTH)]
    k.XL = [scratch("XL0", (TA, D)), None]
    k.YTd = scratch("YTd", (1024, TA), F32)
    stage_consts(k)
    stage_mods(k, 0)
    xsrc, csrc = I["x"], I["ctx"]
    for l in range(DEPTH):
        last = (l == DEPTH - 1)
        with ExitStack() as hs:
            hp_ = Pool_(nc, hs)
            k.HT = hp_.sb([128, 8, TA], BF, "HT")
            Wp = proj_weights(k, hp_, l)
            stage_norm(k, l, 0, xsrc, csrc, spill=True)
            stage_proj(k, l, Wp)
        if upto == "proj":
            break
        if upto == "ex1":
            break
        with ExitStack() as les:
            lp = Pool_(nc, les)
            k.YT = lp.sb([128, 8, TA], BF, "YT")
            stage_pool(k, l)
            if upto == "pool":
                stage_dump_yt(k)
                break
            if "mla" not in skip:
                stage_attn(k, l, "mla")
            if "na" not in skip:
                stage_attn(k, l, "na")
            if upto == "na":
                stage_dump_yt(k)
                break
            stage_hgrn(k, l, upto)
            if "YTd" in k.dbg:
                stage_dump_yt(k)
            if upto in ("hgrn", "hgrn1", "hgrn2"):
                break
            stage_merge(k, l, xsrc, csrc)
        if upto == "merge":
            break
        with ExitStack() as hs:
            hp_ = Pool_(nc, hs)
            k.HT = hp_.sb([128, 8, TA], BF, "HT")
            Wff = ffn_weights(k, hp_, l)
            stage_norm(k, l, 1, k.XM[l][0:T, :], k.XM[l][T:TA, :], nstream=2, ntile=(16 if last else NT))
            stage_ffn(k, l, Wff)
        xsrc, csrc = k.XL[0][0:T, :], k.XL[0][T:TA, :]
        if upto == "l0":
            break
    return k


def dump(k, name, ap, shape, reads=()):
    if name not in k.dbg:
        return
    t = k.nc.dram_tensor(name, list(shape), F32, kind="ExternalOutput").ap()
    k.P.dma("pool", t, ap, reads=list(reads))


def stage_dump_yt(k):
    P = k.P
    for j in range(8):
        P.dma("pool", k.YTd[j * 128:(j + 1) * 128, :], k.YT[:, j, :])
    P.flush()


def stage_consts(k):
    nc, P = k.nc, k.P
    P.c("pool", lambda e: e.memset(k.identf[:], 0.0), writes=["identf"])
    P.c("pool", lambda e: e.affine_select(out=k.identf[:], in_=k.identf[:], compare_op=ALU.not_equal, fill=1.0,
                                         base=0, pattern=[[-1, 128]], channel_multiplier=1),
        reads=["identf"], writes=["identf"])
    P.c("dve", lambda e: e.tensor_copy(out=k.ident[:], in_=k.identf[:]), reads=["identf"], writes=["ident"])
    P.dma("sp", k.gains[:], k.I["gains"], writes=["gains"])
    P.dma("sp", k.hm[:], k.I["hm"], writes=["hm"])
    P.flush()


def mods_ops(k, pl, l):
    nc, P, I = k.nc, k.P, k.I
    m = "m%d" % l
    cT = pl.sb([128, 8, 2])
    e1 = pl.sb([128, 8, 2])
    sT = pl.sb([128, 8, 2], BF)
    abT = pl.sb([128, DEPTH, 48])
    nT = pl.sb([128, DEPTH, 2, 8])
    wbuf = [pl.sb([128, 8, 512], BF, "adaw") for _ in range(2)]
    ps = pl.ps([128, 48, 2])
    psr = pl.ps([128, 128])
    rowsb = pl.sb([96, 128])
    P.dma("sp", cT[:], I["cT"], writes=[m + "cT"])
    P.dma("sp", abT[:], I["ada_bT"], writes=[m + "abT"])
    P.dma("sp", nT[:], I["normT"], writes=[m + "nT"])
    P.c("act", lambda e: e.activation(out=e1[:], in_=cT[:], func=AF.Exp, scale=-1.0), reads=[m + "cT"], writes=[m + "e1"])
    P.c("dve", lambda e: e.tensor_scalar_add(out=e1[:], in0=e1[:], scalar1=1.0), reads=[m + "e1"], writes=[m + "e1"])
    P.c("dve", lambda e: e.reciprocal(out=e1[:], in_=e1[:]), reads=[m + "e1"], writes=[m + "e1"])
    P.c("dve", lambda e: e.tensor_tensor(out=sT[:], in0=e1[:], in1=cT[:], op=ALU.mult), reads=[m + "e1", m + "cT"], writes=[m + "sT"])
    wv = I["ada_w"][l].rearrange("(kc p) n -> p kc n", p=128)
    for nb in range(12):
        wb = wbuf[nb % 2]
        key = m + "adaw%d" % (nb % 2)
        P.dma("pool", wb[:], wv[:, :, nb * 512:(nb + 1) * 512], writes=[key])
        for jj in range(4):
            j = nb * 4 + jj
            for kc in range(8):
                P.c("pe", (lambda e, wb=wb, jj=jj, kc=kc, j=j: e.matmul(
                    ps[:, j, :], lhsT=wb[:, kc, jj * 128:(jj + 1) * 128], rhs=sT[:, kc, :],
                    start=(kc == 0), stop=(kc == 7))), reads=[key, m + "sT"], writes=[m + "psmod"])
    P.c("dve", (lambda e: e.tensor_tensor(
        out=k.modT[:, l], in0=ps[:], in1=abT[:, l].unsqueeze(2).to_broadcast([128, 48, 2]), op=ALU.add)),
        reads=[m + "psmod", m + "abT"], writes=[m + "modT"])
    P.c("pe", (lambda e: e.transpose(psr[0:96, :], k.modT[:, l].rearrange("p j s -> p (j s)"), k.identf[:])),
        reads=[m + "modT", "identf"], writes=[m + "psr"])
    P.c("dve", (lambda e: e.tensor_copy(out=rowsb[:], in_=psr[0:96, :])), reads=[m + "psr"], writes=[m + "rowsb"])
    P.dma("sp", k.MODROW[l], rowsb[:], reads=[m + "rowsb"], writes=[m + "MODROW"])
    for w in range(2):
        sc = k.modT[:, l, (8 + 24 * w):(16 + 24 * w), :]
        P.c("dve", (lambda e, w=w, sc=sc: e.tensor_scalar_add(out=k.AM[:, l, w], in0=sc, scalar1=1.0)),
            reads=[m + "modT"], writes=[m + "AM%d" % w])
        P.c("dve", (lambda e, w=w: e.tensor_tensor(
            out=k.AM[:, l, w], in0=k.AM[:, l, w], in1=nT[:, l, w].unsqueeze(2).to_broadcast([128, 8, 2]),
            op=ALU.mult)), reads=[m + "AM%d" % w, m + "nT"], writes=[m + "AM%d" % w])


def stage_mods(k, l):
    with ExitStack() as es:
        mods_ops(k, Pool_(k.nc, es), l)
        k.P.flush()


def stage_norm(k, l, w, xsrc, csrc, spill=False, nstream=3, ntile=NT):
    nc, P = k.nc, k.P
    with ExitStack() as es:
        pl = Pool_(nc, es)
        ss = pl.sb([128, NT])
        rstd = pl.sb([128, NT])
        bufs = [(pl.sb([128, D]), pl.sb([128, D], BF), pl.sb([128, D], BF), pl.ps([128, 8, 128], BF), pl.sb([128, 8, 128])) for _ in range(nstream)]

        def tile(i):
            s_ = 0 if i < 16 else 1
            src = xsrc[i * 128:(i + 1) * 128, :] if i < 16 else csrc[(i - 16) * 128:(i - 15) * 128, :]
            b = i % nstream
            X, XN, junk, PT, TM = bufs[b]
            P.dma("sp", X[:], src, writes=["xt%d" % b])
            P.c("act", (lambda e: e.activation(out=junk[:], in_=X[:], func=AF.Square, accum_out=ss[:, i:i + 1])),
                reads=["xt%d" % b], writes=["junk%d" % b, "ss%d" % i])
            rsqrt(P, rstd[:, i:i + 1], ss[:, i:i + 1], 1.0 / D, EPS, ["ss%d" % i], ["rs%d" % i])
            P.c("act", (lambda e: e.activation(out=XN[:], in_=X[:], func=AF.Copy, scale=rstd[:, i:i + 1])),
                reads=["xt%d" % b, "rs%d" % i], writes=["xn%d" % b])
            for kc in range(8):
                P.c("pe", (lambda e, kc=kc: e.transpose(PT[:, kc, :], XN[:, kc * 128:(kc + 1) * 128], k.ident[:])),
                    reads=["xn%d" % b, "ident"], writes=["pt%d" % b])
            A = k.AM[:, l, w, :, s_:s_ + 1].to_broadcast([128, 8, 128])
            Bv = k.modT[:, l, 24 * w:24 * w + 8, s_:s_ + 1].to_broadcast([128, 8, 128])
            P.c("dve", (lambda e: e.tensor_tensor(out=TM[:], in0=PT[:], in1=A, op=ALU.mult)),
                reads=["pt%d" % b], writes=["tm%d" % b])
            P.c("dve", (lambda e: e.tensor_tensor(out=k.HT[:, :, i * 128:(i + 1) * 128], in0=TM[:], in1=Bv, op=ALU.add)),
                reads=["tm%d" % b], writes=["HT%d" % i])

        P.record_streams([(lambda st=st: [tile(i) for i in range(st, ntile, nstream)]) for st in range(nstream)])
        if spill:
            hk = ["HT%d" % i for i in range(ntile)]
            for kc in range(8):
                P.dma("sp", k.HTd[kc * 128:(kc + 1) * 128, :], k.HT[:, kc, :], reads=hk, writes=["HTd"])
        P.flush()


def proj_weights(k, pl, l):
    W = pl.sb([128, 8, NCOL], BF, "win")
    wv = k.I["w_in"][l].rearrange("(kc p) n -> p kc n", p=128)
    for c0 in range(0, NCOL, 680):
        k.P.dma("pool", W[:, :, c0:c0 + 680], wv[:, :, c0:c0 + 680], writes=["W%d" % c0], carry=True)
    return W


def stage_proj(k, l, W=None):
    nc, P, I = k.nc, k.P, k.I
    tm_cols = [(0, 672), (1184, 1440), (1696, 2208), (2464, 2720)]
    fm_cols = [(1440, 1696), (672, 1184), (2208, 2464)]
    with ExitStack() as es:
        pl = Pool_(nc, es)
        if W is None:
            W = proj_weights(k, pl, l)
        wkeys = ["W%d" % c0 for c0 in range(0, NCOL, 680)]
        ps = [pl.ps([128, 512]) for _ in range(4)]
        st = [pl.sb([128, 1696]) for _ in range(2)]
        n = 0
        for i in range(NT):
            S = st[i % 2]
            uc = 0
            for (a, b) in tm_cols:
                c = a
                while c < b:
                    w_ = min(512, b - c)
                    pb = ps[n % 4]
                    pk = "ps%d" % (n % 4)
                    n += 1
                    for kc in range(8):
                        P.c("pe", (lambda e, pb=pb, kc=kc, c=c, w_=w_, i=i: e.matmul(
                            pb[:, 0:w_], lhsT=k.HT[:, kc, i * 128:(i + 1) * 128], rhs=W[:, kc, c:c + w_],
                            start=(kc == 0), stop=(kc == 7))), reads=["HT"] + wkeys, writes=[pk])
                    eng = "act" if n % 2 else "dve"
                    if eng == "act":
                        P.c("act", (lambda e, pb=pb, S=S, uc=uc, w_=w_: e.copy(out=S[:, uc:uc + w_], in_=pb[:, 0:w_])),
                            reads=[pk], writes=["st%d_%d" % (i % 2, uc)])
                    else:
                        P.c("dve", (lambda e, pb=pb, S=S, uc=uc, w_=w_: e.tensor_copy(out=S[:, uc:uc + w_], in_=pb[:, 0:w_])),
                            reads=[pk], writes=["st%d_%d" % (i % 2, uc)])
                    uc += w_
                    c += w_
            skeys = [kk for kk in list(P.lastw.keys()) if isinstance(kk, str) and kk.startswith("st%d_" % (i % 2))]
            P.dma("sp", k.U[i * 128:(i + 1) * 128, :], S[:], reads=skeys, writes=["U%d" % i])
        st2 = [pl.sb([128, 512]) for _ in range(2)]
        m = 0
        exchange_alloc(k)
        ukeys = ["U%d" % i for i in range(NT)]
        P.dma("sp", k.EX1in, k.U[0:T, 0:160], reads=ukeys, writes=["EX1in"])
        P.dma("sp", k.EXNin[0:256, :], k.U[0:256, 160:672], reads=ukeys, writes=["EXNin"])
        P.dma("sp", k.EXNin[256:512, :], k.U[T - 256:T, 160:672], reads=ukeys, writes=["EXNin"])
        P.cc(lambda e: e.collective_compute("AllGather", ALU.bypass, replica_groups=k.groups, ins=[k.EXNin], outs=[k.GN]),
             reads=["EXNin"], writes=["GN"])
        P.cc(lambda e: e.collective_compute("AllGather", ALU.bypass, replica_groups=k.groups, ins=[k.EX1in], outs=[k.G1]),
             reads=["EX1in"], writes=["G1"])
        r0 = 0
        for (a, b) in fm_cols:
            for c in range(a, b, 128):
                for t0 in range(0, TA, 512):
                    tw = min(512, TA - t0)
                    pb = ps[n % 4]
                    pk = "ps%d" % (n % 4)
                    n += 1
                    S2 = st2[m % 2]
                    sk = "st2_%d" % (m % 2)
                    m += 1
                    for kc in range(8):
                        P.c("pe", (lambda e, pb=pb, kc=kc, c=c, t0=t0, tw=tw: e.matmul(
                            pb[:, 0:tw], lhsT=W[:, kc, c:c + 128], rhs=k.HT[:, kc, t0:t0 + tw],
                            start=(kc == 0), stop=(kc == 7))), reads=["HT"] + wkeys, writes=[pk])
                    if m % 2:
                        P.c("act", (lambda e, pb=pb, S2=S2, tw=tw: e.copy(out=S2[:, 0:tw], in_=pb[:, 0:tw])), reads=[pk], writes=[sk])
                    else:
                        P.c("dve", (lambda e, pb=pb, S2=S2, tw=tw: e.tensor_copy(out=S2[:, 0:tw], in_=pb[:, 0:tw])), reads=[pk], writes=[sk])
                    P.dma("sp", k.UT[r0:r0 + 128, t0:t0 + tw], S2[:, 0:tw], reads=[sk], writes=["UT%d_%d" % (r0, t0)])
                r0 += 128
                if r0 == 256:
                    pk_ = ["UT%d_%d" % (rr, tt) for rr in (0, 128) for tt in range(0, TA, 512)]
                    P.dma("sp", k.EXPin[:, 0:8], k.UT[0:256, 0:8], reads=pk_, writes=["EXPin"])
                    P.dma("sp", k.EXPin[:, 8:16], k.UT[0:256, T - 8:T], reads=pk_, writes=["EXPin"])
                    P.cc(lambda e: e.collective_compute("AllGather", ALU.bypass, replica_groups=k.groups, ins=[k.EXPin], outs=[k.GP]),
                         reads=["EXPin"], writes=["GP"])
        P.flush()


def exchange_alloc(k):
    nc = k.nc
    if not hasattr(k, "EX1in"):
        k.EX1in = nc.dram_tensor("EX1in", [T, 160], F32, kind="Internal").ap()
        k.G1 = nc.dram_tensor("G1", [2 * T, 160], F32, kind="Internal").ap()
        k.EXNin = nc.dram_tensor("EXNin", [512, 512], F32, kind="Internal").ap()
        k.GN = nc.dram_tensor("GN", [1024, 512], F32, kind="Internal").ap()
        k.EXPin = nc.dram_tensor("EXPin", [256, 16], F32, kind="Internal").ap()
        k.GP = nc.dram_tensor("GP", [512, 16], F32, kind="Internal").ap()


def stage_exchange1(k, l):
    nc, P = k.nc, k.P
    if not hasattr(k, "EX1in"):
        k.EX1in = nc.dram_tensor("EX1in", [T, 160], F32, kind="Internal").ap()
        k.G1 = nc.dram_tensor("G1", [2 * T, 160], F32, kind="Internal").ap()
        k.EXNin = nc.dram_tensor("EXNin", [512, 512], F32, kind="Internal").ap()
        k.GN = nc.dram_tensor("GN", [1024, 512], F32, kind="Internal").ap()
        k.EXPin = nc.dram_tensor("EXPin", [256, 16], F32, kind="Internal").ap()
        k.GP = nc.dram_tensor("GP", [512, 16], F32, kind="Internal").ap()
    P.dma("sp", k.EX1in, k.U[0:T, 0:160], writes=["EX1in"])
    P.dma("sp", k.EXNin[0:256, :], k.U[0:256, 160:672], writes=["EXNin"])
    P.dma("sp", k.EXNin[256:512, :], k.U[T - 256:T, 160:672], writes=["EXNin"])
    P.cc(lambda e: e.collective_compute("AllGather", ALU.bypass, replica_groups=k.groups, ins=[k.EXNin], outs=[k.GN]),
         reads=["EXNin"], writes=["GN"])
    P.dma("sp", k.EXPin[:, 0:8], k.UT[0:256, 0:8], writes=["EXPin"])
    P.dma("sp", k.EXPin[:, 8:16], k.UT[0:256, T - 8:T], writes=["EXPin"])
    P.cc(lambda e: e.collective_compute("AllGather", ALU.bypass, replica_groups=k.groups, ins=[k.EX1in], outs=[k.G1]),
         reads=["EX1in"], writes=["G1"])
    P.cc(lambda e: e.collective_compute("AllGather", ALU.bypass, replica_groups=k.groups, ins=[k.EXPin], outs=[k.GP]),
         reads=["EXPin"], writes=["GP"])
    P.flush()


def stage_pool(k, l):
    nc, P, I = k.nc, k.P, k.I
    with ExitStack() as es:
        pl = Pool_(nc, es)
        Wb = pl.sb([128, 2, 128])
        Wbb = pl.sb([128, 2, 128], BF)
        psc = pl.sb([128, DEPTH, 2])
        P.c("pool", lambda e: e.memset(Wb[:], 0.0), writes=["Wb"])
        for g in range(4):
            t_, o_ = g // 2, (g % 2) * 64
            P.dma("sp", Wb[o_:o_ + 64, t_, o_:o_ + 64], I["pool_w"][l, g], reads=[], writes=["Wb"])
        P.c("dve", lambda e: e.tensor_copy(out=Wbb[:], in_=Wb[:]), reads=["Wb"], writes=["Wbb"])
        P.dma("sp", psc[:], I["pool_scT"], writes=["psc"])
        ps = [pl.ps([128, 512]) for _ in range(2)]
        npsm = [0]
        for (n, c0, invsrc, halo) in ((T, 0, I["pinv"], True), (TC, T, I["pinvc"], False)):
            if n == TC and l == DEPTH - 1:
                continue
            N = n + 16
            sfx = "L" if halo else "C"
            B = pl.sb([128, 2, N], F32, "pB")
            A1 = pl.sb([128, 2, N], F32, "pA1")
            A2 = pl.sb([128, 2, N], F32, "pA2")
            R = pl.sb([128, 2, n], F32, "pR")
            Rb = pl.sb([128, 2, n], BF, "pRb")
            inv = pl.sb([128, 2, n], F32, "pinv")
            kB = "pB" + sfx
            P.c("pool", (lambda e, B=B: e.memset(B[:], 0.0)), writes=[kB])
            for t_ in range(2):
                P.dma("sp", B[:, t_, 8:8 + n], k.UT[t_ * 128:(t_ + 1) * 128, c0:c0 + n], writes=[kB])
                if halo:
                    P.dma("sp", B[:, t_, 0:8], k.GP[t_ * 128:(t_ + 1) * 128, 8:16], writes=[kB])
                    P.dma("sp", B[:, t_, 8 + n:16 + n], k.GP[256 + t_ * 128:256 + (t_ + 1) * 128, 0:8], writes=[kB])
            P.dma("sp", inv[:], invsrc, writes=["inv" + sfx])
            if halo:
                P.c("dve", (lambda e, B=B: e.tensor_scalar_mul(out=B[:, :, 0:8], in0=B[:, :, 0:8], scalar1=k.hm[:, 0:1])),
                    reads=[kB, "hm"], writes=[kB])
                P.c("dve", (lambda e, B=B, n=n: e.tensor_scalar_mul(out=B[:, :, 8 + n:16 + n], in0=B[:, :, 8 + n:16 + n], scalar1=k.hm[:, 1:2])),
                    reads=[kB], writes=[kB])
            kk = [kB, "A1" + sfx, "A2" + sfx, "R" + sfx]
            P.c("dve", (lambda e, B=B, A1=A1, N=N: e.tensor_tensor(out=A1[:, :, 1:N], in0=B[:, :, 0:N - 1], in1=B[:, :, 1:N], op=ALU.add)),
                reads=[kB], writes=[kk[1]])
            P.c("dve", (lambda e, A1=A1, A2=A2, N=N: e.tensor_tensor(out=A2[:, :, 2:N - 1], in0=A1[:, :, 1:N - 2], in1=A1[:, :, 3:N], op=ALU.add)),
                reads=[kk[1]], writes=[kk[2]])
            P.c("dve", (lambda e, A1=A1, R=R, n=n: e.tensor_copy(out=R[0:64, 0, :], in_=A1[0:64, 0, 8:8 + n])), reads=[kk[1]], writes=[kk[3] + "a"])
            P.c("dve", (lambda e, A2=A2, R=R, n=n: e.tensor_copy(out=R[64:128, 0, :], in_=A2[64:128, 0, 8:8 + n])), reads=[kk[2]], writes=[kk[3] + "b"])
            P.c("dve", (lambda e, A1=A1, A2=A2, N=N: e.tensor_tensor(out=A1[:, :, 4:N - 3], in0=A2[:, :, 2:N - 5], in1=A2[:, :, 6:N - 1], op=ALU.add)),
                reads=[kk[2], kk[1]], writes=[kk[1]])
            P.c("dve", (lambda e, A1=A1, R=R, n=n: e.tensor_copy(out=R[0:64, 1, :], in_=A1[0:64, 1, 8:8 + n])), reads=[kk[1]], writes=[kk[3] + "c"])
            P.c("dve", (lambda e, A1=A1, A2=A2, N=N: e.tensor_tensor(out=A2[:, :, 8:N - 7], in0=A1[:, :, 4:N - 11], in1=A1[:, :, 12:N - 3], op=ALU.add)),
                reads=[kk[1], kk[2]], writes=[kk[2]])
            P.c("dve", (lambda e, A2=A2, R=R, n=n: e.tensor_copy(out=R[64:128, 1, :], in_=A2[64:128, 1, 8:8 + n])), reads=[kk[2]], writes=[kk[3] + "d"])
            rk = [kk[3] + x for x in "abcd"]
            P.c("dve", (lambda e, R=R, inv=inv: e.tensor_tensor(out=R[:], in0=R[:], in1=inv[:], op=ALU.mult)), reads=rk + ["inv" + sfx], writes=[kk[3]])
            P.c("dve", (lambda e, R=R, Rb=Rb, B=B, n=n: e.tensor_tensor(out=Rb[:], in0=R[:], in1=B[:, :, 8:8 + n], op=ALU.subtract)),
                reads=[kk[3], kB], writes=["Rb" + sfx])
            for t_ in range(2):
                for t0 in range(0, n, 512):
                    tw = min(512, n - t0)
                    pb = ps[npsm[0] % 2]
                    pk = "pps%d" % (npsm[0] % 2)
                    npsm[0] += 1
                    P.c("pe", (lambda e, pb=pb, t_=t_, t0=t0, tw=tw, Rb=Rb: e.matmul(pb[:, 0:tw], lhsT=Wbb[:, t_, :], rhs=Rb[:, t_, t0:t0 + tw], start=True, stop=True)),
                        reads=["Wbb", "Rb" + sfx], writes=[pk])
                    P.c("act", (lambda e, pb=pb, t_=t_, t0=t0, tw=tw, c0=c0: e.activation(out=k.YT[:, t_, c0 + t0:c0 + t0 + tw], in_=pb[:, 0:tw], func=AF.Copy,
                                                                                     scale=psc[:, l, t_:t_ + 1])),
                        reads=[pk, "psc"], writes=["YT"])
        P.flush()


def head_norm(P, pl, tag, src, H, dh, gain_bc, out, rk, wk, extra_ss=None):
    sq = pl.get(tag + "sq", [128, H, dh], F32)
    ssh = pl.get(tag + "ssh", [128, H], F32)
    P.c("act", lambda e: e.activation(out=sq[:], in_=src, func=AF.Square), reads=rk, writes=[tag + "sq"])
    P.c("dve", lambda e: e.tensor_reduce(out=ssh[:], in_=sq[:], axis=AX.X, op=ALU.add), reads=[tag + "sq"], writes=[tag + "ssh"])
    if extra_ss is not None:
        ap, key, tot = extra_ss
        P.c("dve", lambda e: e.tensor_scalar_add(out=ssh[:], in0=ssh[:], scalar1=ap), reads=[tag + "ssh", key], writes=[tag + "ssh"])
    else:
        tot = dh
    rsqrt(P, ssh[:], ssh[:], 1.0 / tot, EPS, [tag + "ssh"], [tag + "ssh"])
    P.c("dve", lambda e: e.tensor_tensor(out=sq[:], in0=src, in1=ssh[:].unsqueeze(2).to_broadcast([128, H, dh]), op=ALU.mult),
        reads=rk + [tag + "ssh"], writes=[tag + "sq"])
    P.c("dve", lambda e: e.tensor_tensor(out=out, in0=sq[:], in1=gain_bc, op=ALU.mult), reads=[tag + "sq", "gains"], writes=wk)
    return ssh


def rope_apply(P, pl, tag, x, tab, out, rk, wk):
    xv = x.rearrange("p h (a b j) -> p h a b j", a=2, b=2)
    ov = out.rearrange("p h (a b j) -> p h a b j", a=2, b=2)
    x1, x2 = xv[:, :, :, 0, :], xv[:, :, :, 1, :]
    cos = tab[:, 0:16].rearrange("p (a j) -> p a j", a=2).unsqueeze(1).to_broadcast([128, 4, 2, 8])
    sin = tab[:, 16:32].rearrange("p (a j) -> p a j", a=2).unsqueeze(1).to_broadcast([128, 4, 2, 8])
    t = [pl.get(tag + "rt%d" % j, [128, 4, 2, 8], F32) for j in range(4)]
    P.c("dve", lambda e: e.tensor_tensor(out=t[0][:], in0=x1, in1=cos, op=ALU.mult), reads=rk, writes=[tag + "t0"])
    P.c("dve", lambda e: e.tensor_tensor(out=t[1][:], in0=x2, in1=sin, op=ALU.mult), reads=rk, writes=[tag + "t1"])
    P.c("dve", lambda e: e.tensor_tensor(out=ov[:, :, :, 0, :], in0=t[0][:], in1=t[1][:], op=ALU.subtract), reads=[tag + "t0", tag + "t1"], writes=wk)
    P.c("dve", lambda e: e.tensor_tensor(out=t[2][:], in0=x2, in1=cos, op=ALU.mult), reads=rk, writes=[tag + "t2"])
    P.c("dve", lambda e: e.tensor_tensor(out=t[3][:], in0=x1, in1=sin, op=ALU.mult), reads=rk, writes=[tag + "t3"])
    P.c("dve", lambda e: e.tensor_tensor(out=ov[:, :, :, 1, :], in0=t[2][:], in1=t[3][:], op=ALU.add), reads=[tag + "t2", tag + "t3"], writes=wk)


def heads_to_fm(P, pt, fin, dh, dst, rk, ptk, wk, ident, eng="act"):
    for h in range(4):
        P.c("pe", (lambda e, h=h: e.transpose(pt[0:dh, h, :], fin[:, h, :], ident[:])), reads=rk + ["ident"], writes=[ptk])
    if eng == "act":
        P.c("act", lambda e: e.copy(out=dst, in_=pt[0:dh, :, :]), reads=[ptk], writes=wk)
    else:
        P.c("dve", lambda e: e.tensor_copy(out=dst, in_=pt[0:dh, :, :]), reads=[ptk], writes=wk)


class AttnRes:
    pass


def attn_alloc(nc, pl):
    r = AttnRes()
    r.psS = [pl.ps([128, 4, 128], F32, "psS") for _ in range(2)]
    r.PT = [pl.sb([128, 4, 128], BF, "PT") for _ in range(3)]
    r.tb = [pl.sb([128, 4, 128], F32, "tb") for _ in range(2)]
    r.acc = [pl.ps([128, 4, 128], F32, "acc") for _ in range(2)]
    r.pty = pl.ps([128, 2, 128], BF, "pty")
    r.rec = pl.sb([128, 4], F32, "rec")
    r.yb = pl.sb([128, 4, 64], BF, "yb")
    r.ng = 0
    r.na = 0
    return r


def attn_tile(k, r, qT, dq, chunks, scale, ytj, col0, bias=None, nloc=0, bkey="bias"):
    P = k.P
    a = r.na % 2
    r.na += 1
    acc = r.acc[a]
    ak = "acc%d" % a
    groups = []
    for h in range(4):
        ch = chunks(h)
        n = len(ch)
        for g0 in range(0, n, 4):
            groups.append((h, ch, n, g0, min(n, g0 + 4)))

    def qk(g):
        h, ch, n, g0, g1 = g
        q_ap, qkeys = qT(h)
        b = r.ng % 2
        pb3 = r.ng % 3
        r.ng += 1
        psS, PT, tb = r.psS[b], r.PT[pb3], r.tb[b]
        sk, pk, tk = "psS%d" % b, "PT%d" % pb3, "tb%d" % b
        for ci in range(g0, g1):
            kT_ap, vp_ap, ckeys = ch[ci]
            P.c("pe", (lambda e, psS=psS, gi=ci - g0, kT_ap=kT_ap, q_ap=q_ap: e.matmul(psS[:, gi, :], lhsT=kT_ap, rhs=q_ap, start=True, stop=True)),
                reads=qkeys + ckeys, writes=[sk])
        nb = max(0, min(g1, nloc) - g0)
        if nb > 0:
            bv = bias(h)[:, g0:g0 + nb, :]
            P.c("dve", (lambda e, psS=psS, tb=tb, nb=nb, bv=bv: e.scalar_tensor_tensor(
                out=tb[:, 0:nb, :], in0=psS[:, 0:nb, :], scalar=scale, in1=bv, op0=ALU.mult, op1=ALU.add)),
                reads=[sk, bkey], writes=[tk])
            P.c("act", (lambda e, PT=PT, tb=tb, nb=nb: e.activation(out=PT[:, 0:nb, :], in_=tb[:, 0:nb, :], func=AF.Exp)),
                reads=[tk], writes=[pk])
        if g1 - g0 > nb:
            P.c("act", (lambda e, PT=PT, psS=psS, nb=nb, m=g1 - g0: e.activation(out=PT[:, nb:m, :], in_=psS[:, nb:m, :], func=AF.Exp, scale=scale)),
                reads=[sk], writes=[pk])
        return (PT, pk)

    def pv(g, st):
        h, ch, n, g0, g1 = g
        PT, pk = st
        for ci in range(g0, g1):
            kT_ap, vp_ap, ckeys = ch[ci]
            P.c("pe", (lambda e, PT=PT, gi=ci - g0, vp_ap=vp_ap, h=h, ci=ci, n=n: e.matmul(
                acc[:, h, 0:65], lhsT=PT[:, gi, :], rhs=vp_ap, start=(ci == 0), stop=(ci == n - 1))),
                reads=[pk] + ckeys, writes=[ak])

    prev = None
    for g in groups:
        st = qk(g)
        if prev is not None:
            pv(*prev)
        prev = (g, st)
    pv(*prev)
    P.c("dve", lambda e: e.reciprocal(out=r.rec[:], in_=acc[:, :, 64]), reads=[ak], writes=["rec"])
    P.c("dve", lambda e: e.tensor_tensor(out=r.yb[:], in0=acc[:, :, 0:64], in1=r.rec[:].unsqueeze(2).to_broadcast([128, 4, 64]), op=ALU.mult),
        reads=[ak, "rec"], writes=["yb"])
    ybf = r.yb[:].rearrange("p h d -> p (h d)")
    for j in range(2):
        P.c("pe", (lambda e, j=j: e.transpose(r.pty[:, j, :], ybf[:, j * 128:(j + 1) * 128], k.ident[:])), reads=["yb", "ident"], writes=["pty"])
    P.c("act", lambda e: e.copy(out=k.YT[:, ytj:ytj + 2, col0:col0 + 128], in_=r.pty[:]), reads=["pty"], writes=["YT"])


def stage_mla(k, l):
    nc, P, I = k.nc, k.P, k.I
    last = (l == DEPTH - 1)
    G = k.gains
    NS = 3
    with ExitStack() as es:
        pl = Pool_(nc, es)
        KT = pl.sb([96, 4, 34 * 128], BF, "KT")
        VP = pl.sb([128, 34, 4, 65], BF, "VP")
        QT = pl.sb([96, 4, TA], BF, "QT")
        with ExitStack() as es2:
            p2 = Pool_(nc, es2)
            Wkv = p2.sb([128, 512], BF, "wkv")
            Wq = p2.sb([128, 2, 384], BF, "wq")
            P.dma("pool", Wkv[:], I["mla_wkv_b"][l], writes=["Wkv"])
            P.dma("pool", Wq[:], I["mla_wq_b"][l].rearrange("(j p) n -> p j n", p=128), writes=["Wq"])
            P.c("pool", lambda e: e.memset(VP[:, :, :, 64:65], 1.0), writes=["VP1"])
            ss = p2.sb([128, 64], F32, "ss")
            junk = [p2.sb([128, 256], F32, "junk") for _ in range(2)]
            for i in range(34):
                st = i % NS
                tg = "k%d" % st
                src = k.G1[i * 128:(i + 1) * 128, 0:160] if i < 32 else k.U[T + (i - 32) * 128:T + (i - 31) * 128, 0:160]
                kva = p2.get(tg + "kva", [128, 160])
                ckvn = p2.get(tg + "ckvn", [128, 128], BF)
                cT_ = p2.get(tg + "cT", [128, 128], BF)
                sr = p2.get(tg + "sr", [128, 1])
                kfin = p2.get(tg + "kfin", [128, 4, 96], BF)
                t32 = p2.get(tg + "t32", [128, 32])
                kr = p2.get(tg + "kr", [128, 4, 32])
                tab = p2.get(tg + "tab", [128, 32])
                pT = p2.get("pT%d" % (i % 2), [128, 2, 128], BF, psum=True)
                pkv = p2.get("pkv%d" % (i % 2), [128, 512], F32, psum=True)
                pt4 = p2.get("pt4%d" % (i % 2), [128, 4, 128], BF, psum=True)
                pTk, pkvk, pt4k = "pT%d" % (i % 2), "pkv%d" % (i % 2), "pt4%d" % (i % 2)
                jk = junk[i % 2]
                jkk = "junk%d" % (i % 2)
                P.dma("sp", kva[:], src, writes=[tg + "kva"])
                if i < 32:
                    P.dma("sp", tab[:], I["ropek"][i * 128:(i + 1) * 128, :], writes=[tg + "tab"])
                P.c("act", (lambda e, kva=kva, i=i, jk=jk: e.activation(out=jk[:, 0:128], in_=kva[:, 0:128], func=AF.Square, accum_out=ss[:, i:i + 1])),
                    reads=[tg + "kva"], writes=[jkk, "ss%d" % i])
                P.c("act", (lambda e, kva=kva, sr=sr, jk=jk: e.activation(out=jk[:, 128:160], in_=kva[:, 128:160], func=AF.Square, accum_out=sr[:])),
                    reads=[tg + "kva"], writes=[jkk + "b", tg + "sr"])
                rsqrt(P, ss[:, i:i + 1], ss[:, i:i + 1], 1.0 / 128, EPS, ["ss%d" % i], ["ss%d" % i])
                P.c("dve", (lambda e, kva=kva, ckvn=ckvn, i=i: e.scalar_tensor_tensor(out=ckvn[:], in0=kva[:, 0:128], scalar=ss[:, i:i + 1], in1=G[:, l, 256:384],
                                                                                   op0=ALU.mult, op1=ALU.mult)), reads=[tg + "kva", "ss%d" % i, "gains"], writes=[tg + "ckvn"])
                P.c("pe", (lambda e, ckvn=ckvn, pT=pT: e.transpose(pT[:, 0, :], ckvn[:], k.ident[:])), reads=[tg + "ckvn", "ident"], writes=[pTk])
                P.c("act", (lambda e, cT_=cT_, pT=pT: e.copy(out=cT_[:], in_=pT[:, 0, :])), reads=[pTk], writes=[tg + "cT"])
                P.c("pe", (lambda e, cT_=cT_, pkv=pkv: e.matmul(pkv[:], lhsT=cT_[:], rhs=Wkv[:], start=True, stop=True)), reads=[tg + "cT", "Wkv"], writes=[pkvk])
                pv = pkv[:].rearrange("p (h c) -> p h c", h=4)
                P.c("act", (lambda e, i=i, pv=pv: e.copy(out=VP[:, i, :, 0:64], in_=pv[:, :, 64:128])), reads=[pkvk], writes=["VP%d" % i])
                rh = head_norm(P, p2, tg + "hn", pv[:, :, 0:64], 4, 64, G[:, l, 480:544].unsqueeze(1).to_broadcast([128, 4, 64]), kfin[:, :, 0:64],
                               [pkvk], [tg + "kfinA"], extra_ss=(sr[:], tg + "sr", 96))
                P.c("dve", (lambda e, kva=kva, t32=t32: e.tensor_tensor(out=t32[:], in0=kva[:, 128:160], in1=G[:, l, 544:576], op=ALU.mult)),
                    reads=[tg + "kva", "gains"], writes=[tg + "t32"])
                if i < 32:
                    P.c("dve", (lambda e, t32=t32, kr=kr, rh=rh: e.tensor_tensor(out=kr[:], in0=t32[:].unsqueeze(1).to_broadcast([128, 4, 32]),
                                                                             in1=rh[:].unsqueeze(2).to_broadcast([128, 4, 32]), op=ALU.mult)),
                        reads=[tg + "t32", tg + "hnssh"], writes=[tg + "kr"])
                    rope_apply(P, p2, tg + "rp", kr[:], tab, kfin[:, :, 64:96], [tg + "kr", tg + "tab"], [tg + "kfinB"])
                else:
                    P.c("dve", (lambda e, t32=t32, kfin=kfin, rh=rh: e.tensor_tensor(out=kfin[:, :, 64:96], in0=t32[:].unsqueeze(1).to_broadcast([128, 4, 32]),
                                                                                 in1=rh[:].unsqueeze(2).to_broadcast([128, 4, 32]), op=ALU.mult)),
                        reads=[tg + "t32", tg + "hnssh"], writes=[tg + "kfinB"])
                heads_to_fm(P, pt4, kfin, 96, KT[:, :, i * 128:(i + 1) * 128], [tg + "kfinA", tg + "kfinB"], pt4k, ["KT%d" % i], k.ident,
                            eng=("act" if i % 2 else "dve"))
            nq = 16 if last else 18
            for i in range(nq):
                st = i % NS
                tg = "q%d" % st
                qa = p2.get(tg + "qa", [128, 256])
                qan = p2.get(tg + "qan", [128, 256], BF)
                qT_ = p2.get(tg + "qanT", [128, 2, 128], BF)
                qn = p2.get(tg + "qn", [128, 4, 96])
                qfin = p2.get(tg + "qfin", [128, 4, 96], BF)
                tab = p2.get(tg + "tab", [128, 32])
                pT = p2.get("pT%d" % (i % 2), [128, 2, 128], BF, psum=True)
                pkv = p2.get("pkv%d" % (i % 2), [128, 512], F32, psum=True)
                pt4 = p2.get("pt4%d" % (i % 2), [128, 4, 128], BF, psum=True)
                pTk, pkvk, pt4k = "pT%d" % (i % 2), "pkv%d" % (i % 2), "pt4%d" % (i % 2)
                jk = junk[i % 2]
                jkk = "junk%d" % (i % 2)
                P.dma("sp", qa[:], k.U[i * 128:(i + 1) * 128, 928:1184], writes=[tg + "qa"])
                if i < 16:
                    P.dma("sp", tab[:], I["ropeq"][i * 128:(i + 1) * 128, :], writes=[tg + "tab"])
                P.c("act", (lambda e, qa=qa, i=i, jk=jk: e.activation(out=jk[:], in_=qa[:], func=AF.Square, accum_out=ss[:, 40 + i:41 + i])),
                    reads=[tg + "qa"], writes=[jkk, "qs%d" % i])
                rsqrt(P, ss[:, 40 + i:41 + i], ss[:, 40 + i:41 + i], 1.0 / 256, EPS, ["qs%d" % i], ["qs%d" % i])
                P.c("dve", (lambda e, qa=qa, qan=qan, i=i: e.scalar_tensor_tensor(out=qan[:], in0=qa[:], scalar=ss[:, 40 + i:41 + i], in1=G[:, l, 0:256],
                                                                                 op0=ALU.mult, op1=ALU.mult)), reads=[tg + "qa", "qs%d" % i, "gains"], writes=[tg + "qan"])
                for j in range(2):
                    P.c("pe", (lambda e, qan=qan, j=j, pT=pT: e.transpose(pT[:, j, :], qan[:, j * 128:(j + 1) * 128], k.ident[:])), reads=[tg + "qan", "ident"], writes=[pTk])
                P.c("act", (lambda e, qT_=qT_, pT=pT: e.copy(out=qT_[:], in_=pT[:])), reads=[pTk], writes=[tg + "qT"])
                for j in range(2):
                    P.c("pe", (lambda e, qT_=qT_, j=j, pkv=pkv: e.matmul(pkv[:, 0:384], lhsT=qT_[:, j, :], rhs=Wq[:, j, :], start=(j == 0), stop=(j == 1))),
                        reads=[tg + "qT", "Wq"], writes=[pkvk])
                pq = pkv[:, 0:384].rearrange("p (h c) -> p h c", h=4)
                head_norm(P, p2, tg + "hn", pq, 4, 96, G[:, l, 384:480].unsqueeze(1).to_broadcast([128, 4, 96]), qn[:], [pkvk], [tg + "qn"])
                P.c("act", (lambda e, qn=qn, qfin=qfin: e.copy(out=qfin[:, :, 0:64], in_=qn[:, :, 0:64])), reads=[tg + "qn"], writes=[tg + "qfinA"])
                if i < 16:
                    rope_apply(P, p2, tg + "rp", qn[:, :, 64:96], tab, qfin[:, :, 64:96], [tg + "qn", tg + "tab"], [tg + "qfinB"])
                else:
                    P.c("dve", (lambda e, qn=qn, qfin=qfin: e.tensor_copy(out=qfin[:, :, 64:96], in_=qn[:, :, 64:96])), reads=[tg + "qn"], writes=[tg + "qfinB"])
                heads_to_fm(P, pt4, qfin, 96, QT[:, :, i * 128:(i + 1) * 128], [tg + "qfinA", tg + "qfinB"], pt4k, ["QT%d" % i], k.ident,
                            eng=("act" if i % 2 else "dve"))
            P.flush()
        with ExitStack() as es3:
            p3 = Pool_(nc, es3)
            r = attn_alloc(nc, p3)
            sc = 96 ** -0.5
            for qi in range(16 if last else 18):
                cl = list(range(34)) if qi < 16 else [32, 33]
                attn_tile(k, r, (lambda h, qi=qi: (QT[:, h, qi * 128:(qi + 1) * 128], [])), 96,
                          (lambda h, cl=cl: [(KT[:, h, c * 128:(c + 1) * 128], VP[:, c, h, :], []) for c in cl]), sc, 2, qi * 128)
                if qi % 4 == 3:
                    P.flush()
            P.flush()


def stage_na(k, l):
    nc, P, I = k.nc, k.P, k.I
    last = (l == DEPTH - 1)
    G = k.gains
    NS = 3
    with ExitStack() as es:
        pl = Pool_(nc, es)
        KT = pl.sb([64, 4, 22 * 128], BF, "nKT")
        VP = pl.sb([128, 22, 4, 65], BF, "nVP")
        QT = pl.sb([64, 4, TA], BF, "nQT")
        with ExitStack() as es2:
            p2 = Pool_(nc, es2)
            P.c("pool", lambda e: e.memset(VP[:, :, :, 64:65], 1.0), writes=["VP1"])
            for i in range(22):
                if i < 2:
                    src = k.GN[256 + i * 128:256 + (i + 1) * 128, :]
                elif i < 18:
                    src = k.U[(i - 2) * 128:(i - 1) * 128, 160:672]
                elif i < 20:
                    src = k.GN[512 + (i - 18) * 128:512 + (i - 17) * 128, :]
                else:
                    src = k.U[T + (i - 20) * 128:T + (i - 19) * 128, 160:672]
                tg = "nk%d" % (i % NS)
                kv = p2.get(tg + "kv", [128, 512])
                kfin = p2.get(tg + "kfin", [128, 4, 64], BF)
                pt4 = p2.get("npt4%d" % (i % 2), [128, 4, 128], BF, psum=True)
                P.dma("sp", kv[:], src, writes=[tg + "kv"])
                P.c("act", (lambda e, kv=kv, i=i: e.copy(out=VP[:, i, :, 0:64], in_=kv[:, 256:512].rearrange("p (h d) -> p h d", h=4))),
                    reads=[tg + "kv"], writes=["VP%d" % i])
                head_norm(P, p2, tg + "hn", kv[:, 0:256].rearrange("p (h d) -> p h d", h=4), 4, 64,
                          G[:, l, 640:704].unsqueeze(1).to_broadcast([128, 4, 64]), kfin[:], [tg + "kv"], [tg + "kfin"])
                heads_to_fm(P, pt4, kfin, 64, KT[:, :, i * 128:(i + 1) * 128], [tg + "kfin"], "npt4%d" % (i % 2), ["KT%d" % i], k.ident,
                            eng=("act" if i % 2 else "dve"))
            for i in range(16 if last else 18):
                tg = "nq%d" % (i % NS)
                q = p2.get(tg + "q", [128, 256])
                qfin = p2.get(tg + "qfin", [128, 4, 64], BF)
                pt4 = p2.get("npt4%d" % (i % 2), [128, 4, 128], BF, psum=True)
                P.dma("sp", q[:], k.U[i * 128:(i + 1) * 128, 1184:1440], writes=[tg + "q"])
                head_norm(P, p2, tg + "hn", q[:].rearrange("p (h d) -> p h d", h=4), 4, 64,
                          G[:, l, 576:640].unsqueeze(1).to_broadcast([128, 4, 64]), qfin[:], [tg + "q"], [tg + "qfin"])
                heads_to_fm(P, pt4, qfin, 64, QT[:, :, i * 128:(i + 1) * 128], [tg + "qfin"], "npt4%d" % (i % 2), ["QT%d" % i], k.ident,
                            eng=("act" if i % 2 else "dve"))
            P.flush()
        with ExitStack() as es3:
            p3 = Pool_(nc, es3)
            r = attn_alloc(nc, p3)
            bb = [p3.sb([128, 6, 4, 128], F32, "nab") for _ in range(2)]
            sc = 64 ** -0.5
            for qi in range(16 if last else 18):
                if qi < 16:
                    t0 = 0 if qi == 0 else (14 if qi == 15 else qi)
                    nl = 6 if qi in (0, 15) else 5
                    cl = list(range(t0, t0 + nl)) + [20, 21]
                    B = bb[qi % 2]
                    P.dma("sp", B[:], I["nab"][l, qi], writes=["bias"])
                    bias = (lambda h, B=B: B[:, :, h, :])
                else:
                    cl, nl, bias = [20, 21], 0, None
                attn_tile(k, r, (lambda h, qi=qi: (QT[:, h, qi * 128:(qi + 1) * 128], [])), 64,
                          (lambda h, cl=cl: [(KT[:, h, c * 128:(c + 1) * 128], VP[:, c, h, :], []) for c in cl]), sc, 4, qi * 128,
                          bias=bias, nloc=nl)
                if qi % 4 == 3:
                    P.flush()
            P.flush()


def stage_attn(k, l, which):
    mla, na = (which == "mla"), (which == "na")
    nc, P, I = k.nc, k.P, k.I
    last = (l == DEPTH - 1)
    G = k.gains
    NS = 3
    nq = 16 if last else 18
    with ExitStack() as es:
        pl = Pool_(nc, es)
        if mla:
            KT = pl.sb([96, 4, 34 * 128], BF, "KT")
            VP = pl.sb([128, 34, 4, 65], BF, "VP")
            QT = pl.sb([96, 4, TA], BF, "QT")
        else:
            nKT = pl.sb([64, 4, 22 * 128], BF, "nKT")
            nVP = pl.sb([128, 22, 4, 65], BF, "nVP")
            nQT = pl.sb([64, 4, TA], BF, "nQT")
        with ExitStack() as es2:
            p2 = Pool_(nc, es2)
            if mla:
                Wkv = p2.sb([128, 512], BF, "wkv")
                Wq = p2.sb([128, 2, 384], BF, "wq")
                P.dma("pool", Wkv[:], I["mla_wkv_b"][l], writes=["Wkv"])
                P.dma("pool", Wq[:], I["mla_wq_b"][l].rearrange("(j p) n -> p j n", p=128), writes=["Wq"])
                P.c("pool", lambda e: e.memset(VP[:, :, :, 64:65], 1.0), writes=["VP1"])
                ss = p2.sb([128, 64], F32, "ss")
                junkk = [p2.sb([128, 160], F32, "junkk") for _ in range(4)]
                junkq = [p2.sb([128, 256], F32, "junkq") for _ in range(4)]
                pkvs = [p2.ps([128, 512], F32, "pkv") for _ in range(4)]
                pt4s = [p2.ps([128, 4, 128], BF, "pt4") for _ in range(4)]
            else:
                P.c("pool", lambda e: e.memset(nVP[:, :, :, 64:65], 1.0), writes=["nVP1"])
                npt4s = [p2.ps([128, 4, 128], BF, "npt4") for _ in range(4)]

            def ktile(i):
                sid = i % 2
                tg = "k%d_%d" % (sid, (i // 2) % 2)
                src = k.G1[i * 128:(i + 1) * 128, 0:160] if i < 32 else k.U[T + (i - 32) * 128:T + (i - 31) * 128, 0:160]
                kva = p2.get(tg + "kva", [128, 160])
                ckvn = p2.get(tg + "ckvn", [128, 128], BF)
                cT_ = p2.get(tg + "cT", [128, 128], BF)
                sr = p2.get(tg + "sr", [128, 1])
                kfin = p2.get(tg + "kfin", [128, 4, 96], BF)
                t32 = p2.get(tg + "t32", [128, 32])
                kr = p2.get(tg + "kr", [128, 4, 32])
                tab = p2.get(tg + "tab", [128, 32])
                pkv, pt4 = pkvs[sid], pt4s[sid]
                pT = pt4
                pTk, pkvk, pt4k = "pt4%d" % sid, "pkv%d" % sid, "pt4%d" % sid
                jk = junkk[sid * 2 + (i // 2) % 2]
                jkk = "junkk%d" % (sid * 2 + (i // 2) % 2)
                P.dma("sp", kva[:], src, writes=[tg + "kva"])
                if i < 32:
                    P.dma("sp", tab[:], I["ropek"][i * 128:(i + 1) * 128, :], writes=[tg + "tab"])
                P.c("act", (lambda e: e.activation(out=jk[:, 0:128], in_=kva[:, 0:128], func=AF.Square, accum_out=ss[:, i:i + 1])),
                    reads=[tg + "kva"], writes=[jkk, "ss%d" % i])
                P.c("act", (lambda e: e.activation(out=jk[:, 128:160], in_=kva[:, 128:160], func=AF.Square, accum_out=sr[:])),
                    reads=[tg + "kva"], writes=[jkk + "b", tg + "sr"])
                rsqrt(P, ss[:, i:i + 1], ss[:, i:i + 1], 1.0 / 128, EPS, ["ss%d" % i], ["ss%d" % i])
                P.c("dve", (lambda e: e.scalar_tensor_tensor(out=ckvn[:], in0=kva[:, 0:128], scalar=ss[:, i:i + 1], in1=G[:, l, 256:384],
                                                             op0=ALU.mult, op1=ALU.mult)), reads=[tg + "kva", "ss%d" % i, "gains"], writes=[tg + "ckvn"])
                P.c("pe", (lambda e: e.transpose(pT[:, 0, :], ckvn[:], k.ident[:])), reads=[tg + "ckvn", "ident"], writes=[pTk])
                P.c("act", (lambda e: e.copy(out=cT_[:], in_=pT[:, 0, :])), reads=[pTk], writes=[tg + "cT"])
                P.c("pe", (lambda e: e.matmul(pkv[:], lhsT=cT_[:], rhs=Wkv[:], start=True, stop=True)), reads=[tg + "cT", "Wkv"], writes=[pkvk])
                pv = pkv[:].rearrange("p (h c) -> p h c", h=4)
                P.c("act", (lambda e: e.copy(out=VP[:, i, :, 0:64], in_=pv[:, :, 64:128])), reads=[pkvk], writes=["VP%d" % i])
                rh = head_norm(P, p2, tg + "hn", pv[:, :, 0:64], 4, 64, G[:, l, 480:544].unsqueeze(1).to_broadcast([128, 4, 64]), kfin[:, :, 0:64],
                               [pkvk], [tg + "kfinA"], extra_ss=(sr[:], tg + "sr", 96))
                P.c("dve", (lambda e: e.tensor_tensor(out=t32[:], in0=kva[:, 128:160], in1=G[:, l, 544:576], op=ALU.mult)),
                    reads=[tg + "kva", "gains"], writes=[tg + "t32"])
                if i < 32:
                    P.c("dve", (lambda e: e.tensor_tensor(out=kr[:], in0=t32[:].unsqueeze(1).to_broadcast([128, 4, 32]),
                                                          in1=rh[:].unsqueeze(2).to_broadcast([128, 4, 32]), op=ALU.mult)),
                        reads=[tg + "t32", tg + "hnssh"], writes=[tg + "kr"])
                    rope_apply(P, p2, tg + "rp", kr[:], tab, kfin[:, :, 64:96], [tg + "kr", tg + "tab"], [tg + "kfinB"])
                else:
                    P.c("dve", (lambda e: e.tensor_tensor(out=kfin[:, :, 64:96], in0=t32[:].unsqueeze(1).to_broadcast([128, 4, 32]),
                                                          in1=rh[:].unsqueeze(2).to_broadcast([128, 4, 32]), op=ALU.mult)),
                        reads=[tg + "t32", tg + "hnssh"], writes=[tg + "kfinB"])
                heads_to_fm(P, pt4, kfin, 96, KT[:, :, i * 128:(i + 1) * 128], [tg + "kfinA", tg + "kfinB"], pt4k, ["KT%d" % i], k.ident,
                            eng=("act" if i % 2 else "dve"))

            def qtile(i):
                sid = 2 + i % 2
                tg = "q%d_%d" % (sid, (i // 2) % 2)
                qa = p2.get(tg + "qa", [128, 256])
                qan = p2.get(tg + "qan", [128, 256], BF)
                qT_ = p2.get(tg + "qanT", [128, 2, 128], BF)
                qn = p2.get(tg + "qn", [128, 4, 96])
                qfin = p2.get(tg + "qfin", [128, 4, 96], BF)
                tab = p2.get(tg + "tab", [128, 32])
                pkv, pt4 = pkvs[sid], pt4s[sid]
                pT = pt4
                pTk, pkvk, pt4k = "pt4%d" % sid, "pkv%d" % sid, "pt4%d" % sid
                jk = junkq[(sid - 2) * 2 + (i // 2) % 2]
                jkk = "junkq%d" % ((sid - 2) * 2 + (i // 2) % 2)
                P.dma("sp", qa[:], k.U[i * 128:(i + 1) * 128, 928:1184], writes=[tg + "qa"])
                if i < 16:
                    P.dma("sp", tab[:], I["ropeq"][i * 128:(i + 1) * 128, :], writes=[tg + "tab"])
                P.c("act", (lambda e: e.activation(out=jk[:], in_=qa[:], func=AF.Square, accum_out=ss[:, 40 + i:41 + i])),
                    reads=[tg + "qa"], writes=[jkk, "qs%d" % i])
                rsqrt(P, ss[:, 40 + i:41 + i], ss[:, 40 + i:41 + i], 1.0 / 256, EPS, ["qs%d" % i], ["qs%d" % i])
                P.c("dve", (lambda e: e.scalar_tensor_tensor(out=qan[:], in0=qa[:], scalar=ss[:, 40 + i:41 + i], in1=G[:, l, 0:256],
                                                             op0=ALU.mult, op1=ALU.mult)), reads=[tg + "qa", "qs%d" % i, "gains"], writes=[tg + "qan"])
                for j in range(2):
                    P.c("pe", (lambda e, j=j: e.transpose(pT[:, j, :], qan[:, j * 128:(j + 1) * 128], k.ident[:])), reads=[tg + "qan", "ident"], writes=[pTk])
                P.c("act", (lambda e: e.copy(out=qT_[:], in_=pT[:, 0:2, :])), reads=[pTk], writes=[tg + "qT"])
                for j in range(2):
                    P.c("pe", (lambda e, j=j: e.matmul(pkv[:, 0:384], lhsT=qT_[:, j, :], rhs=Wq[:, j, :], start=(j == 0), stop=(j == 1))),
                        reads=[tg + "qT", "Wq"], writes=[pkvk])
                pq = pkv[:, 0:384].rearrange("p (h c) -> p h c", h=4)
                head_norm(P, p2, tg + "hn", pq, 4, 96, G[:, l, 384:480].unsqueeze(1).to_broadcast([128, 4, 96]), qn[:], [pkvk], [tg + "qn"])
                P.c("act", (lambda e: e.copy(out=qfin[:, :, 0:64], in_=qn[:, :, 0:64])), reads=[tg + "qn"], writes=[tg + "qfinA"])
                if i < 16:
                    rope_apply(P, p2, tg + "rp", qn[:, :, 64:96], tab, qfin[:, :, 64:96], [tg + "qn", tg + "tab"], [tg + "qfinB"])
                else:
                    P.c("dve", (lambda e: e.tensor_copy(out=qfin[:, :, 64:96], in_=qn[:, :, 64:96])), reads=[tg + "qn"], writes=[tg + "qfinB"])
                heads_to_fm(P, pt4, qfin, 96, QT[:, :, i * 128:(i + 1) * 128], [tg + "qfinA", tg + "qfinB"], pt4k, ["QT%d" % i], k.ident,
                            eng=("act" if i % 2 else "dve"))

            def nktile(i):
                if i < 2:
                    src = k.GN[256 + i * 128:256 + (i + 1) * 128, :]
                elif i < 18:
                    src = k.U[(i - 2) * 128:(i - 1) * 128, 160:672]
                elif i < 20:
                    src = k.GN[512 + (i - 18) * 128:512 + (i - 17) * 128, :]
                else:
                    src = k.U[T + (i - 20) * 128:T + (i - 19) * 128, 160:672]
                tg = "nk%d_%d" % (i % 2, (i // 2) % 2)
                kv = p2.get(tg + "kv", [128, 512])
                kfin = p2.get(tg + "kfin", [128, 4, 64], BF)
                P.dma("sp", kv[:], src, writes=[tg + "kv"])
                P.c("act", (lambda e: e.copy(out=nVP[:, i, :, 0:64], in_=kv[:, 256:512].rearrange("p (h d) -> p h d", h=4))),
                    reads=[tg + "kv"], writes=["nVP%d" % i])
                head_norm(P, p2, tg + "hn", kv[:, 0:256].rearrange("p (h d) -> p h d", h=4), 4, 64,
                          G[:, l, 640:704].unsqueeze(1).to_broadcast([128, 4, 64]), kfin[:], [tg + "kv"], [tg + "kfin"])
                heads_to_fm(P, npt4s[i % 2], kfin, 64, nKT[:, :, i * 128:(i + 1) * 128], [tg + "kfin"], "npt4%d" % (i % 2), ["nKT%d" % i], k.ident,
                            eng=("act" if i % 2 else "dve"))

            def nqtile(i):
                tg = "nq%d_%d" % (i % 2, (i // 2) % 2)
                q = p2.get(tg + "q", [128, 256])
                qfin = p2.get(tg + "qfin", [128, 4, 64], BF)
                P.dma("sp", q[:], k.U[i * 128:(i + 1) * 128, 1184:1440], writes=[tg + "q"])
                head_norm(P, p2, tg + "hn", q[:].rearrange("p (h d) -> p h d", h=4), 4, 64,
                          G[:, l, 576:640].unsqueeze(1).to_broadcast([128, 4, 64]), qfin[:], [tg + "q"], [tg + "qfin"])
                heads_to_fm(P, npt4s[2 + i % 2], qfin, 64, nQT[:, :, i * 128:(i + 1) * 128], [tg + "qfin"], "npt4%d" % (2 + i % 2), ["nQT%d" % i], k.ident,
                            eng=("dve" if i % 2 else "act"))

            def run(fn, idxs):
                return lambda: [fn(i) for i in idxs]
            if mla:
                P.record_streams([run(ktile, range(0, 34, 2)), run(ktile, range(1, 34, 2)),
                                  run(qtile, range(0, nq, 2)), run(qtile, range(1, nq, 2))])
            else:
                P.record_streams([run(nktile, range(0, 22, 2)), run(nktile, range(1, 22, 2)),
                                  run(nqtile, range(0, nq, 2)), run(nqtile, range(1, nq, 2))])
            P.flush()
        with ExitStack() as es3:
            p3 = Pool_(nc, es3)
            r = attn_alloc(nc, p3)
            if na:
                bb = [p3.sb([128, 6, 4, 128], F32, "nab") for _ in range(2)]
            sc = 96 ** -0.5

            def mla_sweep():
                for qi in range(nq if mla else 0):
                    cl = list(range(34)) if qi < 16 else [32, 33]
                    attn_tile(k, r, (lambda h, qi=qi: (QT[:, h, qi * 128:(qi + 1) * 128], [])), 96,
                              (lambda h, cl=cl: [(KT[:, h, c * 128:(c + 1) * 128], VP[:, c, h, :], []) for c in cl]), sc, 2, qi * 128)
            if mla and l + 1 < DEPTH:
                P.record_streams([mla_sweep, lambda: mods_ops(k, p3, l + 1)])
            else:
                mla_sweep()
            sc = 64 ** -0.5
            for qi in range(nq if na else 0):
                if qi < 16:
                    t0 = 0 if qi == 0 else (14 if qi == 15 else qi)
                    nl = 6 if qi in (0, 15) else 5
                    cl = list(range(t0, t0 + nl)) + [20, 21]
                    B = bb[qi % 2]
                    bk = "bias%d" % (qi % 2)
                    P.dma("sp", B[:], I["nab"][l, qi], writes=[bk])
                    bias = (lambda h, B=B: B[:, :, h, :])
                else:
                    cl, nl, bias, bk = [20, 21], 0, None, "bias0"
                attn_tile(k, r, (lambda h, qi=qi: (nQT[:, h, qi * 128:(qi + 1) * 128], [])), 64,
                          (lambda h, cl=cl: [(nKT[:, h, c * 128:(c + 1) * 128], nVP[:, c, h, :], []) for c in cl]), sc, 4, qi * 128,
                          bias=bias, nloc=nl, bkey=bk)
            P.flush()


def stage_hgrn(k, l, upto=None):
    nc, P, I = k.nc, k.P, k.I
    last = (l == DEPTH - 1)
    NCH = 36
    G = k.gains
    if not hasattr(k, "EXSin"):
        k.EXSin = nc.dram_tensor("EXSin", [256, 256], F32, kind="Internal").ap()
        k.GS = nc.dram_tensor("GS", [512, 256], F32, kind="Internal").ap()
    with ExitStack() as es:
        pl = Pool_(nc, es)
        LB = pl.sb([128, 2, 2], F32, "LB")
        OML = pl.sb([128, 2, 2], F32, "OML")
        lbt = pl.sb([128, 3, 4], F32, "lbt")
        lsum = pl.sb([128, 4], F32, "lsum")
        P.dma("sp", lbt[:], I["hg_lbT"].rearrange("p s d h -> p s (d h)"), writes=["lbt"])
        P.c("act", lambda e: e.activation(out=lbt[:], in_=lbt[:], func=AF.Exp), reads=["lbt"], writes=["lbt"])
        P.c("dve", lambda e: e.tensor_tensor(out=lsum[:], in0=lbt[:, 0, :], in1=lbt[:, 1, :], op=ALU.add), reads=["lbt"], writes=["lsum"])
        P.c("dve", lambda e: e.tensor_tensor(out=lsum[:], in0=lsum[:], in1=lbt[:, 2, :], op=ALU.add), reads=["lsum", "lbt"], writes=["lsum"])
        P.c("dve", lambda e: e.reciprocal(out=lsum[:], in_=lsum[:]), reads=["lsum"], writes=["lsum"])
        LBf = LB[:].rearrange("p d h -> p (d h)")
        OMf = OML[:].rearrange("p d h -> p (d h)")
        if l == 0:
            P.c("dve", lambda e: e.tensor_tensor(out=LBf, in0=lbt[:, 0, :], in1=lsum[:], op=ALU.mult), reads=["lsum", "lbt"], writes=["LB"])
        else:
            P.c("dve", lambda e: e.tensor_tensor(out=LBf, in0=lbt[:, 0, :], in1=lbt[:, 1, :], op=ALU.add), reads=["lbt"], writes=["LB"])
            P.c("dve", lambda e: e.tensor_tensor(out=LBf, in0=LBf, in1=lsum[:], op=ALU.mult), reads=["lsum", "LB"], writes=["LB"])
        P.c("dve", lambda e: e.tensor_scalar(out=LBf, in0=LBf, scalar1=1e-6, scalar2=1.0 - 1e-6, op0=ALU.max, op1=ALU.min), reads=["LB"], writes=["LB"])
        P.c("dve", lambda e: e.tensor_scalar(out=OMf, in0=LBf, scalar1=-1.0, scalar2=1.0, op0=ALU.mult, op1=ALU.add), reads=["LB"], writes=["OML"])
        zt = pl.sb([128, 256], F32, "zt")
        P.c("pool", lambda e: e.memset(zt[:], 0.0), writes=["zt"])
        for d_ in range(2):
            P.dma("sp", k.EXSin[d_ * 128:(d_ + 1) * 128, :], zt[:], reads=["zt"], writes=["EXSin"])
        cmask = pl.sb([128, TA // 2], F32, "cmask")
        P.c("pool", lambda e: e.memset(cmask[:], 1.0), writes=["cmask"])
        P.c("pool", lambda e: e.memset(cmask[:].rearrange("p (c t) -> p c t", t=64)[:, :, 0:1], 0.0), reads=["cmask"], writes=["cmask"])
        mk = pl.sb([64, 2, 64], F32, "trimask")
        P.c("pool", lambda e: e.memset(mk[:], 1.0), writes=["mk"])
        P.c("pool", lambda e: e.affine_select(out=mk[:, 0, :], in_=mk[:, 0, :], compare_op=ALU.is_ge, fill=0.0, base=0, pattern=[[1, 64]], channel_multiplier=-1),
            reads=["mk"], writes=["mk"])
        P.c("pool", lambda e: e.affine_select(out=mk[:, 1, :], in_=mk[:, 1, :], compare_op=ALU.is_ge, fill=0.0, base=0, pattern=[[-1, 64]], channel_multiplier=1),
            reads=["mk"], writes=["mk"])
        mka = pl.sb([64, 2, 64], F32, "mka")
        P.c("pool", lambda e: e.memset(mka[:], 0.0), writes=["mka"])
        P.c("pool", lambda e: e.memset(mka[0:32, 0, 32:64], 1.0), reads=["mka"], writes=["mka"])
        P.c("pool", lambda e: e.memset(mka[32:64, 1, 0:32], 1.0), reads=["mka"], writes=["mka"])
        P.c("pool", lambda e: e.memset(mk[0:32, 0, 32:64], 0.0), reads=["mk"], writes=["mk"])
        P.c("pool", lambda e: e.memset(mk[32:64, 1, 0:32], 0.0), reads=["mk"], writes=["mk"])
        mk4 = pl.sb([64, 4, 64], F32, "mk4")
        P.c("dve", lambda e: e.tensor_copy(out=mk4[:, 0:2, :], in_=mk[:]), reads=["mk"], writes=["mk4"])
        P.c("dve", lambda e: e.tensor_copy(out=mk4[:, 2:4, :], in_=mka[:]), reads=["mka"], writes=["mk4"])
        bm = pl.sb([128, 128], F32, "bm")
        P.c("pool", lambda e: e.memset(bm[:], 0.0), writes=["bm"])
        P.c("pool", lambda e: e.memset(bm[0:64, 0:64], 1.0), reads=["bm"], writes=["bm"])
        P.c("pool", lambda e: e.memset(bm[64:128, 64:128], 1.0), reads=["bm"], writes=["bm"])
        pm = pl.sb([128, 2], F32, "pm")
        P.c("pool", lambda e: e.memset(pm[:], 0.0), writes=["pm"])
        P.c("pool", lambda e: e.memset(pm[0:64, 0:1], 1.0), reads=["pm"], writes=["pm"])
        P.c("pool", lambda e: e.memset(pm[64:128, 1:2], 1.0), reads=["pm"], writes=["pm"])
        P.flush()
        nch_out = 32 if last else 36
        for hp in range(2):
            with ExitStack() as es2:
                p2 = Pool_(nc, es2)
                QT_ = p2.sb([128, 2, TA], BF, "hq")
                KT_ = p2.sb([128, 2, TA], BF, "hk")
                QE = p2.sb([128, 2, TA], BF, "hqe")
                KA = p2.sb([128, 2, TA], BF, "hka")
                EBL = p2.sb([128, 2, NCH], F32, "ebl")
                BLs = p2.sb([128, 2, NCH], F32, "bls")
                KH = p2.sb([64, NCH, 2, 128], BF, "KH")
                IT = p2.sb([64, NCH, 128], BF, "IT")
                SB = p2.sb([128, 2, NCH, 128], BF, "SB")
                SG = p2.sb([64, NCH, 128], F32, "SG")
                for c9 in range(0, NCH, 9):
                    P.dma("pool", IT[:, c9:c9 + 9, :], k.U[c9 * 64:(c9 + 9) * 64, 672 + hp * 128:672 + (hp + 1) * 128].rearrange("(c s) n -> s c n", s=64), writes=["IT%d" % c9])
                    P.dma("sp", SG[:, c9:c9 + 9, :], k.U[c9 * 64:(c9 + 9) * 64, 1440 + hp * 128:1440 + (hp + 1) * 128].rearrange("(c s) n -> s c n", s=64), writes=["SG"])
                P.c("act", lambda e: e.activation(out=SG[:], in_=SG[:], func=AF.Exp, scale=-1.0), reads=["SG"], writes=["SG"])
                P.c("dve", lambda e: e.tensor_scalar_add(out=SG[:], in0=SG[:], scalar1=1.0), reads=["SG"], writes=["SG"])
                P.c("dve", lambda e: e.reciprocal(out=SG[:], in_=SG[:]), reads=["SG"], writes=["SG"])
                NTB = 3
                for tb in range(NTB):
                  with ExitStack() as es3:
                    TB, NB = TA // NTB, NCH // NTB
                    tsl = slice(tb * TB, (tb + 1) * TB)
                    csl = slice(tb * NB, (tb + 1) * NB)
                    p3 = Pool_(nc, es3)
                    q = p3.sb([128, TB], F32, "hqq")
                    P.dma("sp", q[:], k.UT[768 + hp * 128:768 + (hp + 1) * 128, tsl], writes=["q"])

                    def dstream(d):
                        sd = "%d" % d
                        f = p3.sb([128, TB], F32, "hf")
                        lf = p3.sb([128, TB], F32, "hlf")
                        Pc = p3.sb([128, TB], F32, "hP")
                        b_ = p3.sb([128, TB], F32, "hb")
                        kk = p3.sb([128, TB], F32, "hkk")
                        kh = p3.sb([128, TB], BF, "hkh")
                        mref = p3.sb([128, NB, 2], F32, "hmref")
                        blt = p3.sb([128, NB], F32, "hblt")
                        c0t = p3.sb([128, NB], F32, "hc0t")
                        ptk = p3.ps([64, 4, 128], BF, "ptk")
                        lbs, oms = LB[:, d, hp:hp + 1], OML[:, d, hp:hp + 1]
                        kf, klf, kP, kb, kkk = "f" + sd, "lf" + sd, "Pc" + sd, "b" + sd, "kk" + sd
                        P.dma("sp", f[:], k.UT[256 + d * 256 + hp * 128:256 + d * 256 + (hp + 1) * 128, tsl], writes=[kf])
                        P.c("act", lambda e: e.activation(out=f[:], in_=f[:], func=AF.Sigmoid), reads=[kf], writes=[kf])
                        P.c("act", lambda e: e.activation(out=f[:], in_=f[:], func=AF.Identity, scale=oms, bias=lbs), reads=[kf, "LB", "OML"], writes=[kf])
                        P.c("act", lambda e: e.activation(out=lf[:], in_=f[:], func=AF.Ln), reads=[kf], writes=[klf])
                        P.c("act", lambda e: e.activation(out=kk[:], in_=f[:], func=AF.Identity, scale=-1.0, bias=1.0), reads=[kf], writes=[kkk])
                        P.c("dve", lambda e: e.tensor_tensor_scan(out=Pc[:], data0=cmask[:, 0:TB], data1=lf[:], initial=0.0, op0=ALU.mult, op1=ALU.add),
                            reads=[klf, "cmask"], writes=[kP])
                        Pv = Pc[:].rearrange("p (c t) -> p c t", t=64)
                        P.c("dve", lambda e: e.tensor_copy(out=blt[:], in_=Pv[:, :, 63]), reads=[kP], writes=["blt" + sd])
                        P.c("dve", lambda e: e.tensor_copy(out=c0t[:], in_=Pv[:, :, 31]), reads=[kP], writes=["c0t" + sd])
                        blv, c0v = blt[:], c0t[:]
                        blb = blv.unsqueeze(2).to_broadcast([128, NB, 64])
                        bv = b_[:].rearrange("p (c t) -> p c t", t=64)
                        lfv = lf[:].rearrange("p (c t) -> p c t", t=64)
                        rk = [kP, "blt" + sd, "c0t" + sd]
                        P.c("act", lambda e: e.activation(out=EBL[:, d, csl], in_=blv, func=AF.Exp), reads=rk, writes=["EBL" + sd])
                        P.c("dve", lambda e: e.tensor_copy(out=BLs[:, d, csl], in_=blv), reads=["blt" + sd], writes=["BLs" + sd])
                        if d == 0:
                            P.c("act", lambda e: e.copy(out=b_[:], in_=Pc[:]), reads=rk, writes=[kb])
                            P.c("dve", lambda e: e.tensor_scalar_mul(out=mref[:, :, 0], in0=c0v, scalar1=0.5), reads=rk, writes=["mref0" + sd])
                            P.c("dve", lambda e: e.tensor_tensor(out=mref[:, :, 1], in0=c0v, in1=blv, op=ALU.add), reads=rk, writes=["mref1" + sd])
                            P.c("dve", lambda e: e.tensor_scalar_mul(out=mref[:, :, 1], in0=mref[:, :, 1], scalar1=0.5), reads=["mref1" + sd], writes=["mref1" + sd])
                        else:
                            P.c("dve", lambda e: e.tensor_tensor(out=bv, in0=blb, in1=Pv, op=ALU.subtract), reads=rk, writes=[kb])
                            P.c("dve", lambda e: e.tensor_tensor(out=bv, in0=bv, in1=lfv, op=ALU.add), reads=[kb, klf], writes=[kb])
                            P.c("dve", lambda e: e.scalar_tensor_tensor(out=mref[:, :, 0], in0=c0v, scalar=-0.5, in1=blv, op0=ALU.mult, op1=ALU.add),
                                reads=rk, writes=["mref0" + sd])
                            P.c("dve", lambda e: e.tensor_tensor(out=mref[:, :, 1], in0=blv, in1=c0v, op=ALU.subtract), reads=rk, writes=["mref1" + sd])
                            P.c("dve", lambda e: e.tensor_scalar_mul(out=mref[:, :, 1], in0=mref[:, :, 1], scalar1=0.5), reads=["mref1" + sd], writes=["mref1" + sd])
                        P.c("act", lambda e: e.activation(out=f[:], in_=b_[:], func=AF.Exp), reads=[kb], writes=[kf])
                        P.c("dve", lambda e: e.scalar_tensor_tensor(out=QE[:, d, tsl], in0=q[:], scalar=0.125, in1=f[:], op0=ALU.mult, op1=ALU.mult),
                            reads=["q", kf], writes=["QE" + sd])
                        P.c("dve", lambda e: e.tensor_tensor(out=Pv, in0=blb, in1=bv, op=ALU.subtract), reads=[kb, kP, "blt" + sd], writes=[kP])
                        P.c("act", lambda e: e.activation(out=Pc[:], in_=Pc[:], func=AF.Exp), reads=[kP], writes=[kP])
                        P.c("dve", lambda e: e.tensor_tensor(out=kh[:], in0=kk[:], in1=Pc[:], op=ALU.mult), reads=[kkk, kP], writes=["kh" + sd])
                        b4 = b_[:].rearrange("p (c j t) -> p c j t", j=2, t=32)
                        P4 = Pc[:].rearrange("p (c j t) -> p c j t", j=2, t=32)
                        k4 = kk[:].rearrange("p (c j t) -> p c j t", j=2, t=32)
                        KA4 = KA[:, d, tsl].rearrange("p (c j t) -> p c j t", j=2, t=32)
                        P.c("act", lambda e: e.activation(out=P4[:, :, d, :], in_=b4[:, :, d, :], func=AF.Exp, scale=-1.0), reads=[kb, "kh" + sd], writes=[kP])
                        P.c("dve", lambda e: e.scalar_tensor_tensor(out=KA4[:, :, d, :], in0=P4[:, :, d, :], scalar=5.0e34, in1=k4[:, :, d, :], op0=ALU.min, op1=ALU.mult),
                            reads=[kkk, kP], writes=["KA" + sd])
                        P.c("pool", lambda e: e.memset(KA4[:, :, 1 - d, :], 0.0), reads=[], writes=["KAz" + sd])
                        P.c("dve", lambda e: e.tensor_tensor(out=b4, in0=b4, in1=mref[:].unsqueeze(3).to_broadcast([128, NB, 2, 32]), op=ALU.subtract),
                            reads=[kb, "mref0" + sd, "mref1" + sd, kf, kP], writes=[kb])
                        P.c("act", lambda e: e.activation(out=f[:], in_=b_[:], func=AF.Exp), reads=[kb, "QE" + sd], writes=[kf])
                        P.c("act", lambda e: e.activation(out=lf[:], in_=b_[:], func=AF.Exp, scale=-1.0), reads=[kb], writes=[klf])
                        P.c("dve", lambda e: e.scalar_tensor_tensor(out=QT_[:, d, tsl], in0=q[:], scalar=0.125, in1=f[:], op0=ALU.mult, op1=ALU.mult),
                            reads=["q", kf], writes=["QT_" + sd])
                        P.c("dve", lambda e: e.tensor_tensor(out=KT_[:, d, tsl], in0=kk[:], in1=lf[:], op=ALU.mult), reads=[kkk, klf], writes=["KT_" + sd])
                        for c0 in range(0, NB, 4):
                            n4 = min(4, NB - c0)
                            for ci in range(n4):
                                c = c0 + ci
                                P.c("pe", (lambda e, ci=ci, c=c: e.transpose(ptk[:, ci, :], kh[:, c * 64:(c + 1) * 64], k.ident[:])),
                                    reads=["kh" + sd, "ident"], writes=["ptk" + sd])
                            P.c("dve", (lambda e, c0=c0, n4=n4: e.tensor_copy(out=KH[:, tb * NB + c0:tb * NB + c0 + n4, d, :], in_=ptk[:, 0:n4, :])),
                                reads=["ptk" + sd], writes=["KH" + sd])

                    P.record_streams([lambda: dstream(0), lambda: dstream(1)])
                    P.flush()
                if upto == "hgrn1":
                    continue
                with ExitStack() as es4:
                    p4 = Pool_(nc, es4)
                    S = [p4.sb([128, 128], F32, "S%d" % d) for d in range(2)]
                    psG = [[p4.ps([128, 128], F32, "psG") for _ in range(2)] for d in range(2)]
                    tmpG = [[p4.sb([128, 128], F32, "tmpG") for _ in range(2)] for d in range(2)]
                    Sc = [p4.sb([128, 128], F32, "Sc%d" % d) for d in range(2)]
                    E = p4.sb([128, 2, 128], F32, "Eend")
                    Gs = p4.sb([128, 2, 128], F32, "Gs")
                    ng = [0, 0]

                    def step(d, c, Sd):
                        b = ng[d] % 2
                        ng[d] += 1
                        pg, tg_, gk = psG[d][b], tmpG[d][b], "G%d_%d" % (d, b)
                        P.c("pe", (lambda e: e.matmul(pg[:], lhsT=KH[:, c, d, :], rhs=IT[:, c, :], start=True, stop=True)),
                            reads=["KH", "IT"], writes=["ps" + gk])
                        P.c("dve", (lambda e: e.tensor_tensor(out=tg_[:], in0=pg[:], in1=bm[:], op=ALU.mult)), reads=["ps" + gk, "bm"], writes=["tm" + gk])
                        P.c("dve", (lambda e: e.scalar_tensor_tensor(out=Sd[:], in0=Sd[:], scalar=EBL[:, d, c:c + 1], in1=tg_[:],
                                                                     op0=ALU.mult, op1=ALU.add)), reads=["tm" + gk, "S%d" % d, "EBL"], writes=["S%d" % d])

                    def save(d, c, Sd):
                        P.c("act", (lambda e: e.copy(out=SB[:, d, c, :], in_=Sd[:])), reads=["S%d" % d], writes=["SB%d" % d])

                    def chain(d):
                        P.c("pool", (lambda e: e.memset(S[d][:], 0.0)), writes=["S%d" % d])
                        for i in range(4):
                            c = 32 + i if d == 0 else 35 - i
                            save(d, c, S[d])
                            step(d, c, S[d])
                        P.c("dve", (lambda e: e.tensor_copy(out=Sc[d][:], in_=S[d][:])), reads=["S%d" % d], writes=["Sc%d" % d])
                        for i in range(32):
                            c = i if d == 0 else 31 - i
                            save(d, c, S[d])
                            step(d, c, S[d])
                        P.c("dve", (lambda e: e.tensor_copy(out=E[:, d, :], in_=S[d][:])), reads=["S%d" % d], writes=["E%d" % d])
                        P.dma("sp", k.EXSin[d * 128:(d + 1) * 128, hp * 128:(hp + 1) * 128], E[:, d, :], reads=["E%d" % d], writes=["EXSin%d" % d])

                    P.record_streams([lambda: chain(0), lambda: chain(1)])
                    PF = p4.sb([128, 2, 32], F32, "PF")
                    ones32 = p4.sb([128, 32], F32, "ones32")
                    P.c("pool", lambda e: e.memset(ones32[:], 1.0), writes=["ones32"])
                    for d in range(2):
                        P.c("dve", (lambda e, d=d: e.tensor_tensor_scan(out=PF[:, d, :], data0=ones32[:], data1=BLs[:, d, 0:32], initial=0.0, op0=ALU.mult, op1=ALU.add)),
                            reads=["ones32"], writes=["PF%d" % d])
                    Dc = p4.sb([128, 2, 32], F32, "Dc")
                    P.c("dve", lambda e: e.tensor_tensor(out=Dc[:, 0, :], in0=PF[:, 0, :], in1=BLs[:, 0, 0:32], op=ALU.subtract), reads=["PF0"], writes=["Dc0"])
                    P.c("dve", lambda e: e.tensor_tensor(out=Dc[:, 1, :], in0=PF[:, 1, 31:32].to_broadcast([128, 32]), in1=PF[:, 1, :], op=ALU.subtract), reads=["PF1"], writes=["Dc1"])
                    P.c("act", lambda e: e.activation(out=Dc[:], in_=Dc[:], func=AF.Exp), reads=["Dc0", "Dc1"], writes=["Dc"])
                    P.cc(lambda e: e.collective_compute("AllGather", ALU.bypass, replica_groups=k.groups, ins=[k.EXSin], outs=[k.GS]),
                         reads=["EXSin0", "EXSin1"], writes=["GS"])
                    P.dma("sp", Gs[:, 0, :], k.GS[0:128, hp * 128:(hp + 1) * 128], reads=["GS"], writes=["Gs"])
                    P.dma("sp", Gs[:, 1, :], k.GS[384:512, hp * 128:(hp + 1) * 128], reads=["GS"], writes=["Gs"])
                    tmpT = p4.sb([128, 32, 128], F32, "tmpT")
                    for d in range(2):
                        oth = 0 if d == 0 else 1
                        P.c("dve", (lambda e, d=d: e.tensor_tensor(out=S[d][:], in0=Gs[:, d, :], in1=Sc[d][:], op=ALU.subtract)), reads=["Gs", "Sc%d" % d], writes=["S%d" % d])
                        P.c("dve", (lambda e, d=d, oth=oth: e.tensor_scalar_mul(out=S[d][:], in0=S[d][:], scalar1=k.hm[:, oth:oth + 1])), reads=["S%d" % d, "hm"], writes=["S%d" % d])
                        P.c("dve", (lambda e, d=d: e.tensor_tensor(out=tmpT[:], in0=Dc[:, d, :].unsqueeze(2).to_broadcast([128, 32, 128]),
                                                                  in1=S[d][:].unsqueeze(1).to_broadcast([128, 32, 128]), op=ALU.mult)), reads=["S%d" % d, "Dc"], writes=["tmpT"])
                        P.c("dve", (lambda e, d=d: e.tensor_tensor(out=SB[:, d, 0:32, :], in0=SB[:, d, 0:32, :], in1=tmpT[:], op=ALU.add)), reads=["tmpT", "SB%d" % d], writes=["SB%d" % d])
                    P.flush()
                if upto == "hgrn2":
                    continue
                with ExitStack() as es5:
                    p5 = Pool_(nc, es5)
                    psAB = [[p5.ps([64, 4, 2, 64], F32, "psAB") for _ in range(2)] for s_ in range(2)]
                    psOl = [p5.ps([64, 128], F32, "psO") for s_ in range(2)]
                    ptyl = [p5.ps([128, 64], BF, "hpty") for s_ in range(2)]
                    tA = [[p5.sb([64, 4, 2, 64], F32, "htA") for _ in range(2)] for s_ in range(2)]
                    AT = [[p5.sb([64, 2, 2, 64], BF, "hAT") for _ in range(2)] for s_ in range(2)]
                    on = [p5.sb([64, 2, 64], F32, "hon") for s_ in range(2)]
                    yb = [p5.sb([64, 2, 64], BF, "hyb") for s_ in range(2)]
                    KTm = p5.sb([128, 2, 2, TA], BF, "KTm")
                    KAm = p5.sb([128, 2, 2, TA], BF, "KAm")
                    for hl in range(2):
                        P.c("dve", (lambda e, hl=hl: e.tensor_scalar_mul(out=KTm[:, hl], in0=KT_[:], scalar1=pm[:, hl:hl + 1])), reads=["pm"], writes=["KTm%d" % hl])
                        P.c("dve", (lambda e, hl=hl: e.tensor_scalar_mul(out=KAm[:, hl], in0=KA[:], scalar1=pm[:, hl:hl + 1])), reads=["pm"], writes=["KAm%d" % hl])
                    P.c("dve", lambda e: e.tensor_tensor(out=SG[:].rearrange("p c (h v) -> p c h v", h=2), in0=SG[:].rearrange("p c (h v) -> p c h v", h=2),
                                                         in1=G[0:64, l, 704:768].unsqueeze(1).unsqueeze(1).to_broadcast([64, NCH, 2, 64]), op=ALU.mult),
                        reads=["gains"], writes=["SGg"])

                    def A_mm(c):
                        st, sl = c % 2, (c // 2) % 2
                        pa, ta_, at_ = psAB[st][sl], tA[st][sl], AT[st][sl]
                        kk_ = "%d_%d" % (st, sl)
                        for d in range(2):
                            for hl in range(2):
                                P.c("pe", (lambda e, d=d, hl=hl: e.matmul(
                                    pa[:, d, hl, :], lhsT=KTm[:, hl, d, c * 64:(c + 1) * 64], rhs=QT_[:, d, c * 64:(c + 1) * 64],
                                    start=True, stop=True)), reads=["KTm%d" % hl], writes=["psAB" + kk_])
                                P.c("pe", (lambda e, d=d, hl=hl: e.matmul(
                                    pa[:, 2 + d, hl, :], lhsT=KAm[:, hl, d, c * 64:(c + 1) * 64], rhs=QE[:, d, c * 64:(c + 1) * 64],
                                    start=True, stop=True)), reads=["KAm%d" % hl], writes=["psAB" + kk_])
                        P.c("dve", (lambda e: e.tensor_tensor(out=ta_[:], in0=pa[:], in1=mk4[:].unsqueeze(2).to_broadcast([64, 4, 2, 64]), op=ALU.mult)),
                            reads=["psAB" + kk_, "mk4"], writes=["tA" + kk_])
                        P.c("dve", (lambda e: e.tensor_tensor(out=at_[:], in0=ta_[:, 0:2], in1=ta_[:, 2:4], op=ALU.add)),
                            reads=["tA" + kk_], writes=["AT" + kk_])

                    def O_mm(c):
                        st, sl = c % 2, (c // 2) % 2
                        at_ = AT[st][sl]
                        kk_ = "%d_%d" % (st, sl)
                        psO = psOl[st][:]
                        ok_ = "psO%d" % st
                        for d in range(2):
                            P.c("pe", (lambda e, d=d: e.matmul(
                                psO, lhsT=QE[:, d, c * 64:(c + 1) * 64], rhs=SB[:, d, c, :], start=(d == 0), stop=False)),
                                reads=[], writes=[ok_])
                        for hl in range(2):
                            for d in range(2):
                                P.c("pe", (lambda e, d=d, hl=hl: e.matmul(
                                    psOl[st][:, hl * 64:(hl + 1) * 64], lhsT=at_[:, d, hl, :], rhs=IT[:, c, hl * 64:(hl + 1) * 64],
                                    start=False, stop=(hl == 1 and d == 1))), reads=["AT" + kk_], writes=[ok_])
                        tg = "ho%d" % st
                        hn_small(P, p5, tg, psO.rearrange("p (h v) -> p h v", h=2), None, on[st], c, [ok_], [tg + "on"])
                        P.c("dve", (lambda e: e.tensor_tensor(out=yb[st][:], in0=on[st][:], in1=SG[:, c, :].rearrange("p (h v) -> p h v", h=2), op=ALU.mult)),
                            reads=[tg + "on", "SGg"], writes=[tg + "yb"])
                        P.c("pe", (lambda e: e.transpose(ptyl[st][:], yb[st][:].rearrange("p h v -> p (h v)"), k.ident[0:64, 0:64])),
                            reads=[tg + "yb", "ident"], writes=["hpty%d" % st])
                        P.c("act", (lambda e: e.copy(out=k.YT[:, 6 + hp, c * 64:(c + 1) * 64], in_=ptyl[st][:])), reads=["hpty%d" % st], writes=["YT%d" % st])

                    def ostream(st):
                        cs = list(range(st, nch_out, 2))
                        for j, c in enumerate(cs):
                            A_mm(c)
                            if j > 0:
                                O_mm(cs[j - 1])
                        O_mm(cs[-1])

                    P.record_streams([lambda: ostream(0), lambda: ostream(1)])
                    P.flush()


def psO_dbg(k, pl, ps, key, c):
    t = pl.sb([64, 128], F32, "dbgO")
    k.P.c("dve", lambda e: e.tensor_copy(out=t[:], in_=ps[:]), reads=[key], writes=["dbgO%d" % c])
    return t[:]


_HN = {}


def hn_small(P, pl, tag, src, gain_bc, out, c, rk, wk):
    cache = pl.__dict__.setdefault("_hn", {})
    if tag not in cache:
        cache[tag] = (pl.sb([64, 2, 64], F32, tag + "sq"), pl.sb([64, 2], F32, tag + "ssh"))
    sq, ssh = cache[tag]
    P.c("act", lambda e: e.activation(out=sq[:], in_=src, func=AF.Square), reads=rk, writes=[tag + "sq"])
    P.c("dve", lambda e: e.tensor_reduce(out=ssh[:], in_=sq[:], axis=AX.X, op=ALU.add), reads=[tag + "sq"], writes=[tag + "ssh"])
    rsqrt(P, ssh[:], ssh[:], 1.0 / 64, EPS, [tag + "ssh"], [tag + "ssh"])
    if gain_bc is None:
        P.c("dve", lambda e: e.tensor_tensor(out=out[:], in0=src, in1=ssh[:].unsqueeze(2).to_broadcast([64, 2, 64]), op=ALU.mult),
            reads=rk + [tag + "ssh"], writes=wk)
        return
    P.c("dve", lambda e: e.tensor_tensor(out=sq[:], in0=src, in1=ssh[:].unsqueeze(2).to_broadcast([64, 2, 64]), op=ALU.mult),
        reads=rk + [tag + "ssh"], writes=[tag + "sq"])
    P.c("dve", lambda e: e.tensor_tensor(out=out[:], in0=sq[:], in1=gain_bc, op=ALU.mult), reads=[tag + "sq", "gains"], writes=wk)


def stage_merge(k, l, xsrc, csrc):
    nc, P, I = k.nc, k.P, k.I
    last = (l == DEPTH - 1)
    ntok = T if last else TA
    with ExitStack() as es:
        pl = Pool_(nc, es)
        Wbr = pl.sb([128, 8, D], BF, "wbr")
        Wo = pl.sb([128, 8, D], BF, "wo")
        Wg = [pl.sb([128, 8, 512], BF, "wg") for _ in range(2)]
        wv = I["w_in"][l].rearrange("(kc p) n -> p kc n", p=128)
        HTa = pl.sb([128, 8, ntok], BF, "HTa")
        hv = k.HTd.rearrange("(kc p) t -> p kc t", p=128)
        hkeys = []
        for t0 in range(0, ntok, 512):
            tw = min(512, ntok - t0)
            P.dma("sp", HTa[:, :, t0:t0 + tw], hv[:, :, t0:t0 + tw], writes=["HTa%d" % t0])
            hkeys.append("HTa%d" % t0)

        def load_wg(fc):
            wg = Wg[fc % 2]
            for n in range(4):
                c = NCOL + n * D + fc * 128
                P.dma("pool", wg[:, :, n * 128:(n + 1) * 128], wv[:, :, c:c + 128], writes=["wg%d_%d" % (fc % 2, n)])
        load_wg(0)
        P.dma("pool", Wbr[:], I["w_branch"][l].rearrange("n (j p) d -> p (n j) d", p=128), writes=["Wbr"])
        load_wg(1)
        P.dma("pool", Wo[:], I["w_out"][l].rearrange("(kc p) d -> p kc d", p=128), writes=["Wo"])
        grow = pl.sb([128, 2, D], F32, "grow")
        for s_ in range(2):
            src = bass.AP(k.MODROW.tensor, k.MODROW[l, 32 + s_, :].offset, [[0, 128], [256, 8], [1, 128]])
            P.dma("sp", grow[:, s_, :].rearrange("p (j f) -> p j f", f=128), src, writes=["grow"])
        ST = pl.sb([128, 8, ntok], BF, "ST")
        acc = pl.sb([128, 512], F32, "acc")
        sg = [pl.sb([128, 512], F32, "sg") for _ in range(2)]
        tmp = [pl.sb([128, 512], F32, "mt") for _ in range(2)]
        psg = [pl.ps([128, 512]) for _ in range(2)]
        psp = [pl.ps([128, 512]) for _ in range(2)]
        psy = [pl.ps([128, 512]) for _ in range(2)]
        xt = [pl.sb([128, D], F32, "mx") for _ in range(2)]
        cnt = 0
        nx = 0
        for fc in range(8):
            wg = Wg[fc % 2]
            if fc >= 2:
                load_wg(fc)
            for t0 in range(0, ntok, 512):
                tw = min(512, ntok - t0)
                hk = "HTa%d" % t0
                for n in range(4):
                    b = cnt % 2
                    cnt += 1
                    wk = "wg%d_%d" % (fc % 2, n)
                    for kc in range(8):
                        P.c("pe", (lambda e, b=b, kc=kc, n=n, wg=wg, t0=t0, tw=tw: e.matmul(
                            psg[b][:, 0:tw], lhsT=wg[:, kc, n * 128:(n + 1) * 128], rhs=HTa[:, kc, t0:t0 + tw],
                            start=(kc == 0), stop=(kc == 7))), reads=[wk, hk], writes=["psg%d" % b])
                    for j in range(2):
                        P.c("pe", (lambda e, b=b, j=j, n=n, fc=fc, t0=t0, tw=tw: e.matmul(
                            psp[b][:, 0:tw], lhsT=Wbr[:, n * 2 + j, fc * 128:(fc + 1) * 128], rhs=k.YT[:, n * 2 + j, t0:t0 + tw],
                            start=(j == 0), stop=(j == 1))), reads=["Wbr", "YT"], writes=["psp%d" % b])
                    P.c("act", (lambda e, b=b, tw=tw: e.activation(out=sg[b][:, 0:tw], in_=psg[b][:, 0:tw], func=AF.Sigmoid)),
                        reads=["psg%d" % b], writes=["sg%d" % b])
                    if n == 0:
                        P.c("dve", (lambda e, b=b, tw=tw: e.tensor_tensor(out=acc[:, 0:tw], in0=sg[b][:, 0:tw], in1=psp[b][:, 0:tw], op=ALU.mult)),
                            reads=["sg%d" % b, "psp%d" % b], writes=["acc"])
                    else:
                        P.c("dve", (lambda e, b=b, tw=tw: e.tensor_tensor(out=tmp[b][:, 0:tw], in0=sg[b][:, 0:tw], in1=psp[b][:, 0:tw], op=ALU.mult)),
                            reads=["sg%d" % b, "psp%d" % b], writes=["mt%d" % b])
                        if n < 3:
                            P.c("dve", (lambda e, b=b, tw=tw: e.tensor_tensor(out=acc[:, 0:tw], in0=acc[:, 0:tw], in1=tmp[b][:, 0:tw], op=ALU.add)),
                                reads=["acc", "mt%d" % b], writes=["acc"])
                        else:
                            P.c("dve", (lambda e, b=b, tw=tw, fc=fc, t0=t0: e.tensor_tensor(out=ST[:, fc, t0:t0 + tw], in0=acc[:, 0:tw], in1=tmp[b][:, 0:tw], op=ALU.add)),
                                reads=["acc", "mt%d" % b], writes=["ST%d_%d" % (fc, t0)])
        for tok0 in range(0, ntok, 128):
            t0 = (tok0 // 512) * 512
            s_ = 0 if tok0 < T else 1
            X = xt[nx % 2]
            xk = "mx%d" % (nx % 2)
            nx += 1
            src = xsrc[tok0:tok0 + 128, :] if tok0 < T else csrc[tok0 - T:tok0 - T + 128, :]
            P.dma("sp", X[:], src, writes=[xk])
            for hf in range(2):
                pb = psy[hf]
                for fc in range(8):
                    P.c("pe", (lambda e, pb=pb, fc=fc, tok0=tok0, hf=hf: e.matmul(
                        pb[:], lhsT=ST[:, fc, tok0:tok0 + 128], rhs=Wo[:, fc, hf * 512:(hf + 1) * 512],
                        start=(fc == 0), stop=(fc == 7))), reads=["ST%d_%d" % (fc, t0), "Wo"], writes=["psy%d" % hf])
                P.c("dve", (lambda e, pb=pb, hf=hf, s_=s_: e.tensor_tensor(out=tmp[hf][:], in0=pb[:], in1=grow[:, s_, hf * 512:(hf + 1) * 512], op=ALU.mult)),
                    reads=["psy%d" % hf, "grow"], writes=["mt%d" % hf])
                P.c("dve", (lambda e, X=X, hf=hf: e.tensor_tensor(out=X[:, hf * 512:(hf + 1) * 512], in0=X[:, hf * 512:(hf + 1) * 512], in1=tmp[hf][:], op=ALU.add)),
                    reads=["mt%d" % hf, xk], writes=[xk])
            P.dma("sp", k.XM[l][tok0:tok0 + 128, :], X[:], reads=[xk], writes=["XM"])
        P.flush()


def ffn_weights(k, pl, l):
    W1 = pl.sb([128, 8, 4 * D], BF, "w1")
    W2 = pl.sb([128, 32, D], BF, "w2")
    w1v = k.I["w_ff1"][l].rearrange("(kc p) n -> p kc n", p=128)
    w2v = k.I["w_ff2"][l].rearrange("(kc p) n -> p kc n", p=128)
    for q_ in range(8):
        k.P.dma("pool", W1[:, :, q_ * 512:(q_ + 1) * 512], w1v[:, :, q_ * 512:(q_ + 1) * 512], writes=["W1_%d" % q_], carry=True)
        k.P.dma("pool", W2[:, q_ * 4:(q_ + 1) * 4, :], w2v[:, q_ * 4:(q_ + 1) * 4, :], writes=["W2_%d" % q_], carry=True)
    return W1, W2


def stage_ffn(k, l, Wff=None):
    nc, P, I = k.nc, k.P, k.I
    last = (l == DEPTH - 1)
    ntok = T if last else TA
    with ExitStack() as es:
        pl = Pool_(nc, es)
        W1, W2 = Wff if Wff is not None else ffn_weights(k, pl, l)
        grow = pl.sb([128, 2, D], F32, "grow2")
        for s_ in range(2):
            src = bass.AP(k.MODROW.tensor, k.MODROW[l, 80 + s_, :].offset, [[0, 128], [256, 8], [1, 128]])
            P.dma("sp", grow[:, s_, :].rearrange("p (j f) -> p j f", f=128), src, writes=["grow2"])
        psh = [pl.ps([128, 256]) for _ in range(2)]
        pso = [pl.ps([128, 512]) for _ in range(4)]
        r1 = [pl.sb([128, 256], F32, "r1") for _ in range(2)]
        hid = [pl.sb([128, 256], BF, "hid") for _ in range(2)]
        xt = [pl.sb([128, D], F32, "fx") for _ in range(2)]
        tmp = [pl.sb([128, 512], F32, "ft") for _ in range(2)]
        nx = 0
        nh = 0
        for t0 in range(0, ntok, 256):
            def w2_mm(hc, b):
                for ti in range(2):
                    for hf in range(2):
                        P.c("pe", (lambda e, b=b, ti=ti, hf=hf, hc=hc: e.matmul(
                            pso[ti * 2 + hf][:], lhsT=hid[b][:, ti * 128:(ti + 1) * 128], rhs=W2[:, hc, hf * 512:(hf + 1) * 512],
                            start=(hc == 0), stop=(hc == 31))), reads=["hid%d" % b, "W2_%d" % (hc // 4)], writes=["pso%d" % (ti * 2 + hf)])
            prev = None
            for hc in range(32):
                b = nh % 2
                nh += 1
                for kc in range(8):
                    P.c("pe", (lambda e, b=b, kc=kc, hc=hc, t0=t0: e.matmul(
                        psh[b][:], lhsT=W1[:, kc, hc * 128:(hc + 1) * 128], rhs=k.HT[:, kc, t0:t0 + 256],
                        start=(kc == 0), stop=(kc == 7))), reads=["W1_%d" % (hc // 4), "HT"], writes=["psh%d" % b])
                if prev is not None:
                    w2_mm(*prev)
                P.c("act", (lambda e, b=b: e.activation(out=r1[b][:], in_=psh[b][:], func=AF.Relu)), reads=["psh%d" % b], writes=["r1%d" % b])
                P.c("dve", (lambda e, b=b: e.tensor_tensor(out=hid[b][:], in0=r1[b][:], in1=r1[b][:], op=ALU.mult)), reads=["r1%d" % b], writes=["hid%d" % b])
                prev = (hc, b)
            w2_mm(*prev)
            for ti in range(2):
                tok0 = t0 + ti * 128
                s_ = 0 if tok0 < T else 1
                X = xt[nx % 2]
                xk = "fx%d" % (nx % 2)
                nx += 1
                P.dma("sp", X[:], k.XM[l][tok0:tok0 + 128, :], writes=[xk])
                for hf in range(2):
                    pb = pso[ti * 2 + hf]
                    P.c("dve", (lambda e, pb=pb, hf=hf, s_=s_: e.tensor_tensor(out=tmp[hf][:], in0=pb[:], in1=grow[:, s_, hf * 512:(hf + 1) * 512], op=ALU.mult)),
                        reads=["pso%d" % (ti * 2 + hf), "grow2"], writes=["ft%d" % hf])
                    P.c("dve", (lambda e, X=X, hf=hf: e.tensor_tensor(out=X[:, hf * 512:(hf + 1) * 512], in0=X[:, hf * 512:(hf + 1) * 512], in1=tmp[hf][:], op=ALU.add)),
                        reads=["ft%d" % hf, xk], writes=[xk])
                dst = k.out[tok0:tok0 + 128, :] if last else k.XL[0][tok0:tok0 + 128, :]
                P.dma("sp", dst, X[:], reads=[xk], writes=["XLout"])
        P.flush()


def _rope_tab(pos):
    r, c = pos // 64, pos % 64
    inv = (10000.0 ** (-np.arange(8, dtype=np.float32) / 8)).astype(np.float32)
    ar = r[:, None].astype(np.float32) * inv
    ac = c[:, None].astype(np.float32) * inv
    return np.concatenate([np.cos(ar), np.cos(ac), np.sin(ar), np.sin(ac)], axis=1).astype(np.float32)


def _na_index(half):
    idx = -np.ones((16, 128, 6, 128), np.int64)
    kp = np.arange(128)
    q = np.arange(128)
    for p in range(16):
        t0 = 0 if p == 0 else (14 if p == 15 else p)
        nl = 6 if p in (0, 15) else 5
        for c in range(nl):
            kl = 2 * (t0 + c) + kp // 64
            kc = kp % 64
            kr = 32 * half - 4 + kl
            r = 32 * half + 2 * p + q // 64
            cq = q % 64
            rs = np.clip(r - 4, 0, 56)
            cs = np.clip(cq - 8, 0, 48)
            okr = (kr[:, None] >= rs[None, :]) & (kr[:, None] < rs[None, :] + 8) & (kr[:, None] >= 0) & (kr[:, None] < 64)
            okc = (kc[:, None] >= cs[None, :]) & (kc[:, None] < cs[None, :] + 16)
            dr = kr[:, None] - r[None, :] + 7
            dc = kc[:, None] - cq[None, :] + 15
            ok = okr & okc
            v = np.where(ok, dr * 31 + dc, -1)
            idx[p, :, c, :] = v
    return idx


def prep(inp, ncores=8):
    f = np.float32
    g = {}
    for n_ in ("ada_w", "w_in", "w_branch", "w_out", "w_ff1", "w_ff2", "pool_w", "mla_wq_b", "mla_wkv_b"):
        g[n_] = np.ascontiguousarray(inp[n_], f)
    g["ada_bT"] = np.ascontiguousarray(inp["ada_b"].reshape(DEPTH, 48, 128).transpose(2, 0, 1), f)
    nrm = np.stack([inp["norm1"], inp["norm2"]], axis=1)
    g["normT"] = np.ascontiguousarray(nrm.reshape(DEPTH, 2, 8, 128).transpose(3, 0, 1, 2), f)
    g["pool_scT"] = np.ascontiguousarray(inp["pool_scale"].reshape(DEPTH, 2, 128).transpose(2, 0, 1), f)
    gn = np.zeros((DEPTH, 800), f)
    for l in range(DEPTH):
        gn[l, 0:256] = inp["mla_q_norm"][l]
        gn[l, 256:384] = inp["mla_kv_norm"][l]
        gn[l, 384:480] = inp["mla_gq"][l]
        gn[l, 480:576] = inp["mla_gk"][l]
        gn[l, 576:640] = inp["na_gq"][l]
        gn[l, 640:704] = inp["na_gk"][l]
        gn[l, 704:768] = inp["hg_norm"][l]
    g["gains"] = np.ascontiguousarray(np.broadcast_to(gn[None], (128, DEPTH, 800)), f)
    g["ropek"] = _rope_tab(np.arange(L))
    lb = inp["hg_lb"].reshape(DEPTH + 1, 2, 2, 2, 64)
    g["hg_lbT"] = np.ascontiguousarray(lb.transpose(3, 4, 0, 1, 2).reshape(128, DEPTH + 1, 2, 2), f)
    def invcnt(t, n):
        out = np.zeros((4, len(t)), f)
        for gi, w in enumerate((2, 4, 8, 16)):
            lo = np.clip(t - w // 2, 0, n)
            hi = np.clip(t - w // 2 + w, 0, n)
            out[gi] = 1.0 / (hi - lo)
        return out
    def to_pt(a):
        return np.ascontiguousarray(np.repeat(a.reshape(2, 2, 1, -1), 64, axis=2).reshape(2, 128, -1).transpose(1, 0, 2), f)
    g["pinvc"] = to_pt(invcnt(np.arange(TC), TC))
    rpb_ext = [np.concatenate([inp["na_rpb"][l].reshape(4, -1), np.full((4, 1), NEG, f)], axis=1) for l in range(DEPTH)]
    maps = []
    for core in range(ncores):
        b, half = core // 2, core % 2
        m = dict(g)
        m["x"] = np.ascontiguousarray(inp["x"][b, half * T:(half + 1) * T], f)
        m["ctx"] = np.ascontiguousarray(inp["ctx"][b], f)
        cc = np.stack([inp["c"][b], inp["c_ctx"]], axis=-1)
        m["cT"] = np.ascontiguousarray(cc.reshape(8, 128, 2).transpose(1, 0, 2), f)
        m["pinv"] = to_pt(invcnt(np.arange(half * T, (half + 1) * T), L))
        m["hm"] = np.ascontiguousarray(np.broadcast_to(np.array([half, 1 - half], f), (128, 2)))
        m["ropeq"] = np.ascontiguousarray(g["ropek"][half * T:(half + 1) * T])
        idx = _na_index(half)
        nab = np.stack([rpb_ext[l][:, idx] for l in range(DEPTH)])
        m["nab"] = np.ascontiguousarray(nab.transpose(0, 2, 3, 4, 1, 5), f)
        maps.append(m)
    return maps


def kernel(**inputs):
    inp = {k_: np.asarray(v) for k_, v in inputs.items()}
    k = build(ncores=8)
    maps = prep(inp, 8)
    names = list(k.I.keys())
    maps = [{n: m[n] for n in names} for m in maps]
    res = run_bass_kernel_spmd(k.nc, maps, core_ids=list(range(8)))
    out = np.zeros((4, L, D), np.float32)
    for core in range(8):
        b, half = core // 2, core % 2
        out[b, half * T:(half + 1) * T] = np.asarray(res.results[core]["out"], np.float32)
    return out
```

```python
import numpy as np
from contextlib import ExitStack
import concourse.bass as bass
import concourse.mybir as mb
from concourse.bass_utils import run_bass_kernel_spmd

F32 = mb.dt.float32
BF = mb.dt.bfloat16
AF = mb.ActivationFunctionType
ALU = mb.AluOpType
AX = mb.AxisListType

D = 1024
T = 2048
TC = 256
TA = T + TC
NT = TA // 128
L = 4096
DEPTH = 2
NCOL = 2720
EPS = 1e-6
NEG = -30000.0


class _Op:
    __slots__ = ("eng", "fn", "kind", "deps", "needs_inc", "token", "waits", "ring_wait", "carry")

    def __init__(self, eng, fn, kind):
        self.eng = eng
        self.fn = fn
        self.kind = kind
        self.deps = []
        self.needs_inc = False
        self.token = None
        self.waits = []
        self.ring_wait = None
        self.carry = False


class Prog:
    ENGS = ("pe", "act", "dve", "pool", "sp")
    NRING = 8

    def __init__(self, nc, es):
        self.nc = nc
        self.h = {"pe": nc.tensor, "act": nc.scalar, "dve": nc.vector, "pool": nc.gpsimd, "sp": nc.sync}
        self.sem = {e: es.enter_context(nc.semaphore("s_" + e)) for e in self.ENGS}
        self.cnt = {e: 0 for e in self.ENGS}
        self.ring = {q: [es.enter_context(nc.semaphore("r_%s%d" % (q, i))) for i in range(self.NRING)]
                     for q in ("sp", "pool", "act")}
        self.ringcnt = {q: [0] * self.NRING for q in self.ring}
        self.ringlast = {q: [None] * self.NRING for q in self.ring}
        self.ndma = {q: 0 for q in self.ring}
        self.ccsem = es.enter_context(nc.semaphore("s_cc"))
        self.cccnt = 0
        self.waited = {e: {} for e in self.ENGS}
        self.ops = []
        self.lastw = {}
        self.readers = {}
        self.pending_dma = []
        self.nstage = 0

    def _record(self, op, reads, writes):
        deps = []
        for k in reads:
            for w in self.lastw.get(k, ()):
                if w.eng == op.eng == "pe" and w.kind == "c" and op.kind == "c":
                    continue
                deps.append(w)
        for k in writes:
            for w in self.lastw.get(k, ()):
                if w.kind == "c" and op.kind == "c" and w.eng == op.eng == "pe":
                    continue
                deps.append(w)
            for r in self.readers.get(k, ()):
                if r.kind == "c" and op.kind == "c" and r.eng == op.eng == "pe":
                    continue
                deps.append(r)
        seen = set()
        for d in deps:
            if id(d) not in seen and d is not op:
                seen.add(id(d))
                d.needs_inc = True
                op.deps.append(d)
        for k in reads:
            self.readers.setdefault(k, []).append(op)
        for k in writes:
            self.lastw[k] = [op]
            self.readers[k] = []
        self.ops.append(op)
        return op

    def c(self, eng, fn, reads=(), writes=()):
        return self._record(_Op(eng, fn, "c"), reads, writes)

    def dma(self, q, out, in_, reads=(), writes=(), carry=False):
        op = _Op(q, lambda e: e.dma_start(out=out, in_=in_), "d")
        op.needs_inc = True
        op.carry = carry
        if not carry:
            self.pending_dma.append(op)
        return self._record(op, reads, writes)

    def cc(self, fn, reads=(), writes=()):
        op = _Op("pool", fn, "cc")
        op.needs_inc = True
        self.pending_dma.append(op)
        return self._record(op, reads, writes)

    def record_streams(self, fns):
        lists = []
        for fn in fns:
            saved, self.ops = self.ops, []
            fn()
            lists.append(self.ops)
            self.ops = saved
        idx = [0] * len(lists)
        n = max(len(x) for x in lists) if lists else 0
        for t in range(n):
            for i, lst in enumerate(lists):
                upto = ((t + 1) * len(lst) + n - 1) // n
                while idx[i] < min(upto, len(lst)):
                    self.ops.append(lst[idx[i]])
                    idx[i] += 1
        for i, lst in enumerate(lists):
            self.ops.extend(lst[idx[i]:])

    def flush(self):
        per = {e: [] for e in self.ENGS}
        for op in self.ops:
            e = op.eng
            waits = []
            for d in op.deps:
                s, v = d.token
                key = id(s)
                if self.waited[e].get(key, 0) < v:
                    self.waited[e][key] = v
                    waits.append((s, v))
            if op.kind == "d":
                r = self.ndma[e] % self.NRING
                self.ndma[e] += 1
                prev = self.ringlast[e][r]
                if prev is not None:
                    s, v = prev
                    if self.waited[e].get(id(s), 0) < v:
                        self.waited[e][id(s)] = v
                        waits.append((s, v))
                self.ringcnt[e][r] += 16
                op.token = (self.ring[e][r], self.ringcnt[e][r])
                self.ringlast[e][r] = op.token
            elif op.kind == "cc":
                self.cccnt += 1
                op.token = (self.ccsem, self.cccnt)
            elif op.needs_inc:
                self.cnt[e] += 1
                op.token = (self.sem[e], self.cnt[e])
            op.waits = waits
            per[e].append(op)
        tail = {e: [] for e in self.ENGS}
        for op in self.pending_dma:
            s, v = op.token
            e = op.eng
            if self.waited[e].get(id(s), 0) < v:
                self.waited[e][id(s)] = v
                tail[e].append((s, v))
        sem, ringsem, ccsem = self.sem, self.ring, self.ccsem

        def body(e):
            def run(h):
                for op in per[e]:
                    for s, v in op.waits:
                        h.wait_ge(s, v)
                    ins = op.fn(h)
                    if op.kind == "d":
                        ins.then_inc(op.token[0], 16)
                    elif op.kind == "cc":
                        ins.then_inc(op.token[0], 1)
                    elif op.needs_inc:
                        ins.then_inc(op.token[0], 1)
                for s, v in tail[e]:
                    h.wait_ge(s, v)
            return run

        with self.nc.Block() as blk:
            blk.tensor(body("pe"))
            blk.scalar(body("act"))
            blk.vector(body("dve"))
            blk.gpsimd(body("pool"))
            blk.sync(body("sp"))
        self.ops = []
        self.lastw = {kk: ws for kk, ws in self.lastw.items() if all(w.carry for w in ws)}
        self.readers = {}
        self.pending_dma = []
        self.nstage += 1


def rsqrt(P, out, in_, mul, add, rk, wk):
    P.c("act", lambda e: e.activation(out=out, in_=in_, func=AF.Ln, scale=mul, bias=add), reads=rk, writes=wk)
    P.c("act", lambda e: e.activation(out=out, in_=out, func=AF.Exp, scale=-0.5), reads=wk, writes=wk)


class Pool_:
    N = [0]

    def __init__(self, nc, es):
        self.nc, self.es = nc, es

    def sb(self, shape, dt=F32, name=None):
        Pool_.N[0] += 1
        return self.es.enter_context(self.nc.sbuf_tensor((name or "t") + "_%d" % Pool_.N[0], list(shape), dt))

    def ps(self, shape, dt=F32, name=None):
        Pool_.N[0] += 1
        return self.es.enter_context(self.nc.psum_tensor((name or "p") + "_%d" % Pool_.N[0], list(shape), dt))

    def get(self, key, shape, dt=F32, psum=False):
        c = self.__dict__.setdefault("_cache", {})
        if key not in c:
            c[key] = self.ps(shape, dt, key) if psum else self.sb(shape, dt, key)
        return c[key]


class K:
    pass


def _dram_in(nc, name, shape, dt=F32):
    return nc.dram_tensor(name, list(shape), dt, kind="ExternalInput").ap()


def build(dbg=(), upto=None, ncores=8, skip=()):
    nc = bass.Bass("TRN2", target_bir_lowering=False)
    es = ExitStack()
    k = K()
    k.nc, k.es = nc, es
    k.P = Prog(nc, es)
    k.dbg = set(dbg)
    k.upto = upto
    k.groups = [[2 * i, 2 * i + 1] for i in range(ncores // 2)]
    shapes = dict([
        ("x", (T, D)), ("ctx", (TC, D)), ("cT", (128, 8, 2)),
        ("ada_w", (DEPTH, D, 6 * D)), ("ada_bT", (128, DEPTH, 48)), ("normT", (128, DEPTH, 2, 8)),
        ("w_in", (DEPTH, D, 6816)), ("w_branch", (DEPTH, 4, 256, D)), ("w_out", (DEPTH, D, D)),
        ("w_ff1", (DEPTH, D, 4 * D)), ("w_ff2", (DEPTH, 4 * D, D)),
        ("pool_w", (DEPTH, 4, 64, 64)), ("pool_scT", (128, DEPTH, 2)), ("pinv", (128, 2, T)), ("pinvc", (128, 2, TC)),
        ("hm", (128, 2)),
        ("mla_wq_b", (DEPTH, 256, 384)), ("mla_wkv_b", (DEPTH, 128, 512)),
        ("gains", (128, DEPTH, 800)),
        ("ropeq", (T, 32)), ("ropek", (L, 32)),
        ("nab", (DEPTH, 16, 128, 6, 4, 128)),
        ("hg_lbT", (128, 3, 2, 2)),
    ])

    class LazyIn(dict):
        def __missing__(self, name):
            v = _dram_in(nc, name, shapes[name])
            self[name] = v
            return v
    I = LazyIn()
    k.I = I
    k.out = nc.dram_tensor("out", [T, D], F32, kind="ExternalOutput").ap()

    def scratch(name, shape, dt=F32):
        kind = "ExternalOutput" if name in k.dbg else "Internal"
        return nc.dram_tensor(name, list(shape), dt, kind=kind).ap()
    k.scratch = scratch
    pp = Pool_(nc, es)
    k.pp = pp
    k.modT = pp.sb([128, DEPTH, 48, 2], F32, "modT")
    k.AM = pp.sb([128, DEPTH, 2, 8, 2], F32, "AM")
    k.ident = pp.sb([128, 128], BF, "ident")
    k.identf = pp.sb([128, 128], F32, "identf")
    k.gains = pp.sb([128, DEPTH, 800], F32, "gains")
    k.hm = pp.sb([128, 2], F32, "hm")
    k.HTd = nc.dram_tensor("HTd", [D, TA], BF, kind="Internal").ap()
    k.U = scratch("U", (TA, 1696))
    k.UT = scratch("UT", (1024, TA))
    k.MODROW = scratch("MODROW", (DEPTH, 96, 128))
    k.XM = [scratch("XM%d" % l, (TA, D)) for l in range(DEPTH)]
    k.XL = [scratch("XL0", (TA, D)), None]
    k.YTd = scratch("YTd", (1024, TA), F32)
    stage_consts(k)
    stage_mods(k, 0)
    xsrc, csrc = I["x"], I["ctx"]
    for l in range(DEPTH):
        last = (l == DEPTH - 1)
        with ExitStack() as hs:
            hp_ = Pool_(nc, hs)
            k.HT = hp_.sb([128, 8, TA], BF, "HT")
            Wp = proj_weights(k, hp_, l)
            stage_norm(k, l, 0, xsrc, csrc, spill=True)
            stage_proj(k, l, Wp)
        if upto == "proj":
            break
        if upto == "ex1":
            break
        with ExitStack() as les:
            lp = Pool_(nc, les)
            k.YT = lp.sb([128, 8, TA], BF, "YT")
            stage_pool(k, l)
            if upto == "pool":
                stage_dump_yt(k)
                break
            if "mla" not in skip:
                stage_attn(k, l, "mla")
            if "na" not in skip:
                stage_attn(k, l, "na")
            if upto == "na":
                stage_dump_yt(k)
                break
            stage_hgrn(k, l, upto)
            if "YTd" in k.dbg:
                stage_dump_yt(k)
            if upto in ("hgrn", "hgrn1", "hgrn2"):
                break
            stage_merge(k, l, xsrc, csrc)
        if upto == "merge":
            break
        with ExitStack() as hs:
            hp_ = Pool_(nc, hs)
            k.HT = hp_.sb([128, 8, TA], BF, "HT")
            Wff = ffn_weights(k, hp_, l)
            stage_norm(k, l, 1, k.XM[l][0:T, :], k.XM[l][T:TA, :], nstream=2, ntile=(16 if last else NT))
            stage_ffn(k, l, Wff)
        xsrc, csrc = k.XL[0][0:T, :], k.XL[0][T:TA, :]
        if upto == "l0":
            break
    return k


def dump(k, name, ap, shape, reads=()):
    if name not in k.dbg:
        return
    t = k.nc.dram_tensor(name, list(shape), F32, kind="ExternalOutput").ap()
    k.P.dma("pool", t, ap, reads=list(reads))


def stage_dump_yt(k):
    P = k.P
    for j in range(8):
        P.dma("pool", k.YTd[j * 128:(j + 1) * 128, :], k.YT[:, j, :])
    P.flush()


def stage_consts(k):
    nc, P = k.nc, k.P
    P.c("pool", lambda e: e.memset(k.identf[:], 0.0), writes=["identf"])
    P.c("pool", lambda e: e.affine_select(out=k.identf[:], in_=k.identf[:], compare_op=ALU.not_equal, fill=1.0,
                                         base=0, pattern=[[-1, 128]], channel_multiplier=1),
        reads=["identf"], writes=["identf"])
    P.c("dve", lambda e: e.tensor_copy(out=k.ident[:], in_=k.identf[:]), reads=["identf"], writes=["ident"])
    P.dma("sp", k.gains[:], k.I["gains"], writes=["gains"])
    P.dma("sp", k.hm[:], k.I["hm"], writes=["hm"])
    P.flush()


def mods_ops(k, pl, l):
    nc, P, I = k.nc, k.P, k.I
    m = "m%d" % l
    cT = pl.sb([128, 8, 2])
    e1 = pl.sb([128, 8, 2])
    sT = pl.sb([128, 8, 2], BF)
    abT = pl.sb([128, DEPTH, 48])
    nT = pl.sb([128, DEPTH, 2, 8])
    wbuf = [pl.sb([128, 8, 512], BF, "adaw") for _ in range(2)]
    ps = pl.ps([128, 48, 2])
    psr = pl.ps([128, 128])
    rowsb = pl.sb([96, 128])
    P.dma("sp", cT[:], I["cT"], writes=[m + "cT"])
    P.dma("sp", abT[:], I["ada_bT"], writes=[m + "abT"])
    P.dma("sp", nT[:], I["normT"], writes=[m + "nT"])
    P.c("act", lambda e: e.activation(out=e1[:], in_=cT[:], func=AF.Exp, scale=-1.0), reads=[m + "cT"], writes=[m + "e1"])
    P.c("dve", lambda e: e.tensor_scalar_add(out=e1[:], in0=e1[:], scalar1=1.0), reads=[m + "e1"], writes=[m + "e1"])
    P.c("dve", lambda e: e.reciprocal(out=e1[:], in_=e1[:]), reads=[m + "e1"], writes=[m + "e1"])
    P.c("dve", lambda e: e.tensor_tensor(out=sT[:], in0=e1[:], in1=cT[:], op=ALU.mult), reads=[m + "e1", m + "cT"], writes=[m + "sT"])
    wv = I["ada_w"][l].rearrange("(kc p) n -> p kc n", p=128)
    for nb in range(12):
        wb = wbuf[nb % 2]
        key = m + "adaw%d" % (nb % 2)
        P.dma("pool", wb[:], wv[:, :, nb * 512:(nb + 1) * 512], writes=[key])
        for jj in range(4):
            j = nb * 4 + jj
            for kc in range(8):
                P.c("pe", (lambda e, wb=wb, jj=jj, kc=kc, j=j: e.matmul(
                    ps[:, j, :], lhsT=wb[:, kc, jj * 128:(jj + 1) * 128], rhs=sT[:, kc, :],
                    start=(kc == 0), stop=(kc == 7))), reads=[key, m + "sT"], writes=[m + "psmod"])
    P.c("dve", (lambda e: e.tensor_tensor(
        out=k.modT[:, l], in0=ps[:], in1=abT[:, l].unsqueeze(2).to_broadcast([128, 48, 2]), op=ALU.add)),
        reads=[m + "psmod", m + "abT"], writes=[m + "modT"])
    P.c("pe", (lambda e: e.transpose(psr[0:96, :], k.modT[:, l].rearrange("p j s -> p (j s)"), k.identf[:])),
        reads=[m + "modT", "identf"], writes=[m + "psr"])
    P.c("dve", (lambda e: e.tensor_copy(out=rowsb[:], in_=psr[0:96, :])), reads=[m + "psr"], writes=[m + "rowsb"])
    P.dma("sp", k.MODROW[l], rowsb[:], reads=[m + "rowsb"], writes=[m + "MODROW"])
    for w in range(2):
        sc = k.modT[:, l, (8 + 24 * w):(16 + 24 * w), :]
        P.c("dve", (lambda e, w=w, sc=sc: e.tensor_scalar_add(out=k.AM[:, l, w], in0=sc, scalar1=1.0)),
            reads=[m + "modT"], writes=[m + "AM%d" % w])
        P.c("dve", (lambda e, w=w: e.tensor_tensor(
            out=k.AM[:, l, w], in0=k.AM[:, l, w], in1=nT[:, l, w].unsqueeze(2).to_broadcast([128, 8, 2]),
            op=ALU.mult)), reads=[m + "AM%d" % w, m + "nT"], writes=[m + "AM%d" % w])


def stage_mods(k, l):
    with ExitStack() as es:
        mods_ops(k, Pool_(k.nc, es), l)
        k.P.flush()


def stage_norm(k, l, w, xsrc, csrc, spill=False, nstream=3, ntile=NT):
    nc, P = k.nc, k.P
    with ExitStack() as es:
        pl = Pool_(nc, es)
        ss = pl.sb([128, NT])
        rstd = pl.sb([128, NT])
        bufs = [(pl.sb([128, D]), pl.sb([128, D], BF), pl.sb([128, D], BF), pl.ps([128, 8, 128], BF), pl.sb([128, 8, 128])) for _ in range(nstream)]

        def tile(i):
            s_ = 0 if i < 16 else 1
            src = xsrc[i * 128:(i + 1) * 128, :] if i < 16 else csrc[(i - 16) * 128:(i - 15) * 128, :]
            b = i % nstream
            X, XN, junk, PT, TM = bufs[b]
            P.dma("sp", X[:], src, writes=["xt%d" % b])
            P.c("act", (lambda e: e.activation(out=junk[:], in_=X[:], func=AF.Square, accum_out=ss[:, i:i + 1])),
                reads=["xt%d" % b], writes=["junk%d" % b, "ss%d" % i])
            rsqrt(P, rstd[:, i:i + 1], ss[:, i:i + 1], 1.0 / D, EPS, ["ss%d" % i], ["rs%d" % i])
            P.c("act", (lambda e: e.activation(out=XN[:], in_=X[:], func=AF.Copy, scale=rstd[:, i:i + 1])),
                reads=["xt%d" % b, "rs%d" % i], writes=["xn%d" % b])
            for kc in range(8):
                P.c("pe", (lambda e, kc=kc: e.transpose(PT[:, kc, :], XN[:, kc * 128:(kc + 1) * 128], k.ident[:])),
                    reads=["xn%d" % b, "ident"], writes=["pt%d" % b])
            A = k.AM[:, l, w, :, s_:s_ + 1].to_broadcast([128, 8, 128])
            Bv = k.modT[:, l, 24 * w:24 * w + 8, s_:s_ + 1].to_broadcast([128, 8, 128])
            P.c("dve", (lambda e: e.tensor_tensor(out=TM[:], in0=PT[:], in1=A, op=ALU.mult)),
                reads=["pt%d" % b], writes=["tm%d" % b])
            P.c("dve", (lambda e: e.tensor_tensor(out=k.HT[:, :, i * 128:(i + 1) * 128], in0=TM[:], in1=Bv, op=ALU.add)),
                reads=["tm%d" % b], writes=["HT%d" % i])

        P.record_streams([(lambda st=st: [tile(i) for i in range(st, ntile, nstream)]) for st in range(nstream)])
        if spill:
            hk = ["HT%d" % i for i in range(ntile)]
            for kc in range(8):
                P.dma("sp", k.HTd[kc * 128:(kc + 1) * 128, :], k.HT[:, kc, :], reads=hk, writes=["HTd"])
        P.flush()


def proj_weights(k, pl, l):
    W = pl.sb([128, 8, NCOL], BF, "win")
    wv = k.I["w_in"][l].rearrange("(kc p) n -> p kc n", p=128)
    for c0 in range(0, NCOL, 680):
        k.P.dma("pool", W[:, :, c0:c0 + 680], wv[:, :, c0:c0 + 680], writes=["W%d" % c0], carry=True)
    return W


def stage_proj(k, l, W=None):
    nc, P, I = k.nc, k.P, k.I
    tm_cols = [(0, 672), (1184, 1440), (1696, 2208), (2464, 2720)]
    fm_cols = [(1440, 1696), (672, 1184), (2208, 2464)]
    with ExitStack() as es:
        pl = Pool_(nc, es)
        if W is None:
            W = proj_weights(k, pl, l)
        wkeys = ["W%d" % c0 for c0 in range(0, NCOL, 680)]
        ps = [pl.ps([128, 512]) for _ in range(4)]
        st = [pl.sb([128, 1696]) for _ in range(2)]
        n = 0
        for i in range(NT):
            S = st[i % 2]
            uc = 0
            for (a, b) in tm_cols:
                c = a
                while c < b:
                    w_ = min(512, b - c)
                    pb = ps[n % 4]
                    pk = "ps%d" % (n % 4)
                    n += 1
                    for kc in range(8):
                        P.c("pe", (lambda e, pb=pb, kc=kc, c=c, w_=w_, i=i: e.matmul(
                            pb[:, 0:w_], lhsT=k.HT[:, kc, i * 128:(i + 1) * 128], rhs=W[:, kc, c:c + w_],
                            start=(kc == 0), stop=(kc == 7))), reads=["HT"] + wkeys, writes=[pk])
                    eng = "act" if n % 2 else "dve"
                    if eng == "act":
                        P.c("act", (lambda e, pb=pb, S=S, uc=uc, w_=w_: e.copy(out=S[:, uc:uc + w_], in_=pb[:, 0:w_])),
                            reads=[pk], writes=["st%d_%d" % (i % 2, uc)])
                    else:
                        P.c("dve", (lambda e, pb=pb, S=S, uc=uc, w_=w_: e.tensor_copy(out=S[:, uc:uc + w_], in_=pb[:, 0:w_])),
                            reads=[pk], writes=["st%d_%d" % (i % 2, uc)])
                    uc += w_
                    c += w_
            skeys = [kk for kk in list(P.lastw.keys()) if isinstance(kk, str) and kk.startswith("st%d_" % (i % 2))]
            P.dma("sp", k.U[i * 128:(i + 1) * 128, :], S[:], reads=skeys, writes=["U%d" % i])
        st2 = [pl.sb([128, 512]) for _ in range(2)]
        m = 0
        exchange_alloc(k)
        ukeys = ["U%d" % i for i in range(NT)]
        P.dma("sp", k.EX1in, k.U[0:T, 0:160], reads=ukeys, writes=["EX1in"])
        P.dma("sp", k.EXNin[0:256, :], k.U[0:256, 160:672], reads=ukeys, writes=["EXNin"])
        P.dma("sp", k.EXNin[256:512, :], k.U[T - 256:T, 160:672], reads=ukeys, writes=["EXNin"])
        P.cc(lambda e: e.collective_compute("AllGather", ALU.bypass, replica_groups=k.groups, ins=[k.EXNin], outs=[k.GN]),
             reads=["EXNin"], writes=["GN"])
        P.cc(lambda e: e.collective_compute("AllGather", ALU.bypass, replica_groups=k.groups, ins=[k.EX1in], outs=[k.G1]),
             reads=["EX1in"], writes=["G1"])
        r0 = 0
        for (a, b) in fm_cols:
            for c in range(a, b, 128):
                for t0 in range(0, TA, 512):
                    tw = min(512, TA - t0)
                    pb = ps[n % 4]
                    pk = "ps%d" % (n % 4)
                    n += 1
                    S2 = st2[m % 2]
                    sk = "st2_%d" % (m % 2)
                    m += 1
                    for kc in range(8):
                        P.c("pe", (lambda e, pb=pb, kc=kc, c=c, t0=t0, tw=tw: e.matmul(
                            pb[:, 0:tw], lhsT=W[:, kc, c:c + 128], rhs=k.HT[:, kc, t0:t0 + tw],
                            start=(kc == 0), stop=(kc == 7))), reads=["HT"] + wkeys, writes=[pk])
                    if m % 2:
                        P.c("act", (lambda e, pb=pb, S2=S2, tw=tw: e.copy(out=S2[:, 0:tw], in_=pb[:, 0:tw])), reads=[pk], writes=[sk])
                    else:
                        P.c("dve", (lambda e, pb=pb, S2=S2, tw=tw: e.tensor_copy(out=S2[:, 0:tw], in_=pb[:, 0:tw])), reads=[pk], writes=[sk])
                    P.dma("sp", k.UT[r0:r0 + 128, t0:t0 + tw], S2[:, 0:tw], reads=[sk], writes=["UT%d_%d" % (r0, t0)])
                r0 += 128
                if r0 == 256:
                    pk_ = ["UT%d_%d" % (rr, tt) for rr in (0, 128) for tt in range(0, TA, 512)]
                    P.dma("sp", k.EXPin[:, 0:8], k.UT[0:256, 0:8], reads=pk_, writes=["EXPin"])
                    P.dma("sp", k.EXPin[:, 8:16], k.UT[0:256, T - 8:T], reads=pk_, writes=["EXPin"])
                    P.cc(lambda e: e.collective_compute("AllGather", ALU.bypass, replica_groups=k.groups, ins=[k.EXPin], outs=[k.GP]),
                         reads=["EXPin"], writes=["GP"])
        P.flush()


def exchange_alloc(k):
    nc = k.nc
    if not hasattr(k, "EX1in"):
        k.EX1in = nc.dram_tensor("EX1in", [T, 160], F32, kind="Internal").ap()
        k.G1 = nc.dram_tensor("G1", [2 * T, 160], F32, kind="Internal").ap()
        k.EXNin = nc.dram_tensor("EXNin", [512, 512], F32, kind="Internal").ap()
        k.GN = nc.dram_tensor("GN", [1024, 512], F32, kind="Internal").ap()
        k.EXPin = nc.dram_tensor("EXPin", [256, 16], F32, kind="Internal").ap()
        k.GP = nc.dram_tensor("GP", [512, 16], F32, kind="Internal").ap()


def stage_exchange1(k, l):
    nc, P = k.nc, k.P
    if not hasattr(k, "EX1in"):
        k.EX1in = nc.dram_tensor("EX1in", [T, 160], F32, kind="Internal").ap()
        k.G1 = nc.dram_tensor("G1", [2 * T, 160], F32, kind="Internal").ap()
        k.EXNin = nc.dram_tensor("EXNin", [512, 512], F32, kind="Internal").ap()
        k.GN = nc.dram_tensor("GN", [1024, 512], F32, kind="Internal").ap()
        k.EXPin = nc.dram_tensor("EXPin", [256, 16], F32, kind="Internal").ap()
        k.GP = nc.dram_tensor("GP", [512, 16], F32, kind="Internal").ap()
    P.dma("sp", k.EX1in, k.U[0:T, 0:160], writes=["EX1in"])
    P.dma("sp", k.EXNin[0:256, :], k.U[0:256, 160:672], writes=["EXNin"])
    P.dma("sp", k.EXNin[256:512, :], k.U[T - 256:T, 160:672], writes=["EXNin"])
    P.cc(lambda e: e.collective_compute("AllGather", ALU.bypass, replica_groups=k.groups, ins=[k.EXNin], outs=[k.GN]),
         reads=["EXNin"], writes=["GN"])
    P.dma("sp", k.EXPin[:, 0:8], k.UT[0:256, 0:8], writes=["EXPin"])
    P.dma("sp", k.EXPin[:, 8:16], k.UT[0:256, T - 8:T], writes=["EXPin"])
    P.cc(lambda e: e.collective_compute("AllGather", ALU.bypass, replica_groups=k.groups, ins=[k.EX1in], outs=[k.G1]),
         reads=["EX1in"], writes=["G1"])
    P.cc(lambda e: e.collective_compute("AllGather", ALU.bypass, replica_groups=k.groups, ins=[k.EXPin], outs=[k.GP]),
         reads=["EXPin"], writes=["GP"])
    P.flush()


def stage_pool(k, l):
    nc, P, I = k.nc, k.P, k.I
    with ExitStack() as es:
        pl = Pool_(nc, es)
        Wb = pl.sb([128, 2, 128])
        Wbb = pl.sb([128, 2, 128], BF)
        psc = pl.sb([128, DEPTH, 2])
        P.c("pool", lambda e: e.memset(Wb[:], 0.0), writes=["Wb"])
        for g in range(4):
            t_, o_ = g // 2, (g % 2) * 64
            P.dma("sp", Wb[o_:o_ + 64, t_, o_:o_ + 64], I["pool_w"][l, g], reads=[], writes=["Wb"])
        P.c("dve", lambda e: e.tensor_copy(out=Wbb[:], in_=Wb[:]), reads=["Wb"], writes=["Wbb"])
        P.dma("sp", psc[:], I["pool_scT"], writes=["psc"])
        ps = [pl.ps([128, 512]) for _ in range(2)]
        npsm = [0]
        for (n, c0, invsrc, halo) in ((T, 0, I["pinv"], True), (TC, T, I["pinvc"], False)):
            if n == TC and l == DEPTH - 1:
                continue
            N = n + 16
            sfx = "L" if halo else "C"
            B = pl.sb([128, 2, N], F32, "pB")
            A1 = pl.sb([128, 2, N], F32, "pA1")
            A2 = pl.sb([128, 2, N], F32, "pA2")
            R = pl.sb([128, 2, n], F32, "pR")
            Rb = pl.sb([128, 2, n], BF, "pRb")
            inv = pl.sb([128, 2, n], F32, "pinv")
            kB = "pB" + sfx
            P.c("pool", (lambda e, B=B: e.memset(B[:], 0.0)), writes=[kB])
            for t_ in range(2):
                P.dma("sp", B[:, t_, 8:8 + n], k.UT[t_ * 128:(t_ + 1) * 128, c0:c0 + n], writes=[kB])
                if halo:
                    P.dma("sp", B[:, t_, 0:8], k.GP[t_ * 128:(t_ + 1) * 128, 8:16], writes=[kB])
                    P.dma("sp", B[:, t_, 8 + n:16 + n], k.GP[256 + t_ * 128:256 + (t_ + 1) * 128, 0:8], writes=[kB])
            P.dma("sp", inv[:], invsrc, writes=["inv" + sfx])
            if halo:
                P.c("dve", (lambda e, B=B: e.tensor_scalar_mul(out=B[:, :, 0:8], in0=B[:, :, 0:8], scalar1=k.hm[:, 0:1])),
                    reads=[kB, "hm"], writes=[kB])
                P.c("dve", (lambda e, B=B, n=n: e.tensor_scalar_mul(out=B[:, :, 8 + n:16 + n], in0=B[:, :, 8 + n:16 + n], scalar1=k.hm[:, 1:2])),
                    reads=[kB], writes=[kB])
            kk = [kB, "A1" + sfx, "A2" + sfx, "R" + sfx]
            P.c("dve", (lambda e, B=B, A1=A1, N=N: e.tensor_tensor(out=A1[:, :, 1:N], in0=B[:, :, 0:N - 1], in1=B[:, :, 1:N], op=ALU.add)),
                reads=[kB], writes=[kk[1]])
            P.c("dve", (lambda e, A1=A1, A2=A2, N=N: e.tensor_tensor(out=A2[:, :, 2:N - 1], in0=A1[:, :, 1:N - 2], in1=A1[:, :, 3:N], op=ALU.add)),
                reads=[kk[1]], writes=[kk[2]])
            P.c("dve", (lambda e, A1=A1, R=R, n=n: e.tensor_copy(out=R[0:64, 0, :], in_=A1[0:64, 0, 8:8 + n])), reads=[kk[1]], writes=[kk[3] + "a"])
            P.c("dve", (lambda e, A2=A2, R=R, n=n: e.tensor_copy(out=R[64:128, 0, :], in_=A2[64:128, 0, 8:8 + n])), reads=[kk[2]], writes=[kk[3] + "b"])
            P.c("dve", (lambda e, A1=A1, A2=A2, N=N: e.tensor_tensor(out=A1[:, :, 4:N - 3], in0=A2[:, :, 2:N - 5], in1=A2[:, :, 6:N - 1], op=ALU.add)),
                reads=[kk[2], kk[1]], writes=[kk[1]])
            P.c("dve", (lambda e, A1=A1, R=R, n=n: e.tensor_copy(out=R[0:64, 1, :], in_=A1[0:64, 1, 8:8 + n])), reads=[kk[1]], writes=[kk[3] + "c"])
            P.c("dve", (lambda e, A1=A1, A2=A2, N=N: e.tensor_tensor(out=A2[:, :, 8:N - 7], in0=A1[:, :, 4:N - 11], in1=A1[:, :, 12:N - 3], op=ALU.add)),
                reads=[kk[1], kk[2]], writes=[kk[2]])
            P.c("dve", (lambda e, A2=A2, R=R, n=n: e.tensor_copy(out=R[64:128, 1, :], in_=A2[64:128, 1, 8:8 + n])), reads=[kk[2]], writes=[kk[3] + "d"])
            rk = [kk[3] + x for x in "abcd"]
            P.c("dve", (lambda e, R=R, inv=inv: e.tensor_tensor(out=R[:], in0=R[:], in1=inv[:], op=ALU.mult)), reads=rk + ["inv" + sfx], writes=[kk[3]])
            P.c("dve", (lambda e, R=R, Rb=Rb, B=B, n=n: e.tensor_tensor(out=Rb[:], in0=R[:], in1=B[:, :, 8:8 + n], op=ALU.subtract)),
                reads=[kk[3], kB], writes=["Rb" + sfx])
            for t_ in range(2):
                for t0 in range(0, n, 512):
                    tw = min(512, n - t0)
                    pb = ps[npsm[0] % 2]
                    pk = "pps%d" % (npsm[0] % 2)
                    npsm[0] += 1
                    P.c("pe", (lambda e, pb=pb, t_=t_, t0=t0, tw=tw, Rb=Rb: e.matmul(pb[:, 0:tw], lhsT=Wbb[:, t_, :], rhs=Rb[:, t_, t0:t0 + tw], start=True, stop=True)),
                        reads=["Wbb", "Rb" + sfx], writes=[pk])
                    P.c("act", (lambda e, pb=pb, t_=t_, t0=t0, tw=tw, c0=c0: e.activation(out=k.YT[:, t_, c0 + t0:c0 + t0 + tw], in_=pb[:, 0:tw], func=AF.Copy,
                                                                                     scale=psc[:, l, t_:t_ + 1])),
                        reads=[pk, "psc"], writes=["YT"])
        P.flush()


def head_norm(P, pl, tag, src, H, dh, gain_bc, out, rk, wk, extra_ss=None):
    sq = pl.get(tag + "sq", [128, H, dh], F32)
    ssh = pl.get(tag + "ssh", [128, H], F32)
    P.c("act", lambda e: e.activation(out=sq[:], in_=src, func=AF.Square), reads=rk, writes=[tag + "sq"])
    P.c("dve", lambda e: e.tensor_reduce(out=ssh[:], in_=sq[:], axis=AX.X, op=ALU.add), reads=[tag + "sq"], writes=[tag + "ssh"])
    if extra_ss is not None:
        ap, key, tot = extra_ss
        P.c("dve", lambda e: e.tensor_scalar_add(out=ssh[:], in0=ssh[:], scalar1=ap), reads=[tag + "ssh", key], writes=[tag + "ssh"])
    else:
        tot = dh
    rsqrt(P, ssh[:], ssh[:], 1.0 / tot, EPS, [tag + "ssh"], [tag + "ssh"])
    P.c("dve", lambda e: e.tensor_tensor(out=sq[:], in0=src, in1=ssh[:].unsqueeze(2).to_broadcast([128, H, dh]), op=ALU.mult),
        reads=rk + [tag + "ssh"], writes=[tag + "sq"])
    P.c("dve", lambda e: e.tensor_tensor(out=out, in0=sq[:], in1=gain_bc, op=ALU.mult), reads=[tag + "sq", "gains"], writes=wk)
    return ssh


def rope_apply(P, pl, tag, x, tab, out, rk, wk):
    xv = x.rearrange("p h (a b j) -> p h a b j", a=2, b=2)
    ov = out.rearrange("p h (a b j) -> p h a b j", a=2, b=2)
    x1, x2 = xv[:, :, :, 0, :], xv[:, :, :, 1, :]
    cos = tab[:, 0:16].rearrange("p (a j) -> p a j", a=2).unsqueeze(1).to_broadcast([128, 4, 2, 8])
    sin = tab[:, 16:32].rearrange("p (a j) -> p a j", a=2).unsqueeze(1).to_broadcast([128, 4, 2, 8])
    t = [pl.get(tag + "rt%d" % j, [128, 4, 2, 8], F32) for j in range(4)]
    P.c("dve", lambda e: e.tensor_tensor(out=t[0][:], in0=x1, in1=cos, op=ALU.mult), reads=rk, writes=[tag + "t0"])
    P.c("dve", lambda e: e.tensor_tensor(out=t[1][:], in0=x2, in1=sin, op=ALU.mult), reads=rk, writes=[tag + "t1"])
    P.c("dve", lambda e: e.tensor_tensor(out=ov[:, :, :, 0, :], in0=t[0][:], in1=t[1][:], op=ALU.subtract), reads=[tag + "t0", tag + "t1"], writes=wk)
    P.c("dve", lambda e: e.tensor_tensor(out=t[2][:], in0=x2, in1=cos, op=ALU.mult), reads=rk, writes=[tag + "t2"])
    P.c("dve", lambda e: e.tensor_tensor(out=t[3][:], in0=x1, in1=sin, op=ALU.mult), reads=rk, writes=[tag + "t3"])
    P.c("dve", lambda e: e.tensor_tensor(out=ov[:, :, :, 1, :], in0=t[2][:], in1=t[3][:], op=ALU.add), reads=[tag + "t2", tag + "t3"], writes=wk)


def heads_to_fm(P, pt, fin, dh, dst, rk, ptk, wk, ident, eng="act"):
    for h in range(4):
        P.c("pe", (lambda e, h=h: e.transpose(pt[0:dh, h, :], fin[:, h, :], ident[:])), reads=rk + ["ident"], writes=[ptk])
    if eng == "act":
        P.c("act", lambda e: e.copy(out=dst, in_=pt[0:dh, :, :]), reads=[ptk], writes=wk)
    else:
        P.c("dve", lambda e: e.tensor_copy(out=dst, in_=pt[0:dh, :, :]), reads=[ptk], writes=wk)


class AttnRes:
    pass


def attn_alloc(nc, pl):
    r = AttnRes()
    r.psS = [pl.ps([128, 4, 128], F32, "psS") for _ in range(2)]
    r.PT = [pl.sb([128, 4, 128], BF, "PT") for _ in range(3)]
    r.tb = [pl.sb([128, 4, 128], F32, "tb") for _ in range(2)]
    r.acc = [pl.ps([128, 4, 128], F32, "acc") for _ in range(2)]
    r.pty = pl.ps([128, 2, 128], BF, "pty")
    r.rec = pl.sb([128, 4], F32, "rec")
    r.yb = pl.sb([128, 4, 64], BF, "yb")
    r.ng = 0
    r.na = 0
    return r


def attn_tile(k, r, qT, dq, chunks, scale, ytj, col0, bias=None, nloc=0, bkey="bias"):
    P = k.P
    a = r.na % 2
    r.na += 1
    acc = r.acc[a]
    ak = "acc%d" % a
    groups = []
    for h in range(4):
        ch = chunks(h)
        n = len(ch)
        for g0 in range(0, n, 4):
            groups.append((h, ch, n, g0, min(n, g0 + 4)))

    def qk(g):
        h, ch, n, g0, g1 = g
        q_ap, qkeys = qT(h)
        b = r.ng % 2
        pb3 = r.ng % 3
        r.ng += 1
        psS, PT, tb = r.psS[b], r.PT[pb3], r.tb[b]
        sk, pk, tk = "psS%d" % b, "PT%d" % pb3, "tb%d" % b
        for ci in range(g0, g1):
            kT_ap, vp_ap, ckeys = ch[ci]
            P.c("pe", (lambda e, psS=psS, gi=ci - g0, kT_ap=kT_ap, q_ap=q_ap: e.matmul(psS[:, gi, :], lhsT=kT_ap, rhs=q_ap, start=True, stop=True)),
                reads=qkeys + ckeys, writes=[sk])
        nb = max(0, min(g1, nloc) - g0)
        if nb > 0:
            bv = bias(h)[:, g0:g0 + nb, :]
            P.c("dve", (lambda e, psS=psS, tb=tb, nb=nb, bv=bv: e.scalar_tensor_tensor(
                out=tb[:, 0:nb, :], in0=psS[:, 0:nb, :], scalar=scale, in1=bv, op0=ALU.mult, op1=ALU.add)),
                reads=[sk, bkey], writes=[tk])
            P.c("act", (lambda e, PT=PT, tb=tb, nb=nb: e.activation(out=PT[:, 0:nb, :], in_=tb[:, 0:nb, :], func=AF.Exp)),
                reads=[tk], writes=[pk])
        if g1 - g0 > nb:
            P.c("act", (lambda e, PT=PT, psS=psS, nb=nb, m=g1 - g0: e.activation(out=PT[:, nb:m, :], in_=psS[:, nb:m, :], func=AF.Exp, scale=scale)),
                reads=[sk], writes=[pk])
        return (PT, pk)

    def pv(g, st):
        h, ch, n, g0, g1 = g
        PT, pk = st
        for ci in range(g0, g1):
            kT_ap, vp_ap, ckeys = ch[ci]
            P.c("pe", (lambda e, PT=PT, gi=ci - g0, vp_ap=vp_ap, h=h, ci=ci, n=n: e.matmul(
                acc[:, h, 0:65], lhsT=PT[:, gi, :], rhs=vp_ap, start=(ci == 0), stop=(ci == n - 1))),
                reads=[pk] + ckeys, writes=[ak])

    prev = None
    for g in groups:
        st = qk(g)
        if prev is not None:
            pv(*prev)
        prev = (g, st)
    pv(*prev)
    P.c("dve", lambda e: e.reciprocal(out=r.rec[:], in_=acc[:, :, 64]), reads=[ak], writes=["rec"])
    P.c("dve", lambda e: e.tensor_tensor(out=r.yb[:], in0=acc[:, :, 0:64], in1=r.rec[:].unsqueeze(2).to_broadcast([128, 4, 64]), op=ALU.mult),
        reads=[ak, "rec"], writes=["yb"])
    ybf = r.yb[:].rearrange("p h d -> p (h d)")
    for j in range(2):
        P.c("pe", (lambda e, j=j: e.transpose(r.pty[:, j, :], ybf[:, j * 128:(j + 1) * 128], k.ident[:])), reads=["yb", "ident"], writes=["pty"])
    P.c("act", lambda e: e.copy(out=k.YT[:, ytj:ytj + 2, col0:col0 + 128], in_=r.pty[:]), reads=["pty"], writes=["YT"])


def stage_mla(k, l):
    nc, P, I = k.nc, k.P, k.I
    last = (l == DEPTH - 1)
    G = k.gains
    NS = 3
    with ExitStack() as es:
        pl = Pool_(nc, es)
        KT = pl.sb([96, 4, 34 * 128], BF, "KT")
        VP = pl.sb([128, 34, 4, 65], BF, "VP")
        QT = pl.sb([96, 4, TA], BF, "QT")
        with ExitStack() as es2:
            p2 = Pool_(nc, es2)
            Wkv = p2.sb([128, 512], BF, "wkv")
            Wq = p2.sb([128, 2, 384], BF, "wq")
            P.dma("pool", Wkv[:], I["mla_wkv_b"][l], writes=["Wkv"])
            P.dma("pool", Wq[:], I["mla_wq_b"][l].rearrange("(j p) n -> p j n", p=128), writes=["Wq"])
            P.c("pool", lambda e: e.memset(VP[:, :, :, 64:65], 1.0), writes=["VP1"])
            ss = p2.sb([128, 64], F32, "ss")
            junk = [p2.sb([128, 256], F32, "junk") for _ in range(2)]
            for i in range(34):
                st = i % NS
                tg = "k%d" % st
                src = k.G1[i * 128:(i + 1) * 128, 0:160] if i < 32 else k.U[T + (i - 32) * 128:T + (i - 31) * 128, 0:160]
                kva = p2.get(tg + "kva", [128, 160])
                ckvn = p2.get(tg + "ckvn", [128, 128], BF)
                cT_ = p2.get(tg + "cT", [128, 128], BF)
                sr = p2.get(tg + "sr", [128, 1])
                kfin = p2.get(tg + "kfin", [128, 4, 96], BF)
                t32 = p2.get(tg + "t32", [128, 32])
                kr = p2.get(tg + "kr", [128, 4, 32])
                tab = p2.get(tg + "tab", [128, 32])
                pT = p2.get("pT%d" % (i % 2), [128, 2, 128], BF, psum=True)
                pkv = p2.get("pkv%d" % (i % 2), [128, 512], F32, psum=True)
                pt4 = p2.get("pt4%d" % (i % 2), [128, 4, 128], BF, psum=True)
                pTk, pkvk, pt4k = "pT%d" % (i % 2), "pkv%d" % (i % 2), "pt4%d" % (i % 2)
                jk = junk[i % 2]
                jkk = "junk%d" % (i % 2)
                P.dma("sp", kva[:], src, writes=[tg + "kva"])
                if i < 32:
                    P.dma("sp", tab[:], I["ropek"][i * 128:(i + 1) * 128, :], writes=[tg + "tab"])
                P.c("act", (lambda e, kva=kva, i=i, jk=jk: e.activation(out=jk[:, 0:128], in_=kva[:, 0:128], func=AF.Square, accum_out=ss[:, i:i + 1])),
                    reads=[tg + "kva"], writes=[jkk, "ss%d" % i])
                P.c("act", (lambda e, kva=kva, sr=sr, jk=jk: e.activation(out=jk[:, 128:160], in_=kva[:, 128:160], func=AF.Square, accum_out=sr[:])),
                    reads=[tg + "kva"], writes=[jkk + "b", tg + "sr"])
                rsqrt(P, ss[:, i:i + 1], ss[:, i:i + 1], 1.0 / 128, EPS, ["ss%d" % i], ["ss%d" % i])
                P.c("dve", (lambda e, kva=kva, ckvn=ckvn, i=i: e.scalar_tensor_tensor(out=ckvn[:], in0=kva[:, 0:128], scalar=ss[:, i:i + 1], in1=G[:, l, 256:384],
                                                                                   op0=ALU.mult, op1=ALU.mult)), reads=[tg + "kva", "ss%d" % i, "gains"], writes=[tg + "ckvn"])
                P.c("pe", (lambda e, ckvn=ckvn, pT=pT: e.transpose(pT[:, 0, :], ckvn[:], k.ident[:])), reads=[tg + "ckvn", "ident"], writes=[pTk])
                P.c("act", (lambda e, cT_=cT_, pT=pT: e.copy(out=cT_[:], in_=pT[:, 0, :])), reads=[pTk], writes=[tg + "cT"])
                P.c("pe", (lambda e, cT_=cT_, pkv=pkv: e.matmul(pkv[:], lhsT=cT_[:], rhs=Wkv[:], start=True, stop=True)), reads=[tg + "cT", "Wkv"], writes=[pkvk])
                pv = pkv[:].rearrange("p (h c) -> p h c", h=4)
                P.c("act", (lambda e, i=i, pv=pv: e.copy(out=VP[:, i, :, 0:64], in_=pv[:, :, 64:128])), reads=[pkvk], writes=["VP%d" % i])
                rh = head_norm(P, p2, tg + "hn", pv[:, :, 0:64], 4, 64, G[:, l, 480:544].unsqueeze(1).to_broadcast([128, 4, 64]), kfin[:, :, 0:64],
                               [pkvk], [tg + "kfinA"], extra_ss=(sr[:], tg + "sr", 96))
                P.c("dve", (lambda e, kva=kva, t32=t32: e.tensor_tensor(out=t32[:], in0=kva[:, 128:160], in1=G[:, l, 544:576], op=ALU.mult)),
                    reads=[tg + "kva", "gains"], writes=[tg + "t32"])
                if i < 32:
                    P.c("dve", (lambda e, t32=t32, kr=kr, rh=rh: e.tensor_tensor(out=kr[:], in0=t32[:].unsqueeze(1).to_broadcast([128, 4, 32]),
                                                                             in1=rh[:].unsqueeze(2).to_broadcast([128, 4, 32]), op=ALU.mult)),
                        reads=[tg + "t32", tg + "hnssh"], writes=[tg + "kr"])
                    rope_apply(P, p2, tg + "rp", kr[:], tab, kfin[:, :, 64:96], [tg + "kr", tg + "tab"], [tg + "kfinB"])
                else:
                    P.c("dve", (lambda e, t32=t32, kfin=kfin, rh=rh: e.tensor_tensor(out=kfin[:, :, 64:96], in0=t32[:].unsqueeze(1).to_broadcast([128, 4, 32]),
                                                                                 in1=rh[:].unsqueeze(2).to_broadcast([128, 4, 32]), op=ALU.mult)),
                        reads=[tg + "t32", tg + "hnssh"], writes=[tg + "kfinB"])
                heads_to_fm(P, pt4, kfin, 96, KT[:, :, i * 128:(i + 1) * 128], [tg + "kfinA", tg + "kfinB"], pt4k, ["KT%d" % i], k.ident,
                            eng=("act" if i % 2 else "dve"))
            nq = 16 if last else 18
            for i in range(nq):
                st = i % NS
                tg = "q%d" % st
                qa = p2.get(tg + "qa", [128, 256])
                qan = p2.get(tg + "qan", [128, 256], BF)
                qT_ = p2.get(tg + "qanT", [128, 2, 128], BF)
                qn = p2.get(tg + "qn", [128, 4, 96])
                qfin = p2.get(tg + "qfin", [128, 4, 96], BF)
                tab = p2.get(tg + "tab", [128, 32])
                pT = p2.get("pT%d" % (i % 2), [128, 2, 128], BF, psum=True)
                pkv = p2.get("pkv%d" % (i % 2), [128, 512], F32, psum=True)
                pt4 = p2.get("pt4%d" % (i % 2), [128, 4, 128], BF, psum=True)
                pTk, pkvk, pt4k = "pT%d" % (i % 2), "pkv%d" % (i % 2), "pt4%d" % (i % 2)
                jk = junk[i % 2]
                jkk = "junk%d" % (i % 2)
                P.dma("sp", qa[:], k.U[i * 128:(i + 1) * 128, 928:1184], writes=[tg + "qa"])
                if i < 16:
                    P.dma("sp", tab[:], I["ropeq"][i * 128:(i + 1) * 128, :], writes=[tg + "tab"])
                P.c("act", (lambda e, qa=qa, i=i, jk=jk: e.activation(out=jk[:], in_=qa[:], func=AF.Square, accum_out=ss[:, 40 + i:41 + i])),
                    reads=[tg + "qa"], writes=[jkk, "qs%d" % i])
                rsqrt(P, ss[:, 40 + i:41 + i], ss[:, 40 + i:41 + i], 1.0 / 256, EPS, ["qs%d" % i], ["qs%d" % i])
                P.c("dve", (lambda e, qa=qa, qan=qan, i=i: e.scalar_tensor_tensor(out=qan[:], in0=qa[:], scalar=ss[:, 40 + i:41 + i], in1=G[:, l, 0:256],
                                                                                 op0=ALU.mult, op1=ALU.mult)), reads=[tg + "qa", "qs%d" % i, "gains"], writes=[tg + "qan"])
                for j in range(2):
                    P.c("pe", (lambda e, qan=qan, j=j, pT=pT: e.transpose(pT[:, j, :], qan[:, j * 128:(j + 1) * 128], k.ident[:])), reads=[tg + "qan", "ident"], writes=[pTk])
                P.c("act", (lambda e, qT_=qT_, pT=pT: e.copy(out=qT_[:], in_=pT[:])), reads=[pTk], writes=[tg + "qT"])
                for j in range(2):
                    P.c("pe", (lambda e, qT_=qT_, j=j, pkv=pkv: e.matmul(pkv[:, 0:384], lhsT=qT_[:, j, :], rhs=Wq[:, j, :], start=(j == 0), stop=(j == 1))),
                        reads=[tg + "qT", "Wq"], writes=[pkvk])
                pq = pkv[:, 0:384].rearrange("p (h c) -> p h c", h=4)
                head_norm(P, p2, tg + "hn", pq, 4, 96, G[:, l, 384:480].unsqueeze(1).to_broadcast([128, 4, 96]), qn[:], [pkvk], [tg + "qn"])
                P.c("act", (lambda e, qn=qn, qfin=qfin: e.copy(out=qfin[:, :, 0:64], in_=qn[:, :, 0:64])), reads=[tg + "qn"], writes=[tg + "qfinA"])
                if i < 16:
                    rope_apply(P, p2, tg + "rp", qn[:, :, 64:96], tab, qfin[:, :, 64:96], [tg + "qn", tg + "tab"], [tg + "qfinB"])
                else:
                    P.c("dve", (lambda e, qn=qn, qfin=qfin: e.tensor_copy(out=qfin[:, :, 64:96], in_=qn[:, :, 64:96])), reads=[tg + "qn"], writes=[tg + "qfinB"])
                heads_to_fm(P, pt4, qfin, 96, QT[:, :, i * 128:(i + 1) * 128], [tg + "qfinA", tg + "qfinB"], pt4k, ["QT%d" % i], k.ident,
                            eng=("act" if i % 2 else "dve"))
            P.flush()
        with ExitStack() as es3:
            p3 = Pool_(nc, es3)
            r = attn_alloc(nc, p3)
            sc = 96 ** -0.5
            for qi in range(16 if last else 18):
                cl = list(range(34)) if qi < 16 else [32, 33]
                attn_tile(k, r, (lambda h, qi=qi: (QT[:, h, qi * 128:(qi + 1) * 128], [])), 96,
                          (lambda h, cl=cl: [(KT[:, h, c * 128:(c + 1) * 128], VP[:, c, h, :], []) for c in cl]), sc, 2, qi * 128)
                if qi % 4 == 3:
                    P.flush()
            P.flush()


def stage_na(k, l):
    nc, P, I = k.nc, k.P, k.I
    last = (l == DEPTH - 1)
    G = k.gains
    NS = 3
    with ExitStack() as es:
        pl = Pool_(nc, es)
        KT = pl.sb([64, 4, 22 * 128], BF, "nKT")
        VP = pl.sb([128, 22, 4, 65], BF, "nVP")
        QT = pl.sb([64, 4, TA], BF, "nQT")
        with ExitStack() as es2:
            p2 = Pool_(nc, es2)
            P.c("pool", lambda e: e.memset(VP[:, :, :, 64:65], 1.0), writes=["VP1"])
            for i in range(22):
                if i < 2:
                    src = k.GN[256 + i * 128:256 + (i + 1) * 128, :]
                elif i < 18:
                    src = k.U[(i - 2) * 128:(i - 1) * 128, 160:672]
                elif i < 20:
                    src = k.GN[512 + (i - 18) * 128:512 + (i - 17) * 128, :]
                else:
                    src = k.U[T + (i - 20) * 128:T + (i - 19) * 128, 160:672]
                tg = "nk%d" % (i % NS)
                kv = p2.get(tg + "kv", [128, 512])
                kfin = p2.get(tg + "kfin", [128, 4, 64], BF)
                pt4 = p2.get("npt4%d" % (i % 2), [128, 4, 128], BF, psum=True)
                P.dma("sp", kv[:], src, writes=[tg + "kv"])
                P.c("act", (lambda e, kv=kv, i=i: e.copy(out=VP[:, i, :, 0:64], in_=kv[:, 256:512].rearrange("p (h d) -> p h d", h=4))),
                    reads=[tg + "kv"], writes=["VP%d" % i])
                head_norm(P, p2, tg + "hn", kv[:, 0:256].rearrange("p (h d) -> p h d", h=4), 4, 64,
                          G[:, l, 640:704].unsqueeze(1).to_broadcast([128, 4, 64]), kfin[:], [tg + "kv"], [tg + "kfin"])
                heads_to_fm(P, pt4, kfin, 64, KT[:, :, i * 128:(i + 1) * 128], [tg + "kfin"], "npt4%d" % (i % 2), ["KT%d" % i], k.ident,
                            eng=("act" if i % 2 else "dve"))
            for i in range(16 if last else 18):
                tg = "nq%d" % (i % NS)
                q = p2.get(tg + "q", [128, 256])
                qfin = p2.get(tg + "qfin", [128, 4, 64], BF)
                pt4 = p2.get("npt4%d" % (i % 2), [128, 4, 128], BF, psum=True)
                P.dma("sp", q[:], k.U[i * 128:(i + 1) * 128, 1184:1440], writes=[tg + "q"])
                head_norm(P, p2, tg + "hn", q[:].rearrange("p (h d) -> p h d", h=4), 4, 64,
                          G[:, l, 576:640].unsqueeze(1).to_broadcast([128, 4, 64]), qfin[:], [tg + "q"], [tg + "qfin"])
                heads_to_fm(P, pt4, qfin, 64, QT[:, :, i * 128:(i + 1) * 128], [tg + "qfin"], "npt4%d" % (i % 2), ["QT%d" % i], k.ident,
                            eng=("act" if i % 2 else "dve"))
            P.flush()
        with ExitStack() as es3:
            p3 = Pool_(nc, es3)
            r = attn_alloc(nc, p3)
            bb = [p3.sb([128, 6, 4, 128], F32, "nab") for _ in range(2)]
            sc = 64 ** -0.5
            for qi in range(16 if last else 18):
                if qi < 16:
                    t0 = 0 if qi == 0 else (14 if qi == 15 else qi)
                    nl = 6 if qi in (0, 15) else 5
                    cl = list(range(t0, t0 + nl)) + [20, 21]
                    B = bb[qi % 2]
                    P.dma("sp", B[:], I["nab"][l, qi], writes=["bias"])
                    bias = (lambda h, B=B: B[:, :, h, :])
                else:
                    cl, nl, bias = [20, 21], 0, None
                attn_tile(k, r, (lambda h, qi=qi: (QT[:, h, qi * 128:(qi + 1) * 128], [])), 64,
                          (lambda h, cl=cl: [(KT[:, h, c * 128:(c + 1) * 128], VP[:, c, h, :], []) for c in cl]), sc, 4, qi * 128,
                          bias=bias, nloc=nl)
                if qi % 4 == 3:
                    P.flush()
            P.flush()


def stage_attn(k, l, which):
    mla, na = (which == "mla"), (which == "na")
    nc, P, I = k.nc, k.P, k.I
    last = (l == DEPTH - 1)
    G = k.gains
    NS = 3
    nq = 16 if last else 18
    with ExitStack() as es:
        pl = Pool_(nc, es)
        if mla:
            KT = pl.sb([96, 4, 34 * 128], BF, "KT")
            VP = pl.sb([128, 34, 4, 65], BF, "VP")
            QT = pl.sb([96, 4, TA], BF, "QT")
        else:
            nKT = pl.sb([64, 4, 22 * 128], BF, "nKT")
            nVP = pl.sb([128, 22, 4, 65], BF, "nVP")
            nQT = pl.sb([64, 4, TA], BF, "nQT")
        with ExitStack() as es2:
            p2 = Pool_(nc, es2)
            if mla:
                Wkv = p2.sb([128, 512], BF, "wkv")
                Wq = p2.sb([128, 2, 384], BF, "wq")
                P.dma("pool", Wkv[:], I["mla_wkv_b"][l], writes=["Wkv"])
                P.dma("pool", Wq[:], I["mla_wq_b"][l].rearrange("(j p) n -> p j n", p=128), writes=["Wq"])
                P.c("pool", lambda e: e.memset(VP[:, :, :, 64:65], 1.0), writes=["VP1"])
                ss = p2.sb([128, 64], F32, "ss")
                junkk = [p2.sb([128, 160], F32, "junkk") for _ in range(4)]
                junkq = [p2.sb([128, 256], F32, "junkq") for _ in range(4)]
                pkvs = [p2.ps([128, 512], F32, "pkv") for _ in range(4)]
                pt4s = [p2.ps([128, 4, 128], BF, "pt4") for _ in range(4)]
            else:
                P.c("pool", lambda e: e.memset(nVP[:, :, :, 64:65], 1.0), writes=["nVP1"])
                npt4s = [p2.ps([128, 4, 128], BF, "npt4") for _ in range(4)]

            def ktile(i):
                sid = i % 2
                tg = "k%d_%d" % (sid, (i // 2) % 2)
                src = k.G1[i * 128:(i + 1) * 128, 0:160] if i < 32 else k.U[T + (i - 32) * 128:T + (i - 31) * 128, 0:160]
                kva = p2.get(tg + "kva", [128, 160])
                ckvn = p2.get(tg + "ckvn", [128, 128], BF)
                cT_ = p2.get(tg + "cT", [128, 128], BF)
                sr = p2.get(tg + "sr", [128, 1])
                kfin = p2.get(tg + "kfin", [128, 4, 96], BF)
                t32 = p2.get(tg + "t32", [128, 32])
                kr = p2.get(tg + "kr", [128, 4, 32])
                tab = p2.get(tg + "tab", [128, 32])
                pkv, pt4 = pkvs[sid], pt4s[sid]
                pT = pt4
                pTk, pkvk, pt4k = "pt4%d" % sid, "pkv%d" % sid, "pt4%d" % sid
                jk = junkk[sid * 2 + (i // 2) % 2]
                jkk = "junkk%d" % (sid * 2 + (i // 2) % 2)
                P.dma("sp", kva[:], src, writes=[tg + "kva"])
                if i < 32:
                    P.dma("sp", tab[:], I["ropek"][i * 128:(i + 1) * 128, :], writes=[tg + "tab"])
                P.c("act", (lambda e: e.activation(out=jk[:, 0:128], in_=kva[:, 0:128], func=AF.Square, accum_out=ss[:, i:i + 1])),
                    reads=[tg + "kva"], writes=[jkk, "ss%d" % i])
                P.c("act", (lambda e: e.activation(out=jk[:, 128:160], in_=kva[:, 128:160], func=AF.Square, accum_out=sr[:])),
                    reads=[tg + "kva"], writes=[jkk + "b", tg + "sr"])
                rsqrt(P, ss[:, i:i + 1], ss[:, i:i + 1], 1.0 / 128, EPS, ["ss%d" % i], ["ss%d" % i])
                P.c("dve", (lambda e: e.scalar_tensor_tensor(out=ckvn[:], in0=kva[:, 0:128], scalar=ss[:, i:i + 1], in1=G[:, l, 256:384],
                                                             op0=ALU.mult, op1=ALU.mult)), reads=[tg + "kva", "ss%d" % i, "gains"], writes=[tg + "ckvn"])
                P.c("pe", (lambda e: e.transpose(pT[:, 0, :], ckvn[:], k.ident[:])), reads=[tg + "ckvn", "ident"], writes=[pTk])
                P.c("act", (lambda e: e.copy(out=cT_[:], in_=pT[:, 0, :])), reads=[pTk], writes=[tg + "cT"])
                P.c("pe", (lambda e: e.matmul(pkv[:], lhsT=cT_[:], rhs=Wkv[:], start=True, stop=True)), reads=[tg + "cT", "Wkv"], writes=[pkvk])
                pv = pkv[:].rearrange("p (h c) -> p h c", h=4)
                P.c("act", (lambda e: e.copy(out=VP[:, i, :, 0:64], in_=pv[:, :, 64:128])), reads=[pkvk], writes=["VP%d" % i])
                rh = head_norm(P, p2, tg + "hn", pv[:, :, 0:64], 4, 64, G[:, l, 480:544].unsqueeze(1).to_broadcast([128, 4, 64]), kfin[:, :, 0:64],
                               [pkvk], [tg + "kfinA"], extra_ss=(sr[:], tg + "sr", 96))
                P.c("dve", (lambda e: e.tensor_tensor(out=t32[:], in0=kva[:, 128:160], in1=G[:, l, 544:576], op=ALU.mult)),
                    reads=[tg + "kva", "gains"], writes=[tg + "t32"])
                if i < 32:
                    P.c("dve", (lambda e: e.tensor_tensor(out=kr[:], in0=t32[:].unsqueeze(1).to_broadcast([128, 4, 32]),
                                                          in1=rh[:].unsqueeze(2).to_broadcast([128, 4, 32]), op=ALU.mult)),
                        reads=[tg + "t32", tg + "hnssh"], writes=[tg + "kr"])
                    rope_apply(P, p2, tg + "rp", kr[:], tab, kfin[:, :, 64:96], [tg + "kr", tg + "tab"], [tg + "kfinB"])
                else:
                    P.c("dve", (lambda e: e.tensor_tensor(out=kfin[:, :, 64:96], in0=t32[:].unsqueeze(1).to_broadcast([128, 4, 32]),
                                                          in1=rh[:].unsqueeze(2).to_broadcast([128, 4, 32]), op=ALU.mult)),
                        reads=[tg + "t32", tg + "hnssh"], writes=[tg + "kfinB"])
                heads_to_fm(P, pt4, kfin, 96, KT[:, :, i * 128:(i + 1) * 128], [tg + "kfinA", tg + "kfinB"], pt4k, ["KT%d" % i], k.ident,
                            eng=("act" if i % 2 else "dve"))

            def qtile(i):
                sid = 2 + i % 2
                tg = "q%d_%d" % (sid, (i // 2) % 2)
                qa = p2.get(tg + "qa", [128, 256])
                qan = p2.get(tg + "qan", [128, 256], BF)
                qT_ = p2.get(tg + "qanT", [128, 2, 128], BF)
                qn = p2.get(tg + "qn", [128, 4, 96])
                qfin = p2.get(tg + "qfin", [128, 4, 96], BF)
                tab = p2.get(tg + "tab", [128, 32])
                pkv, pt4 = pkvs[sid], pt4s[sid]
                pT = pt4
                pTk, pkvk, pt4k = "pt4%d" % sid, "pkv%d" % sid, "pt4%d" % sid
                jk = junkq[(sid - 2) * 2 + (i // 2) % 2]
                jkk = "junkq%d" % ((sid - 2) * 2 + (i // 2) % 2)
                P.dma("sp", qa[:], k.U[i * 128:(i + 1) * 128, 928:1184], writes=[tg + "qa"])
                if i < 16:
                    P.dma("sp", tab[:], I["ropeq"][i * 128:(i + 1) * 128, :], writes=[tg + "tab"])
                P.c("act", (lambda e: e.activation(out=jk[:], in_=qa[:], func=AF.Square, accum_out=ss[:, 40 + i:41 + i])),
                    reads=[tg + "qa"], writes=[jkk, "qs%d" % i])
                rsqrt(P, ss[:, 40 + i:41 + i], ss[:, 40 + i:41 + i], 1.0 / 256, EPS, ["qs%d" % i], ["qs%d" % i])
                P.c("dve", (lambda e: e.scalar_tensor_tensor(out=qan[:], in0=qa[:], scalar=ss[:, 40 + i:41 + i], in1=G[:, l, 0:256],
                                                             op0=ALU.mult, op1=ALU.mult)), reads=[tg + "qa", "qs%d" % i, "gains"], writes=[tg + "qan"])
                for j in range(2):
                    P.c("pe", (lambda e, j=j: e.transpose(pT[:, j, :], qan[:, j * 128:(j + 1) * 128], k.ident[:])), reads=[tg + "qan", "ident"], writes=[pTk])
                P.c("act", (lambda e: e.copy(out=qT_[:], in_=pT[:, 0:2, :])), reads=[pTk], writes=[tg + "qT"])
                for j in range(2):
                    P.c("pe", (lambda e, j=j: e.matmul(pkv[:, 0:384], lhsT=qT_[:, j, :], rhs=Wq[:, j, :], start=(j == 0), stop=(j == 1))),
                        reads=[tg + "qT", "Wq"], writes=[pkvk])
                pq = pkv[:, 0:384].rearrange("p (h c) -> p h c", h=4)
                head_norm(P, p2, tg + "hn", pq, 4, 96, G[:, l, 384:480].unsqueeze(1).to_broadcast([128, 4, 96]), qn[:], [pkvk], [tg + "qn"])
                P.c("act", (lambda e: e.copy(out=qfin[:, :, 0:64], in_=qn[:, :, 0:64])), reads=[tg + "qn"], writes=[tg + "qfinA"])
                if i < 16:
                    rope_apply(P, p2, tg + "rp", qn[:, :, 64:96], tab, qfin[:, :, 64:96], [tg + "qn", tg + "tab"], [tg + "qfinB"])
                else:
                    P.c("dve", (lambda e: e.tensor_copy(out=qfin[:, :, 64:96], in_=qn[:, :, 64:96])), reads=[tg + "qn"], writes=[tg + "qfinB"])
                heads_to_fm(P, pt4, qfin, 96, QT[:, :, i * 128:(i + 1) * 128], [tg + "qfinA", tg + "qfinB"], pt4k, ["QT%d" % i], k.ident,
                            eng=("act" if i % 2 else "dve"))

            def nktile(i):
                if i < 2:
                    src = k.GN[256 + i * 128:256 + (i + 1) * 128, :]
                elif i < 18:
                    src = k.U[(i - 2) * 128:(i - 1) * 128, 160:672]
                elif i < 20:
                    src = k.GN[512 + (i - 18) * 128:512 + (i - 17) * 128, :]
                else:
                    src = k.U[T + (i - 20) * 128:T + (i - 19) * 128, 160:672]
                tg = "nk%d_%d" % (i % 2, (i // 2) % 2)
                kv = p2.get(tg + "kv", [128, 512])
                kfin = p2.get(tg + "kfin", [128, 4, 64], BF)
                P.dma("sp", kv[:], src, writes=[tg + "kv"])
                P.c("act", (lambda e: e.copy(out=nVP[:, i, :, 0:64], in_=kv[:, 256:512].rearrange("p (h d) -> p h d", h=4))),
                    reads=[tg + "kv"], writes=["nVP%d" % i])
                head_norm(P, p2, tg + "hn", kv[:, 0:256].rearrange("p (h d) -> p h d", h=4), 4, 64,
                          G[:, l, 640:704].unsqueeze(1).to_broadcast([128, 4, 64]), kfin[:], [tg + "kv"], [tg + "kfin"])
                heads_to_fm(P, npt4s[i % 2], kfin, 64, nKT[:, :, i * 128:(i + 1) * 128], [tg + "kfin"], "npt4%d" % (i % 2), ["nKT%d" % i], k.ident,
                            eng=("act" if i % 2 else "dve"))

            def nqtile(i):
                tg = "nq%d_%d" % (i % 2, (i // 2) % 2)
                q = p2.get(tg + "q", [128, 256])
                qfin = p2.get(tg + "qfin", [128, 4, 64], BF)
                P.dma("sp", q[:], k.U[i * 128:(i + 1) * 128, 1184:1440], writes=[tg + "q"])
                head_norm(P, p2, tg + "hn", q[:].rearrange("p (h d) -> p h d", h=4), 4, 64,
                          G[:, l, 576:640].unsqueeze(1).to_broadcast([128, 4, 64]), qfin[:], [tg + "q"], [tg + "qfin"])
                heads_to_fm(P, npt4s[2 + i % 2], qfin, 64, nQT[:, :, i * 128:(i + 1) * 128], [tg + "qfin"], "npt4%d" % (2 + i % 2), ["nQT%d" % i], k.ident,
                            eng=("dve" if i % 2 else "act"))

            def run(fn, idxs):
                return lambda: [fn(i) for i in idxs]
            if mla:
                P.record_streams([run(ktile, range(0, 34, 2)), run(ktile, range(1, 34, 2)),
                                  run(qtile, range(0, nq, 2)), run(qtile, range(1, nq, 2))])
            else:
                P.record_streams([run(nktile, range(0, 22, 2)), run(nktile, range(1, 22, 2)),
                                  run(nqtile, range(0, nq, 2)), run(nqtile, range(1, nq, 2))])
            P.flush()
        with ExitStack() as es3:
            p3 = Pool_(nc, es3)
            r = attn_alloc(nc, p3)
            if na:
                bb = [p3.sb([128, 6, 4, 128], F32, "nab") for _ in range(2)]
            sc = 96 ** -0.5

            def mla_sweep():
                for qi in range(nq if mla else 0):
                    cl = list(range(34)) if qi < 16 else [32, 33]
                    attn_tile(k, r, (lambda h, qi=qi: (QT[:, h, qi * 128:(qi + 1) * 128], [])), 96,
                              (lambda h, cl=cl: [(KT[:, h, c * 128:(c + 1) * 128], VP[:, c, h, :], []) for c in cl]), sc, 2, qi * 128)
            if mla and l + 1 < DEPTH:
                P.record_streams([mla_sweep, lambda: mods_ops(k, p3, l + 1)])
            else:
                mla_sweep()
            sc = 64 ** -0.5
            for qi in range(nq if na else 0):
                if qi < 16:
                    t0 = 0 if qi == 0 else (14 if qi == 15 else qi)
                    nl = 6 if qi in (0, 15) else 5
                    cl = list(range(t0, t0 + nl)) + [20, 21]
                    B = bb[qi % 2]
                    bk = "bias%d" % (qi % 2)
                    P.dma("sp", B[:], I["nab"][l, qi], writes=[bk])
                    bias = (lambda h, B=B: B[:, :, h, :])
                else:
                    cl, nl, bias, bk = [20, 21], 0, None, "bias0"
                attn_tile(k, r, (lambda h, qi=qi: (nQT[:, h, qi * 128:(qi + 1) * 128], [])), 64,
                          (lambda h, cl=cl: [(nKT[:, h, c * 128:(c + 1) * 128], nVP[:, c, h, :], []) for c in cl]), sc, 4, qi * 128,
                          bias=bias, nloc=nl, bkey=bk)
            P.flush()


def stage_hgrn(k, l, upto=None):
    nc, P, I = k.nc, k.P, k.I
    last = (l == DEPTH - 1)
    NCH = 36
    G = k.gains
    if not hasattr(k, "EXSin"):
        k.EXSin = nc.dram_tensor("EXSin", [256, 256], F32, kind="Internal").ap()
        k.GS = nc.dram_tensor("GS", [512, 256], F32, kind="Internal").ap()
    with ExitStack() as es:
        pl = Pool_(nc, es)
        LB = pl.sb([128, 2, 2], F32, "LB")
        OML = pl.sb([128, 2, 2], F32, "OML")
        lbt = pl.sb([128, 3, 4], F32, "lbt")
        lsum = pl.sb([128, 4], F32, "lsum")
        P.dma("sp", lbt[:], I["hg_lbT"].rearrange("p s d h -> p s (d h)"), writes=["lbt"])
        P.c("act", lambda e: e.activation(out=lbt[:], in_=lbt[:], func=AF.Exp), reads=["lbt"], writes=["lbt"])
        P.c("dve", lambda e: e.tensor_tensor(out=lsum[:], in0=lbt[:, 0, :], in1=lbt[:, 1, :], op=ALU.add), reads=["lbt"], writes=["lsum"])
        P.c("dve", lambda e: e.tensor_tensor(out=lsum[:], in0=lsum[:], in1=lbt[:, 2, :], op=ALU.add), reads=["lsum", "lbt"], writes=["lsum"])
        P.c("dve", lambda e: e.reciprocal(out=lsum[:], in_=lsum[:]), reads=["lsum"], writes=["lsum"])
        LBf = LB[:].rearrange("p d h -> p (d h)")
        OMf = OML[:].rearrange("p d h -> p (d h)")
        if l == 0:
            P.c("dve", lambda e: e.tensor_tensor(out=LBf, in0=lbt[:, 0, :], in1=lsum[:], op=ALU.mult), reads=["lsum", "lbt"], writes=["LB"])
        else:
            P.c("dve", lambda e: e.tensor_tensor(out=LBf, in0=lbt[:, 0, :], in1=lbt[:, 1, :], op=ALU.add), reads=["lbt"], writes=["LB"])
            P.c("dve", lambda e: e.tensor_tensor(out=LBf, in0=LBf, in1=lsum[:], op=ALU.mult), reads=["lsum", "LB"], writes=["LB"])
        P.c("dve", lambda e: e.tensor_scalar(out=LBf, in0=LBf, scalar1=1e-6, scalar2=1.0 - 1e-6, op0=ALU.max, op1=ALU.min), reads=["LB"], writes=["LB"])
        P.c("dve", lambda e: e.tensor_scalar(out=OMf, in0=LBf, scalar1=-1.0, scalar2=1.0, op0=ALU.mult, op1=ALU.add), reads=["LB"], writes=["OML"])
        zt = pl.sb([128, 256], F32, "zt")
        P.c("pool", lambda e: e.memset(zt[:], 0.0), writes=["zt"])
        for d_ in range(2):
            P.dma("sp", k.EXSin[d_ * 128:(d_ + 1) * 128, :], zt[:], reads=["zt"], writes=["EXSin"])
        cmask = pl.sb([128, TA // 2], F32, "cmask")
        P.c("pool", lambda e: e.memset(cmask[:], 1.0), writes=["cmask"])
        P.c("pool", lambda e: e.memset(cmask[:].rearrange("p (c t) -> p c t", t=64)[:, :, 0:1], 0.0), reads=["cmask"], writes=["cmask"])
        mk = pl.sb([64, 2, 64], F32, "trimask")
        P.c("pool", lambda e: e.memset(mk[:], 1.0), writes=["mk"])
        P.c("pool", lambda e: e.affine_select(out=mk[:, 0, :], in_=mk[:, 0, :], compare_op=ALU.is_ge, fill=0.0, base=0, pattern=[[1, 64]], channel_multiplier=-1),
            reads=["mk"], writes=["mk"])
        P.c("pool", lambda e: e.affine_select(out=mk[:, 1, :], in_=mk[:, 1, :], compare_op=ALU.is_ge, fill=0.0, base=0, pattern=[[-1, 64]], channel_multiplier=1),
            reads=["mk"], writes=["mk"])
        mka = pl.sb([64, 2, 64], F32, "mka")
        P.c("pool", lambda e: e.memset(mka[:], 0.0), writes=["mka"])
        P.c("pool", lambda e: e.memset(mka[0:32, 0, 32:64], 1.0), reads=["mka"], writes=["mka"])
        P.c("pool", lambda e: e.memset(mka[32:64, 1, 0:32], 1.0), reads=["mka"], writes=["mka"])
        P.c("pool", lambda e: e.memset(mk[0:32, 0, 32:64], 0.0), reads=["mk"], writes=["mk"])
        P.c("pool", lambda e: e.memset(mk[32:64, 1, 0:32], 0.0), reads=["mk"], writes=["mk"])
        mk4 = pl.sb([64, 4, 64], F32, "mk4")
        P.c("dve", lambda e: e.tensor_copy(out=mk4[:, 0:2, :], in_=mk[:]), reads=["mk"], writes=["mk4"])
        P.c("dve", lambda e: e.tensor_copy(out=mk4[:, 2:4, :], in_=mka[:]), reads=["mka"], writes=["mk4"])
        bm = pl.sb([128, 128], F32, "bm")
        P.c("pool", lambda e: e.memset(bm[:], 0.0), writes=["bm"])
        P.c("pool", lambda e: e.memset(bm[0:64, 0:64], 1.0), reads=["bm"], writes=["bm"])
        P.c("pool", lambda e: e.memset(bm[64:128, 64:128], 1.0), reads=["bm"], writes=["bm"])
        pm = pl.sb([128, 2], F32, "pm")
        P.c("pool", lambda e: e.memset(pm[:], 0.0), writes=["pm"])
        P.c("pool", lambda e: e.memset(pm[0:64, 0:1], 1.0), reads=["pm"], writes=["pm"])
        P.c("pool", lambda e: e.memset(pm[64:128, 1:2], 1.0), reads=["pm"], writes=["pm"])
        nch_out = 32 if last else 36
        for hp in range(2):
            with ExitStack() as es2:
                p2 = Pool_(nc, es2)
                QT_ = p2.sb([128, 2, TA], BF, "hq")
                KT_ = p2.sb([128, 2, TA], BF, "hk")
                QE = p2.sb([128, 2, TA], BF, "hqe")
                KA = p2.sb([128, 2, TA], BF, "hka")
                EBL = p2.sb([128, 2, NCH], F32, "ebl")
                BLs = p2.sb([128, 2, NCH], F32, "bls")
                KH = p2.sb([64, NCH, 2, 128], BF, "KH")
                IT = p2.sb([64, NCH, 128], BF, "IT")
                SB = p2.sb([128, 2, NCH, 128], BF, "SB")
                SG = p2.sb([64, NCH, 128], F32, "SG")
                for c9 in range(0, NCH, 9):
                    P.dma("pool", IT[:, c9:c9 + 9, :], k.U[c9 * 64:(c9 + 9) * 64, 672 + hp * 128:672 + (hp + 1) * 128].rearrange("(c s) n -> s c n", s=64), writes=["IT%d" % c9])
                    P.dma("sp", SG[:, c9:c9 + 9, :], k.U[c9 * 64:(c9 + 9) * 64, 1440 + hp * 128:1440 + (hp + 1) * 128].rearrange("(c s) n -> s c n", s=64), writes=["SG"])
                P.c("act", lambda e: e.activation(out=SG[:], in_=SG[:], func=AF.Exp, scale=-1.0), reads=["SG"], writes=["SG"])
                P.c("dve", lambda e: e.tensor_scalar_add(out=SG[:], in0=SG[:], scalar1=1.0), reads=["SG"], writes=["SG"])
                P.c("dve", lambda e: e.reciprocal(out=SG[:], in_=SG[:]), reads=["SG"], writes=["SG"])
                NTB = 3
                for tb in range(NTB):
                  with ExitStack() as es3:
                    TB, NB = TA // NTB, NCH // NTB
                    tsl = slice(tb * TB, (tb + 1) * TB)
                    csl = slice(tb * NB, (tb + 1) * NB)
                    p3 = Pool_(nc, es3)
                    q = p3.sb([128, TB], F32, "hqq")
                    P.dma("sp", q[:], k.UT[768 + hp * 128:768 + (hp + 1) * 128, tsl], writes=["q"])

                    def dstream(d):
                        sd = "%d" % d
                        f = p3.sb([128, TB], F32, "hf")
                        lf = p3.sb([128, TB], F32, "hlf")
                        Pc = p3.sb([128, TB], F32, "hP")
                        b_ = p3.sb([128, TB], F32, "hb")
                        kk = p3.sb([128, TB], F32, "hkk")
                        kh = p3.sb([128, TB], BF, "hkh")
                        mref = p3.sb([128, NB, 2], F32, "hmref")
                        blt = p3.sb([128, NB], F32, "hblt")
                        c0t = p3.sb([128, NB], F32, "hc0t")
                        ptk = p3.ps([64, 4, 128], BF, "ptk")
                        lbs, oms = LB[:, d, hp:hp + 1], OML[:, d, hp:hp + 1]
                        kf, klf, kP, kb, kkk = "f" + sd, "lf" + sd, "Pc" + sd, "b" + sd, "kk" + sd
                        P.dma("sp", f[:], k.UT[256 + d * 256 + hp * 128:256 + d * 256 + (hp + 1) * 128, tsl], writes=[kf])
                        P.c("act", lambda e: e.activation(out=f[:], in_=f[:], func=AF.Sigmoid), reads=[kf], writes=[kf])
                        P.c("act", lambda e: e.activation(out=f[:], in_=f[:], func=AF.Identity, scale=oms, bias=lbs), reads=[kf, "LB", "OML"], writes=[kf])
                        P.c("act", lambda e: e.activation(out=lf[:], in_=f[:], func=AF.Ln), reads=[kf], writes=[klf])
                        P.c("act", lambda e: e.activation(out=kk[:], in_=f[:], func=AF.Identity, scale=-1.0, bias=1.0), reads=[kf], writes=[kkk])
                        P.c("dve", lambda e: e.tensor_tensor_scan(out=Pc[:], data0=cmask[:, 0:TB], data1=lf[:], initial=0.0, op0=ALU.mult, op1=ALU.add),
                            reads=[klf, "cmask"], writes=[kP])
                        Pv = Pc[:].rearrange("p (c t) -> p c t", t=64)
                        P.c("dve", lambda e: e.tensor_copy(out=blt[:], in_=Pv[:, :, 63]), reads=[kP], writes=["blt" + sd])
                        P.c("dve", lambda e: e.tensor_copy(out=c0t[:], in_=Pv[:, :, 31]), reads=[kP], writes=["c0t" + sd])
                        blv, c0v = blt[:], c0t[:]
                        blb = blv.unsqueeze(2).to_broadcast([128, NB, 64])
                        bv = b_[:].rearrange("p (c t) -> p c t", t=64)
                        lfv = lf[:].rearrange("p (c t) -> p c t", t=64)
                        rk = [kP, "blt" + sd, "c0t" + sd]
                        P.c("act", lambda e: e.activation(out=EBL[:, d, csl], in_=blv, func=AF.Exp), reads=rk, writes=["EBL" + sd])
                        P.c("dve", lambda e: e.tensor_copy(out=BLs[:, d, csl], in_=blv), reads=["blt" + sd], writes=["BLs" + sd])
                        if d == 0:
                            P.c("act", lambda e: e.copy(out=b_[:], in_=Pc[:]), reads=rk, writes=[kb])
                            P.c("dve", lambda e: e.tensor_scalar_mul(out=mref[:, :, 0], in0=c0v, scalar1=0.5), reads=rk, writes=["mref0" + sd])
                            P.c("dve", lambda e: e.tensor_tensor(out=mref[:, :, 1], in0=c0v, in1=blv, op=ALU.add), reads=rk, writes=["mref1" + sd])
                            P.c("dve", lambda e: e.tensor_scalar_mul(out=mref[:, :, 1], in0=mref[:, :, 1], scalar1=0.5), reads=["mref1" + sd], writes=["mref1" + sd])
                        else:
                            P.c("dve", lambda e: e.tensor_tensor(out=bv, in0=blb, in1=Pv, op=ALU.subtract), reads=rk, writes=[kb])
                            P.c("dve", lambda e: e.tensor_tensor(out=bv, in0=bv, in1=lfv, op=ALU.add), reads=[kb, klf], writes=[kb])
                            P.c("dve", lambda e: e.scalar_tensor_tensor(out=mref[:, :, 0], in0=c0v, scalar=-0.5, in1=blv, op0=ALU.mult, op1=ALU.add),
                                reads=rk, writes=["mref0" + sd])
                            P.c("dve", lambda e: e.tensor_tensor(out=mref[:, :, 1], in0=blv, in1=c0v, op=ALU.subtract), reads=rk, writes=["mref1" + sd])
                            P.c("dve", lambda e: e.tensor_scalar_mul(out=mref[:, :, 1], in0=mref[:, :, 1], scalar1=0.5), reads=["mref1" + sd], writes=["mref1" + sd])
                        P.c("act", lambda e: e.activation(out=f[:], in_=b_[:], func=AF.Exp), reads=[kb], writes=[kf])
                        P.c("dve", lambda e: e.scalar_tensor_tensor(out=QE[:, d, tsl], in0=q[:], scalar=0.125, in1=f[:], op0=ALU.mult, op1=ALU.mult),
                            reads=["q", kf], writes=["QE" + sd])
                        P.c("dve", lambda e: e.tensor_tensor(out=Pv, in0=blb, in1=bv, op=ALU.subtract), reads=[kb, kP, "blt" + sd], writes=[kP])
                        P.c("act", lambda e: e.activation(out=Pc[:], in_=Pc[:], func=AF.Exp), reads=[kP], writes=[kP])
                        P.c("dve", lambda e: e.tensor_tensor(out=kh[:], in0=kk[:], in1=Pc[:], op=ALU.mult), reads=[kkk, kP], writes=["kh" + sd])
                        b4 = b_[:].rearrange("p (c j t) -> p c j t", j=2, t=32)
                        P4 = Pc[:].rearrange("p (c j t) -> p c j t", j=2, t=32)
                        k4 = kk[:].rearrange("p (c j t) -> p c j t", j=2, t=32)
                        KA4 = KA[:, d, tsl].rearrange("p (c j t) -> p c j t", j=2, t=32)
                        P.c("act", lambda e: e.activation(out=P4[:, :, d, :], in_=b4[:, :, d, :], func=AF.Exp, scale=-1.0), reads=[kb, "kh" + sd], writes=[kP])
                        P.c("dve", lambda e: e.scalar_tensor_tensor(out=KA4[:, :, d, :], in0=P4[:, :, d, :], scalar=5.0e34, in1=k4[:, :, d, :], op0=ALU.min, op1=ALU.mult),
                            reads=[kkk, kP], writes=["KA" + sd])
                        P.c("pool", lambda e: e.memset(KA4[:, :, 1 - d, :], 0.0), reads=[], writes=["KAz" + sd])
                        P.c("dve", lambda e: e.tensor_tensor(out=b4, in0=b4, in1=mref[:].unsqueeze(3).to_broadcast([128, NB, 2, 32]), op=ALU.subtract),
                            reads=[kb, "mref0" + sd, "mref1" + sd, kf, kP], writes=[kb])
                        P.c("act", lambda e: e.activation(out=f[:], in_=b_[:], func=AF.Exp), reads=[kb, "QE" + sd], writes=[kf])
                        P.c("act", lambda e: e.activation(out=lf[:], in_=b_[:], func=AF.Exp, scale=-1.0), reads=[kb], writes=[klf])
                        P.c("dve", lambda e: e.scalar_tensor_tensor(out=QT_[:, d, tsl], in0=q[:], scalar=0.125, in1=f[:], op0=ALU.mult, op1=ALU.mult),
                            reads=["q", kf], writes=["QT_" + sd])
                        P.c("dve", lambda e: e.tensor_tensor(out=KT_[:, d, tsl], in0=kk[:], in1=lf[:], op=ALU.mult), reads=[kkk, klf], writes=["KT_" + sd])
                        for c0 in range(0, NB, 4):
                            n4 = min(4, NB - c0)
                            for ci in range(n4):
                                c = c0 + ci
                                P.c("pe", (lambda e, ci=ci, c=c: e.transpose(ptk[:, ci, :], kh[:, c * 64:(c + 1) * 64], k.ident[:])),
                                    reads=["kh" + sd, "ident"], writes=["ptk" + sd])
                            P.c("dve", (lambda e, c0=c0, n4=n4: e.tensor_copy(out=KH[:, tb * NB + c0:tb * NB + c0 + n4, d, :], in_=ptk[:, 0:n4, :])),
                                reads=["ptk" + sd], writes=["KH" + sd])

                    P.record_streams([lambda: dstream(0), lambda: dstream(1)])
                    P.flush()
                if upto == "hgrn1":
                    continue
                with ExitStack() as es4:
                    p4 = Pool_(nc, es4)
                    S = [p4.sb([128, 128], F32, "S%d" % d) for d in range(2)]
                    psG = [[p4.ps([128, 128], F32, "psG") for _ in range(2)] for d in range(2)]
                    tmpG = [[p4.sb([128, 128], F32, "tmpG") for _ in range(2)] for d in range(2)]
                    Sc = [p4.sb([128, 128], F32, "Sc%d" % d) for d in range(2)]
                    E = p4.sb([128, 2, 128], F32, "Eend")
                    Gs = p4.sb([128, 2, 128], F32, "Gs")
                    ng = [0, 0]

                    def step(d, c, Sd):
                        b = ng[d] % 2
                        ng[d] += 1
                        pg, tg_, gk = psG[d][b], tmpG[d][b], "G%d_%d" % (d, b)
                        P.c("pe", (lambda e: e.matmul(pg[:], lhsT=KH[:, c, d, :], rhs=IT[:, c, :], start=True, stop=True)),
                            reads=["KH", "IT"], writes=["ps" + gk])
                        P.c("dve", (lambda e: e.tensor_tensor(out=tg_[:], in0=pg[:], in1=bm[:], op=ALU.mult)), reads=["ps" + gk, "bm"], writes=["tm" + gk])
                        P.c("dve", (lambda e: e.scalar_tensor_tensor(out=Sd[:], in0=Sd[:], scalar=EBL[:, d, c:c + 1], in1=tg_[:],
                                                                     op0=ALU.mult, op1=ALU.add)), reads=["tm" + gk, "S%d" % d, "EBL"], writes=["S%d" % d])

                    def save(d, c, Sd):
                        P.c("act", (lambda e: e.copy(out=SB[:, d, c, :], in_=Sd[:])), reads=["S%d" % d], writes=["SB%d" % d])

                    def chain(d):
                        P.c("pool", (lambda e: e.memset(S[d][:], 0.0)), writes=["S%d" % d])
                        for i in range(4):
                            c = 32 + i if d == 0 else 35 - i
                            save(d, c, S[d])
                            step(d, c, S[d])
                        P.c("dve", (lambda e: e.tensor_copy(out=Sc[d][:], in_=S[d][:])), reads=["S%d" % d], writes=["Sc%d" % d])
                        for i in range(32):
                            c = i if d == 0 else 31 - i
                            save(d, c, S[d])
                            step(d, c, S[d])
                        P.c("dve", (lambda e: e.tensor_copy(out=E[:, d, :], in_=S[d][:])), reads=["S%d" % d], writes=["E%d" % d])
                        P.dma("sp", k.EXSin[d * 128:(d + 1) * 128, hp * 128:(hp + 1) * 128], E[:, d, :], reads=["E%d" % d], writes=["EXSin%d" % d])

                    P.record_streams([lambda: chain(0), lambda: chain(1)])
                    PF = p4.sb([128, 2, 32], F32, "PF")
                    ones32 = p4.sb([128, 32], F32, "ones32")
                    P.c("pool", lambda e: e.memset(ones32[:], 1.0), writes=["ones32"])
                    for d in range(2):
                        P.c("dve", (lambda e, d=d: e.tensor_tensor_scan(out=PF[:, d, :], data0=ones32[:], data1=BLs[:, d, 0:32], initial=0.0, op0=ALU.mult, op1=ALU.add)),
                            reads=["ones32"], writes=["PF%d" % d])
                    Dc = p4.sb([128, 2, 32], F32, "Dc")
                    P.c("dve", lambda e: e.tensor_tensor(out=Dc[:, 0, :], in0=PF[:, 0, :], in1=BLs[:, 0, 0:32], op=ALU.subtract), reads=["PF0"], writes=["Dc0"])
                    P.c("dve", lambda e: e.tensor_tensor(out=Dc[:, 1, :], in0=PF[:, 1, 31:32].to_broadcast([128, 32]), in1=PF[:, 1, :], op=ALU.subtract), reads=["PF1"], writes=["Dc1"])
                    P.c("act", lambda e: e.activation(out=Dc[:], in_=Dc[:], func=AF.Exp), reads=["Dc0", "Dc1"], writes=["Dc"])
                    P.cc(lambda e: e.collective_compute("AllGather", ALU.bypass, replica_groups=k.groups, ins=[k.EXSin], outs=[k.GS]),
                         reads=["EXSin0", "EXSin1"], writes=["GS"])
                    P.dma("sp", Gs[:, 0, :], k.GS[0:128, hp * 128:(hp + 1) * 128], reads=["GS"], writes=["Gs"])
                    P.dma("sp", Gs[:, 1, :], k.GS[384:512, hp * 128:(hp + 1) * 128], reads=["GS"], writes=["Gs"])
                    tmpT = p4.sb([128, 32, 128], F32, "tmpT")
                    for d in range(2):
                        oth = 0 if d == 0 else 1
                        P.c("dve", (lambda e, d=d: e.tensor_tensor(out=S[d][:], in0=Gs[:, d, :], in1=Sc[d][:], op=ALU.subtract)), reads=["Gs", "Sc%d" % d], writes=["S%d" % d])
                        P.c("dve", (lambda e, d=d, oth=oth: e.tensor_scalar_mul(out=S[d][:], in0=S[d][:], scalar1=k.hm[:, oth:oth + 1])), reads=["S%d" % d, "hm"], writes=["S%d" % d])
                        P.c("dve", (lambda e, d=d: e.tensor_tensor(out=tmpT[:], in0=Dc[:, d, :].unsqueeze(2).to_broadcast([128, 32, 128]),
                                                                  in1=S[d][:].unsqueeze(1).to_broadcast([128, 32, 128]), op=ALU.mult)), reads=["S%d" % d, "Dc"], writes=["tmpT"])
                        P.c("dve", (lambda e, d=d: e.tensor_tensor(out=SB[:, d, 0:32, :], in0=SB[:, d, 0:32, :], in1=tmpT[:], op=ALU.add)), reads=["tmpT", "SB%d" % d], writes=["SB%d" % d])
                    P.flush()
                if upto == "hgrn2":
                    continue
                with ExitStack() as es5:
                    p5 = Pool_(nc, es5)
                    psAB = [[p5.ps([64, 4, 2, 64], F32, "psAB") for _ in range(2)] for s_ in range(2)]
                    psOl = [p5.ps([64, 128], F32, "psO") for s_ in range(2)]
                    ptyl = [p5.ps([128, 64], BF, "hpty") for s_ in range(2)]
                    tA = [[p5.sb([64, 4, 2, 64], F32, "htA") for _ in range(2)] for s_ in range(2)]
                    AT = [[p5.sb([64, 2, 2, 64], BF, "hAT") for _ in range(2)] for s_ in range(2)]
                    on = [p5.sb([64, 2, 64], F32, "hon") for s_ in range(2)]
                    yb = [p5.sb([64, 2, 64], BF, "hyb") for s_ in range(2)]
                    KTm = p5.sb([128, 2, 2, TA], BF, "KTm")
                    KAm = p5.sb([128, 2, 2, TA], BF, "KAm")
                    for hl in range(2):
                        P.c("dve", (lambda e, hl=hl: e.tensor_scalar_mul(out=KTm[:, hl], in0=KT_[:], scalar1=pm[:, hl:hl + 1])), reads=["pm"], writes=["KTm%d" % hl])
                        P.c("dve", (lambda e, hl=hl: e.tensor_scalar_mul(out=KAm[:, hl], in0=KA[:], scalar1=pm[:, hl:hl + 1])), reads=["pm"], writes=["KAm%d" % hl])
                    P.c("dve", lambda e: e.tensor_tensor(out=SG[:].rearrange("p c (h v) -> p c h v", h=2), in0=SG[:].rearrange("p c (h v) -> p c h v", h=2),
                                                         in1=G[0:64, l, 704:768].unsqueeze(1).unsqueeze(1).to_broadcast([64, NCH, 2, 64]), op=ALU.mult),
                        reads=["gains"], writes=["SGg"])

                    def A_mm(c):
                        st, sl = c % 2, (c // 2) % 2
                        pa, ta_, at_ = psAB[st][sl], tA[st][sl], AT[st][sl]
                        kk_ = "%d_%d" % (st, sl)
                        for d in range(2):
                            for hl in range(2):
                                P.c("pe", (lambda e, d=d, hl=hl: e.matmul(
                                    pa[:, d, hl, :], lhsT=KTm[:, hl, d, c * 64:(c + 1) * 64], rhs=QT_[:, d, c * 64:(c + 1) * 64],
                                    start=True, stop=True)), reads=["KTm%d" % hl], writes=["psAB" + kk_])
                                P.c("pe", (lambda e, d=d, hl=hl: e.matmul(
                                    pa[:, 2 + d, hl, :], lhsT=KAm[:, hl, d, c * 64:(c + 1) * 64], rhs=QE[:, d, c * 64:(c + 1) * 64],
                                    start=True, stop=True)), reads=["KAm%d" % hl], writes=["psAB" + kk_])
                        P.c("dve", (lambda e: e.tensor_tensor(out=ta_[:], in0=pa[:], in1=mk4[:].unsqueeze(2).to_broadcast([64, 4, 2, 64]), op=ALU.mult)),
                            reads=["psAB" + kk_, "mk4"], writes=["tA" + kk_])
                        P.c("dve", (lambda e: e.tensor_tensor(out=at_[:], in0=ta_[:, 0:2], in1=ta_[:, 2:4], op=ALU.add)),
                            reads=["tA" + kk_], writes=["AT" + kk_])

                    def O_mm(c):
                        st, sl = c % 2, (c // 2) % 2
                        at_ = AT[st][sl]
                        kk_ = "%d_%d" % (st, sl)
                        psO = psOl[st][:]
                        ok_ = "psO%d" % st
                        for d in range(2):
                            P.c("pe", (lambda e, d=d: e.matmul(
                                psO, lhsT=QE[:, d, c * 64:(c + 1) * 64], rhs=SB[:, d, c, :], start=(d == 0), stop=False)),
                                reads=[], writes=[ok_])
                        for hl in range(2):
                            for d in range(2):
                                P.c("pe", (lambda e, d=d, hl=hl: e.matmul(
                                    psOl[st][:, hl * 64:(hl + 1) * 64], lhsT=at_[:, d, hl, :], rhs=IT[:, c, hl * 64:(hl + 1) * 64],
                                    start=False, stop=(hl == 1 and d == 1))), reads=["AT" + kk_], writes=[ok_])
                        tg = "ho%d" % st
                        hn_small(P, p5, tg, psO.rearrange("p (h v) -> p h v", h=2), None, on[st], c, [ok_], [tg + "on"])
                        P.c("dve", (lambda e: e.tensor_tensor(out=yb[st][:], in0=on[st][:], in1=SG[:, c, :].rearrange("p (h v) -> p h v", h=2), op=ALU.mult)),
                            reads=[tg + "on", "SGg"], writes=[tg + "yb"])
                        P.c("pe", (lambda e: e.transpose(ptyl[st][:], yb[st][:].rearrange("p h v -> p (h v)"), k.ident[0:64, 0:64])),
                            reads=[tg + "yb", "ident"], writes=["hpty%d" % st])
                        P.c("act", (lambda e: e.copy(out=k.YT[:, 6 + hp, c * 64:(c + 1) * 64], in_=ptyl[st][:])), reads=["hpty%d" % st], writes=["YT%d" % st])

                    def ostream(st):
                        cs = list(range(st, nch_out, 2))
                        for j, c in enumerate(cs):
                            A_mm(c)
                            if j > 0:
                                O_mm(cs[j - 1])
                        O_mm(cs[-1])

                    P.record_streams([lambda: ostream(0), lambda: ostream(1)])
                    P.flush()


def psO_dbg(k, pl, ps, key, c):
    t = pl.sb([64, 128], F32, "dbgO")
    k.P.c("dve", lambda e: e.tensor_copy(out=t[:], in_=ps[:]), reads=[key], writes=["dbgO%d" % c])
    return t[:]


_HN = {}


def hn_small(P, pl, tag, src, gain_bc, out, c, rk, wk):
    cache = pl.__dict__.setdefault("_hn", {})
    if tag not in cache:
        cache[tag] = (pl.sb([64, 2, 64], F32, tag + "sq"), pl.sb([64, 2], F32, tag + "ssh"))
    sq, ssh = cache[tag]
    P.c("act", lambda e: e.activation(out=sq[:], in_=src, func=AF.Square), reads=rk, writes=[tag + "sq"])
    P.c("dve", lambda e: e.tensor_reduce(out=ssh[:], in_=sq[:], axis=AX.X, op=ALU.add), reads=[tag + "sq"], writes=[tag + "ssh"])
    rsqrt(P, ssh[:], ssh[:], 1.0 / 64, EPS, [tag + "ssh"], [tag + "ssh"])
    if gain_bc is None:
        P.c("dve", lambda e: e.tensor_tensor(out=out[:], in0=src, in1=ssh[:].unsqueeze(2).to_broadcast([64, 2, 64]), op=ALU.mult),
            reads=rk + [tag + "ssh"], writes=wk)
        return
    P.c("dve", lambda e: e.tensor_tensor(out=sq[:], in0=src, in1=ssh[:].unsqueeze(2).to_broadcast([64, 2, 64]), op=ALU.mult),
        reads=rk + [tag + "ssh"], writes=[tag + "sq"])
    P.c("dve", lambda e: e.tensor_tensor(out=out[:], in0=sq[:], in1=gain_bc, op=ALU.mult), reads=[tag + "sq", "gains"], writes=wk)


def stage_merge(k, l, xsrc, csrc):
    nc, P, I = k.nc, k.P, k.I
    last = (l == DEPTH - 1)
    ntok = T if last else TA
    with ExitStack() as es:
        pl = Pool_(nc, es)
        Wbr = pl.sb([128, 8, D], BF, "wbr")
        Wo = pl.sb([128, 8, D], BF, "wo")
        Wg = [pl.sb([128, 8, 512], BF, "wg") for _ in range(2)]
        wv = I["w_in"][l].rearrange("(kc p) n -> p kc n", p=128)
        HTa = pl.sb([128, 8, ntok], BF, "HTa")
        hv = k.HTd.rearrange("(kc p) t -> p kc t", p=128)
        hkeys = []
        for t0 in range(0, ntok, 512):
            tw = min(512, ntok - t0)
            P.dma("sp", HTa[:, :, t0:t0 + tw], hv[:, :, t0:t0 + tw], writes=["HTa%d" % t0])
            hkeys.append("HTa%d" % t0)

        def load_wg(fc):
            wg = Wg[fc % 2]
            for n in range(4):
                c = NCOL + n * D + fc * 128
                P.dma("pool", wg[:, :, n * 128:(n + 1) * 128], wv[:, :, c:c + 128], writes=["wg%d_%d" % (fc % 2, n)])
        load_wg(0)
        P.dma("pool", Wbr[:], I["w_branch"][l].rearrange("n (j p) d -> p (n j) d", p=128), writes=["Wbr"])
        load_wg(1)
        P.dma("pool", Wo[:], I["w_out"][l].rearrange("(kc p) d -> p kc d", p=128), writes=["Wo"])
        grow = pl.sb([128, 2, D], F32, "grow")
        for s_ in range(2):
            src = bass.AP(k.MODROW.tensor, k.MODROW[l, 32 + s_, :].offset, [[0, 128], [256, 8], [1, 128]])
            P.dma("sp", grow[:, s_, :].rearrange("p (j f) -> p j f", f=128), src, writes=["grow"])
        ST = pl.sb([128, 8, ntok], BF, "ST")
        acc = pl.sb([128, 512], F32, "acc")
        sg = [pl.sb([128, 512], F32, "sg") for _ in range(2)]
        tmp = [pl.sb([128, 512], F32, "mt") for _ in range(2)]
        psg = [pl.ps([128, 512]) for _ in range(2)]
        psp = [pl.ps([128, 512]) for _ in range(2)]
        psy = [pl.ps([128, 512]) for _ in range(2)]
        xt = [pl.sb([128, D], F32, "mx") for _ in range(2)]
        cnt = 0
        nx = 0
        for fc in range(8):
            wg = Wg[fc % 2]
            if fc >= 2:
                load_wg(fc)
            for t0 in range(0, ntok, 512):
                tw = min(512, ntok - t0)
                hk = "HTa%d" % t0
                for n in range(4):
                    b = cnt % 2
                    cnt += 1
                    wk = "wg%d_%d" % (fc % 2, n)
                    for kc in range(8):
                        P.c("pe", (lambda e, b=b, kc=kc, n=n, wg=wg, t0=t0, tw=tw: e.matmul(
                            psg[b][:, 0:tw], lhsT=wg[:, kc, n * 128:(n + 1) * 128], rhs=HTa[:, kc, t0:t0 + tw],
                            start=(kc == 0), stop=(kc == 7))), reads=[wk, hk], writes=["psg%d" % b])
                    for j in range(2):
                        P.c("pe", (lambda e, b=b, j=j, n=n, fc=fc, t0=t0, tw=tw: e.matmul(
                            psp[b][:, 0:tw], lhsT=Wbr[:, n * 2 + j, fc * 128:(fc + 1) * 128], rhs=k.YT[:, n * 2 + j, t0:t0 + tw],
                            start=(j == 0), stop=(j == 1))), reads=["Wbr", "YT"], writes=["psp%d" % b])
                    P.c("act", (lambda e, b=b, tw=tw: e.activation(out=sg[b][:, 0:tw], in_=psg[b][:, 0:tw], func=AF.Sigmoid)),
                        reads=["psg%d" % b], writes=["sg%d" % b])
                    if n == 0:
                        P.c("dve", (lambda e, b=b, tw=tw: e.tensor_tensor(out=acc[:, 0:tw], in0=sg[b][:, 0:tw], in1=psp[b][:, 0:tw], op=ALU.mult)),
                            reads=["sg%d" % b, "psp%d" % b], writes=["acc"])
                    else:
                        P.c("dve", (lambda e, b=b, tw=tw: e.tensor_tensor(out=tmp[b][:, 0:tw], in0=sg[b][:, 0:tw], in1=psp[b][:, 0:tw], op=ALU.mult)),
                            reads=["sg%d" % b, "psp%d" % b], writes=["mt%d" % b])
                        if n < 3:
                            P.c("dve", (lambda e, b=b, tw=tw: e.tensor_tensor(out=acc[:, 0:tw], in0=acc[:, 0:tw], in1=tmp[b][:, 0:tw], op=ALU.add)),
                                reads=["acc", "mt%d" % b], writes=["acc"])
                        else:
                            P.c("dve", (lambda e, b=b, tw=tw, fc=fc, t0=t0: e.tensor_tensor(out=ST[:, fc, t0:t0 + tw], in0=acc[:, 0:tw], in1=tmp[b][:, 0:tw], op=ALU.add)),
                                reads=["acc", "mt%d" % b], writes=["ST%d_%d" % (fc, t0)])
        for tok0 in range(0, ntok, 128):
            t0 = (tok0 // 512) * 512
            s_ = 0 if tok0 < T else 1
            X = xt[nx % 2]
            xk = "mx%d" % (nx % 2)
            nx += 1
            src = xsrc[tok0:tok0 + 128, :] if tok0 < T else csrc[tok0 - T:tok0 - T + 128, :]
            P.dma("sp", X[:], src, writes=[xk])
            for hf in range(2):
                pb = psy[hf]
                for fc in range(8):
                    P.c("pe", (lambda e, pb=pb, fc=fc, tok0=tok0, hf=hf: e.matmul(
                        pb[:], lhsT=ST[:, fc, tok0:tok0 + 128], rhs=Wo[:, fc, hf * 512:(hf + 1) * 512],
                        start=(fc == 0), stop=(fc == 7))), reads=["ST%d_%d" % (fc, t0), "Wo"], writes=["psy%d" % hf])
                P.c("dve", (lambda e, pb=pb, hf=hf, s_=s_: e.tensor_tensor(out=tmp[hf][:], in0=pb[:], in1=grow[:, s_, hf * 512:(hf + 1) * 512], op=ALU.mult)),
                    reads=["psy%d" % hf, "grow"], writes=["mt%d" % hf])
                P.c("dve", (lambda e, X=X, hf=hf: e.tensor_tensor(out=X[:, hf * 512:(hf + 1) * 512], in0=X[:, hf * 512:(hf + 1) * 512], in1=tmp[hf][:], op=ALU.add)),
                    reads=["mt%d" % hf, xk], writes=[xk])
            P.dma("sp", k.XM[l][tok0:tok0 + 128, :], X[:], reads=[xk], writes=["XM"])
        P.flush()


def ffn_weights(k, pl, l):
    W1 = pl.sb([128, 8, 4 * D], BF, "w1")
    W2 = pl.sb([128, 32, D], BF, "w2")
    w1v = k.I["w_ff1"][l].rearrange("(kc p) n -> p kc n", p=128)
    w2v = k.I["w_ff2"][l].rearrange("(kc p) n -> p kc n", p=128)
    for q_ in range(8):
        k.P.dma("pool", W1[:, :, q_ * 512:(q_ + 1) * 512], w1v[:, :, q_ * 512:(q_ + 1) * 512], writes=["W1_%d" % q_], carry=True)
        k.P.dma("pool", W2[:, q_ * 4:(q_ + 1) * 4, :], w2v[:, q_ * 4:(q_ + 1) * 4, :], writes=["W2_%d" % q_], carry=True)
    return W1, W2


def stage_ffn(k, l, Wff=None):
    nc, P, I = k.nc, k.P, k.I
    last = (l == DEPTH - 1)
    ntok = T if last else TA
    with ExitStack() as es:
        pl = Pool_(nc, es)
        W1, W2 = Wff if Wff is not None else ffn_weights(k, pl, l)
        grow = pl.sb([128, 2, D], F32, "grow2")
        for s_ in range(2):
            src = bass.AP(k.MODROW.tensor, k.MODROW[l, 80 + s_, :].offset, [[0, 128], [256, 8], [1, 128]])
            P.dma("sp", grow[:, s_, :].rearrange("p (j f) -> p j f", f=128), src, writes=["grow2"])
        psh = [pl.ps([128, 256]) for _ in range(2)]
        pso = [pl.ps([128, 512]) for _ in range(4)]
        r1 = [pl.sb([128, 256], F32, "r1") for _ in range(2)]
        hid = [pl.sb([128, 256], BF, "hid") for _ in range(2)]
        xt = [pl.sb([128, D], F32, "fx") for _ in range(2)]
        tmp = [pl.sb([128, 512], F32, "ft") for _ in range(2)]
        nx = 0
        nh = 0
        for t0 in range(0, ntok, 256):
            def w2_mm(hc, b):
                for ti in range(2):
                    for hf in range(2):
                        P.c("pe", (lambda e, b=b, ti=ti, hf=hf, hc=hc: e.matmul(
                            pso[ti * 2 + hf][:], lhsT=hid[b][:, ti * 128:(ti + 1) * 128], rhs=W2[:, hc, hf * 512:(hf + 1) * 512],
                            start=(hc == 0), stop=(hc == 31))), reads=["hid%d" % b, "W2_%d" % (hc // 4)], writes=["pso%d" % (ti * 2 + hf)])
            prev = None
            for hc in range(32):
                b = nh % 2
                nh += 1
                for kc in range(8):
                    P.c("pe", (lambda e, b=b, kc=kc, hc=hc, t0=t0: e.matmul(
                        psh[b][:], lhsT=W1[:, kc, hc * 128:(hc + 1) * 128], rhs=k.HT[:, kc, t0:t0 + 256],
                        start=(kc == 0), stop=(kc == 7))), reads=["W1_%d" % (hc // 4), "HT"], writes=["psh%d" % b])
                if prev is not None:
                    w2_mm(*prev)
                P.c("act", (lambda e, b=b: e.activation(out=r1[b][:], in_=psh[b][:], func=AF.Relu)), reads=["psh%d" % b], writes=["r1%d" % b])
                P.c("dve", (lambda e, b=b: e.tensor_tensor(out=hid[b][:], in0=r1[b][:], in1=r1[b][:], op=ALU.mult)), reads=["r1%d" % b], writes=["hid%d" % b])
                prev = (hc, b)
            w2_mm(*prev)
            for ti in range(2):
                tok0 = t0 + ti * 128
                s_ = 0 if tok0 < T else 1
                X = xt[nx % 2]
                xk = "fx%d" % (nx % 2)
                nx += 1
                P.dma("sp", X[:], k.XM[l][tok0:tok0 + 128, :], writes=[xk])
                for hf in range(2):
                    pb = pso[ti * 2 + hf]
                    P.c("dve", (lambda e, pb=pb, hf=hf, s_=s_: e.tensor_tensor(out=tmp[hf][:], in0=pb[:], in1=grow[:, s_, hf * 512:(hf + 1) * 512], op=ALU.mult)),
                        reads=["pso%d" % (ti * 2 + hf), "grow2"], writes=["ft%d" % hf])
                    P.c("dve", (lambda e, X=X, hf=hf: e.tensor_tensor(out=X[:, hf * 512:(hf + 1) * 512], in0=X[:, hf * 512:(hf + 1) * 512], in1=tmp[hf][:], op=ALU.add)),
                        reads=["ft%d" % hf, xk], writes=[xk])
                dst = k.out[tok0:tok0 + 128, :] if last else k.XL[0][tok0:tok0 + 128, :]
                P.dma("sp", dst, X[:], reads=[xk], writes=["XLout"])
        P.flush()


def _rope_tab(pos):
    r, c = pos // 64, pos % 64
    inv = (10000.0 ** (-np.arange(8, dtype=np.float32) / 8)).astype(np.float32)
    ar = r[:, None].astype(np.float32) * inv
    ac = c[:, None].astype(np.float32) * inv
    return np.concatenate([np.cos(ar), np.cos(ac), np.sin(ar), np.sin(ac)], axis=1).astype(np.float32)


def _na_index(half):
    idx = -np.ones((16, 128, 6, 128), np.int64)
    kp = np.arange(128)
    q = np.arange(128)
    for p in range(16):
        t0 = 0 if p == 0 else (14 if p == 15 else p)
        nl = 6 if p in (0, 15) else 5
        for c in range(nl):
            kl = 2 * (t0 + c) + kp // 64
            kc = kp % 64
            kr = 32 * half - 4 + kl
            r = 32 * half + 2 * p + q // 64
            cq = q % 64
            rs = np.clip(r - 4, 0, 56)
            cs = np.clip(cq - 8, 0, 48)
            okr = (kr[:, None] >= rs[None, :]) & (kr[:, None] < rs[None, :] + 8) & (kr[:, None] >= 0) & (kr[:, None] < 64)
            okc = (kc[:, None] >= cs[None, :]) & (kc[:, None] < cs[None, :] + 16)
            dr = kr[:, None] - r[None, :] + 7
            dc = kc[:, None] - cq[None, :] + 15
            ok = okr & okc
            v = np.where(ok, dr * 31 + dc, -1)
            idx[p, :, c, :] = v
    return idx


def prep(inp, ncores=8):
    f = np.float32
    g = {}
    for n_ in ("ada_w", "w_in", "w_branch", "w_out", "w_ff1", "w_ff2", "pool_w", "mla_wq_b", "mla_wkv_b"):
        g[n_] = np.ascontiguousarray(inp[n_], f)
    g["ada_bT"] = np.ascontiguousarray(inp["ada_b"].reshape(DEPTH, 48, 128).transpose(2, 0, 1), f)
    nrm = np.stack([inp["norm1"], inp["norm2"]], axis=1)
    g["normT"] = np.ascontiguousarray(nrm.reshape(DEPTH, 2, 8, 128).transpose(3, 0, 1, 2), f)
    g["pool_scT"] = np.ascontiguousarray(inp["pool_scale"].reshape(DEPTH, 2, 128).transpose(2, 0, 1), f)
    gn = np.zeros((DEPTH, 800), f)
    for l in range(DEPTH):
        gn[l, 0:256] = inp["mla_q_norm"][l]
        gn[l, 256:384] = inp["mla_kv_norm"][l]
        gn[l, 384:480] = inp["mla_gq"][l]
        gn[l, 480:576] = inp["mla_gk"][l]
        gn[l, 576:640] = inp["na_gq"][l]
        gn[l, 640:704] = inp["na_gk"][l]
        gn[l, 704:768] = inp["hg_norm"][l]
    g["gains"] = np.ascontiguousarray(np.broadcast_to(gn[None], (128, DEPTH, 800)), f)
    g["ropek"] = _rope_tab(np.arange(L))
    lb = inp["hg_lb"].reshape(DEPTH + 1, 2, 2, 2, 64)
    g["hg_lbT"] = np.ascontiguousarray(lb.transpose(3, 4, 0, 1, 2).reshape(128, DEPTH + 1, 2, 2), f)
    def invcnt(t, n):
        out = np.zeros((4, len(t)), f)
        for gi, w in enumerate((2, 4, 8, 16)):
            lo = np.clip(t - w // 2, 0, n)
            hi = np.clip(t - w // 2 + w, 0, n)
            out[gi] = 1.0 / (hi - lo)
        return out
    def to_pt(a):
        return np.ascontiguousarray(np.repeat(a.reshape(2, 2, 1, -1), 64, axis=2).reshape(2, 128, -1).transpose(1, 0, 2), f)
    g["pinvc"] = to_pt(invcnt(np.arange(TC), TC))
    rpb_ext = [np.concatenate([inp["na_rpb"][l].reshape(4, -1), np.full((4, 1), NEG, f)], axis=1) for l in range(DEPTH)]
    maps = []
    for core in range(ncores):
        b, half = core // 2, core % 2
        m = dict(g)
        m["x"] = np.ascontiguousarray(inp["x"][b, half * T:(half + 1) * T], f)
        m["ctx"] = np.ascontiguousarray(inp["ctx"][b], f)
        cc = np.stack([inp["c"][b], inp["c_ctx"]], axis=-1)
        m["cT"] = np.ascontiguousarray(cc.reshape(8, 128, 2).transpose(1, 0, 2), f)
        m["pinv"] = to_pt(invcnt(np.arange(half * T, (half + 1) * T), L))
        m["hm"] = np.ascontiguousarray(np.broadcast_to(np.array([half, 1 - half], f), (128, 2)))
        m["ropeq"] = np.ascontiguousarray(g["ropek"][half * T:(half + 1) * T])
        idx = _na_index(half)
        nab = np.stack([rpb_ext[l][:, idx] for l in range(DEPTH)])
        m["nab"] = np.ascontiguousarray(nab.transpose(0, 2, 3, 4, 1, 5), f)
        maps.append(m)
    return maps


def kernel(**inputs):
    inp = {k_: np.asarray(v) for k_, v in inputs.items()}
    k = build(ncores=8)
    maps = prep(inp, 8)
    names = list(k.I.keys())
    maps = [{n: m[n] for n in names} for m in maps]
    res = run_bass_kernel_spmd(k.nc, maps, core_ids=list(range(8)))
    out = np.zeros((4, L, D), np.float32)
    for core in range(8):
        b, half = core // 2, core % 2
        out[b, half * T:(half + 1) * T] = np.asarray(res.results[core]["out"], np.float32)
    return out
```

```python
import numpy as np
from contextlib import ExitStack
import concourse.bass as bass
import concourse.mybir as mb
from concourse.bass_utils import run_bass_kernel_spmd

F32 = mb.dt.float32
BF = mb.dt.bfloat16
AF = mb.ActivationFunctionType
ALU = mb.AluOpType
AX = mb.AxisListType

D = 1024
T = 2048
TC = 256
TA = T + TC
NT = TA // 128
L = 4096
DEPTH = 2
NCOL = 2720
EPS = 1e-6
NEG = -30000.0


class _Op:
    __slots__ = ("eng", "fn", "kind", "deps", "needs_inc", "token", "waits", "ring_wait", "carry")

    def __init__(self, eng, fn, kind):
        self.eng = eng
        self.fn = fn
        self.kind = kind
        self.deps = []
        self.needs_inc = False
        self.token = None
        self.waits = []
        self.ring_wait = None
        self.carry = False


class Prog:
    ENGS = ("pe", "act", "dve", "pool", "sp")
    NRING = 8

    def __init__(self, nc, es):
        self.nc = nc
        self.h = {"pe": nc.tensor, "act": nc.scalar, "dve": nc.vector, "pool": nc.gpsimd, "sp": nc.sync}
        self.sem = {e: es.enter_context(nc.semaphore("s_" + e)) for e in self.ENGS}
        self.cnt = {e: 0 for e in self.ENGS}
        self.ring = {q: [es.enter_context(nc.semaphore("r_%s%d" % (q, i))) for i in range(self.NRING)]
                     for q in ("sp", "pool", "act")}
        self.ringcnt = {q: [0] * self.NRING for q in self.ring}
        self.ringlast = {q: [None] * self.NRING for q in self.ring}
        self.ndma = {q: 0 for q in self.ring}
        self.ccsem = es.enter_context(nc.semaphore("s_cc"))
        self.cccnt = 0
        self.waited = {e: {} for e in self.ENGS}
        self.ops = []
        self.lastw = {}
        self.readers = {}
        self.pending_dma = []
        self.nstage = 0

    def _record(self, op, reads, writes):
        deps = []
        for k in reads:
            for w in self.lastw.get(k, ()):
                if w.eng == op.eng == "pe" and w.kind == "c" and op.kind == "c":
                    continue
                deps.append(w)
        for k in writes:
            for w in self.lastw.get(k, ()):
                if w.kind == "c" and op.kind == "c" and w.eng == op.eng == "pe":
                    continue
                deps.append(w)
            for r in self.readers.get(k, ()):
                if r.kind == "c" and op.kind == "c" and r.eng == op.eng == "pe":
                    continue
                deps.append(r)
        seen = set()
        for d in deps:
            if id(d) not in seen and d is not op:
                seen.add(id(d))
                d.needs_inc = True
                op.deps.append(d)
        for k in reads:
            self.readers.setdefault(k, []).append(op)
        for k in writes:
            self.lastw[k] = [op]
            self.readers[k] = []
        self.ops.append(op)
        return op

    def c(self, eng, fn, reads=(), writes=()):
        return self._record(_Op(eng, fn, "c"), reads, writes)

    def dma(self, q, out, in_, reads=(), writes=(), carry=False):
        op = _Op(q, lambda e: e.dma_start(out=out, in_=in_), "d")
        op.needs_inc = True
        op.carry = carry
        if not carry:
            self.pending_dma.append(op)
        return self._record(op, reads, writes)

    def cc(self, fn, reads=(), writes=()):
        op = _Op("pool", fn, "cc")
        op.needs_inc = True
        self.pending_dma.append(op)
        return self._record(op, reads, writes)

    def record_streams(self, fns):
        lists = []
        for fn in fns:
            saved, self.ops = self.ops, []
            fn()
            lists.append(self.ops)
            self.ops = saved
        idx = [0] * len(lists)
        n = max(len(x) for x in lists) if lists else 0
        for t in range(n):
            for i, lst in enumerate(lists):
                upto = ((t + 1) * len(lst) + n - 1) // n
                while idx[i] < min(upto, len(lst)):
                    self.ops.append(lst[idx[i]])
                    idx[i] += 1
        for i, lst in enumerate(lists):
            self.ops.extend(lst[idx[i]:])

    def flush(self):
        per = {e: [] for e in self.ENGS}
        for op in self.ops:
            e = op.eng
            waits = []
            for d in op.deps:
                s, v = d.token
                key = id(s)
                if self.waited[e].get(key, 0) < v:
                    self.waited[e][key] = v
                    waits.append((s, v))
            if op.kind == "d":
                r = self.ndma[e] % self.NRING
                self.ndma[e] += 1
                prev = self.ringlast[e][r]
                if prev is not None:
                    s, v = prev
                    if self.waited[e].get(id(s), 0) < v:
                        self.waited[e][id(s)] = v
                        waits.append((s, v))
                self.ringcnt[e][r] += 16
                op.token = (self.ring[e][r], self.ringcnt[e][r])
                self.ringlast[e][r] = op.token
            elif op.kind == "cc":
                self.cccnt += 1
                op.token = (self.ccsem, self.cccnt)
            elif op.needs_inc:
                self.cnt[e] += 1
                op.token = (self.sem[e], self.cnt[e])
            op.waits = waits
            per[e].append(op)
        tail = {e: [] for e in self.ENGS}
        for op in self.pending_dma:
            s, v = op.token
            e = op.eng
            if self.waited[e].get(id(s), 0) < v:
                self.waited[e][id(s)] = v
                tail[e].append((s, v))
        sem, ringsem, ccsem = self.sem, self.ring, self.ccsem

        def body(e):
            def run(h):
                for op in per[e]:
                    for s, v in op.waits:
                        h.wait_ge(s, v)
                    ins = op.fn(h)
                    if op.kind == "d":
                        ins.then_inc(op.token[0], 16)
                    elif op.kind == "cc":
                        ins.then_inc(op.token[0], 1)
                    elif op.needs_inc:
                        ins.then_inc(op.token[0], 1)
                for s, v in tail[e]:
                    h.wait_ge(s, v)
            return run

        with self.nc.Block() as blk:
            blk.tensor(body("pe"))
            blk.scalar(body("act"))
            blk.vector(body("dve"))
            blk.gpsimd(body("pool"))
            blk.sync(body("sp"))
        self.ops = []
        self.lastw = {kk: ws for kk, ws in self.lastw.items() if all(w.carry for w in ws)}
        self.readers = {}
        self.pending_dma = []
        self.nstage += 1


def rsqrt(P, out, in_, mul, add, rk, wk):
    P.c("act", lambda e: e.activation(out=out, in_=in_, func=AF.Ln, scale=mul, bias=add), reads=rk, writes=wk)
    P.c("act", lambda e: e.activation(out=out, in_=out, func=AF.Exp, scale=-0.5), reads=wk, writes=wk)


class Pool_:
    N = [0]

    def __init__(self, nc, es):
        self.nc, self.es = nc, es

    def sb(self, shape, dt=F32, name=None):
        Pool_.N[0] += 1
        return self.es.enter_context(self.nc.sbuf_tensor((name or "t") + "_%d" % Pool_.N[0], list(shape), dt))

    def ps(self, shape, dt=F32, name=None):
        Pool_.N[0] += 1
        return self.es.enter_context(self.nc.psum_tensor((name or "p") + "_%d" % Pool_.N[0], list(shape), dt))

    def get(self, key, shape, dt=F32, psum=False):
        c = self.__dict__.setdefault("_cache", {})
        if key not in c:
            c[key] = self.ps(shape, dt, key) if psum else self.sb(shape, dt, key)
        return c[key]


class K:
    pass


def _dram_in(nc, name, shape, dt=F32):
    return nc.dram_tensor(name, list(shape), dt, kind="ExternalInput").ap()


def build(dbg=(), upto=None, ncores=8, skip=()):
    nc = bass.Bass("TRN2", target_bir_lowering=False)
    es = ExitStack()
    k = K()
    k.nc, k.es = nc, es
    k.P = Prog(nc, es)
    k.dbg = set(dbg)
    k.upto = upto
    k.groups = [[2 * i, 2 * i + 1] for i in range(ncores // 2)]
    shapes = dict([
        ("x", (T, D)), ("ctx", (TC, D)), ("cT", (128, 8, 2)),
        ("ada_w", (DEPTH, D, 6 * D)), ("ada_bT", (128, DEPTH, 48)), ("normT", (128, DEPTH, 2, 8)),
        ("w_in", (DEPTH, D, 6816)), ("w_branch", (DEPTH, 4, 256, D)), ("w_out", (DEPTH, D, D)),
        ("w_ff1", (DEPTH, D, 4 * D)), ("w_ff2", (DEPTH, 4 * D, D)),
        ("pool_w", (DEPTH, 4, 64, 64)), ("pool_scT", (128, DEPTH, 2)), ("pinv", (128, 2, T)), ("pinvc", (128, 2, TC)),
        ("hm", (128, 2)),
        ("mla_wq_b", (DEPTH, 256, 384)), ("mla_wkv_b", (DEPTH, 128, 512)),
        ("gains", (128, DEPTH, 800)),
        ("ropeq", (T, 32)), ("ropek", (L, 32)),
        ("nab", (DEPTH, 16, 128, 6, 4, 128)),
        ("hg_lbT", (128, 3, 2, 2)),
    ])

    class LazyIn(dict):
        def __missing__(self, name):
            v = _dram_in(nc, name, shapes[name])
            self[name] = v
            return v
    I = LazyIn()
    k.I = I
    k.out = nc.dram_tensor("out", [T, D], F32, kind="ExternalOutput").ap()

    def scratch(name, shape, dt=F32):
        kind = "ExternalOutput" if name in k.dbg else "Internal"
        return nc.dram_tensor(name, list(shape), dt, kind=kind).ap()
    k.scratch = scratch
    pp = Pool_(nc, es)
    k.pp = pp
    k.modT = pp.sb([128, DEPTH, 48, 2], F32, "modT")
    k.AM = pp.sb([128, DEPTH, 2, 8, 2], F32, "AM")
    k.ident = pp.sb([128, 128], BF, "ident")
    k.identf = pp.sb([128, 128], F32, "identf")
    k.gains = pp.sb([128, DEPTH, 800], F32, "gains")
    k.hm = pp.sb([128, 2], F32, "hm")
    k.HTd = nc.dram_tensor("HTd", [D, TA], BF, kind="Internal").ap()
    k.U = scratch("U", (TA, 1696))
    k.UT = scratch("UT", (1024, TA))
    k.MODROW = scratch("MODROW", (DEPTH, 96, 128))
    k.XM = [scratch("XM%d" % l, (TA, D)) for l in range(DEPTH)]
    k.XL = [scratch("XL0", (TA, D)), None]
    k.YTd = scratch("YTd", (1024, TA), F32)
    stage_consts(k)
    stage_mods(k, 0)
    xsrc, csrc = I["x"], I["ctx"]
    for l in range(DEPTH):
        last = (l == DEPTH - 1)
        with ExitStack() as hs:
            hp_ = Pool_(nc, hs)
            k.HT = hp_.sb([128, 8, TA], BF, "HT")
            Wp = proj_weights(k, hp_, l)
            stage_norm(k, l, 0, xsrc, csrc, spill=True)
            stage_proj(k, l, Wp)
        if upto == "proj":
            break
        if upto == "ex1":
            break
        with ExitStack() as les:
            lp = Pool_(nc, les)
            k.YT = lp.sb([128, 8, TA], BF, "YT")
            stage_pool(k, l)
            if upto == "pool":
                stage_dump_yt(k)
                break
            if "mla" not in skip:
                stage_attn(k, l, "mla")
            if "na" not in skip:
                stage_attn(k, l, "na")
            if upto == "na":
                stage_dump_yt(k)
                break
            stage_hgrn(k, l, upto)
            if "YTd" in k.dbg:
                stage_dump_yt(k)
            if upto in ("hgrn", "hgrn1", "hgrn2"):
                break
            stage_merge(k, l, xsrc, csrc)
        if upto == "merge":
            break
        with ExitStack() as hs:
            hp_ = Pool_(nc, hs)
            k.HT = hp_.sb([128, 8, TA], BF, "HT")
            Wff = ffn_weights(k, hp_, l)
            stage_norm(k, l, 1, k.XM[l][0:T, :], k.XM[l][T:TA, :], nstream=2, ntile=(16 if last else NT))
            stage_ffn(k, l, Wff)
        xsrc, csrc = k.XL[0][0:T, :], k.XL[0][T:TA, :]
        if upto == "l0":
            break
    return k


def dump(k, name, ap, shape, reads=()):
    if name not in k.dbg:
        return
    t = k.nc.dram_tensor(name, list(shape), F32, kind="ExternalOutput").ap()
    k.P.dma("pool", t, ap, reads=list(reads))


def stage_dump_yt(k):
    P = k.P
    for j in range(8):
        P.dma("pool", k.YTd[j * 128:(j + 1) * 128, :], k.YT[:, j, :])
    P.flush()


def stage_consts(k):
    nc, P = k.nc, k.P
    P.c("pool", lambda e: e.memset(k.identf[:], 0.0), writes=["identf"])
    P.c("pool", lambda e: e.affine_select(out=k.identf[:], in_=k.identf[:], compare_op=ALU.not_equal, fill=1.0,
                                         base=0, pattern=[[-1, 128]], channel_multiplier=1),
        reads=["identf"], writes=["identf"])
    P.c("dve", lambda e: e.tensor_copy(out=k.ident[:], in_=k.identf[:]), reads=["identf"], writes=["ident"])
    P.dma("sp", k.gains[:], k.I["gains"], writes=["gains"])
    P.dma("sp", k.hm[:], k.I["hm"], writes=["hm"])
    P.flush()


def mods_ops(k, pl, l):
    nc, P, I = k.nc, k.P, k.I
    m = "m%d" % l
    cT = pl.sb([128, 8, 2])
    e1 = pl.sb([128, 8, 2])
    sT = pl.sb([128, 8, 2], BF)
    abT = pl.sb([128, DEPTH, 48])
    nT = pl.sb([128, DEPTH, 2, 8])
    wbuf = [pl.sb([128, 8, 512], BF, "adaw") for _ in range(4)]
    ps = pl.ps([128, 48, 2])
    psr = pl.ps([128, 128])
    rowsb = pl.sb([96, 128])
    P.dma("sp", cT[:], I["cT"], writes=[m + "cT"])
    P.dma("sp", abT[:], I["ada_bT"], writes=[m + "abT"])
    P.dma("sp", nT[:], I["normT"], writes=[m + "nT"])
    P.c("act", lambda e: e.activation(out=e1[:], in_=cT[:], func=AF.Exp, scale=-1.0), reads=[m + "cT"], writes=[m + "e1"])
    P.c("dve", lambda e: e.tensor_scalar_add(out=e1[:], in0=e1[:], scalar1=1.0), reads=[m + "e1"], writes=[m + "e1"])
    P.c("dve", lambda e: e.reciprocal(out=e1[:], in_=e1[:]), reads=[m + "e1"], writes=[m + "e1"])
    P.c("dve", lambda e: e.tensor_tensor(out=sT[:], in0=e1[:], in1=cT[:], op=ALU.mult), reads=[m + "e1", m + "cT"], writes=[m + "sT"])
    wv = I["ada_w"][l].rearrange("(kc p) n -> p kc n", p=128)
    for nb in range(12):
        wb = wbuf[nb % 4]
        key = m + "adaw%d" % (nb % 4)
        P.dma("pool", wb[:], wv[:, :, nb * 512:(nb + 1) * 512], writes=[key])
        for jj in range(4):
            j = nb * 4 + jj
            for kc in range(8):
                P.c("pe", (lambda e, wb=wb, jj=jj, kc=kc, j=j: e.matmul(
                    ps[:, j, :], lhsT=wb[:, kc, jj * 128:(jj + 1) * 128], rhs=sT[:, kc, :],
                    start=(kc == 0), stop=(kc == 7))), reads=[key, m + "sT"], writes=[m + "psmod"])
    P.c("dve", (lambda e: e.tensor_tensor(
        out=k.modT[:, l], in0=ps[:], in1=abT[:, l].unsqueeze(2).to_broadcast([128, 48, 2]), op=ALU.add)),
        reads=[m + "psmod", m + "abT"], writes=[m + "modT"])
    P.c("pe", (lambda e: e.transpose(psr[0:96, :], k.modT[:, l].rearrange("p j s -> p (j s)"), k.identf[:])),
        reads=[m + "modT", "identf"], writes=[m + "psr"])
    P.c("dve", (lambda e: e.tensor_copy(out=rowsb[:], in_=psr[0:96, :])), reads=[m + "psr"], writes=[m + "rowsb"])
    P.dma("sp", k.MODROW[l], rowsb[:], reads=[m + "rowsb"], writes=[m + "MODROW"])
    for w in range(2):
        sc = k.modT[:, l, (8 + 24 * w):(16 + 24 * w), :]
        P.c("dve", (lambda e, w=w, sc=sc: e.tensor_scalar_add(out=k.AM[:, l, w], in0=sc, scalar1=1.0)),
            reads=[m + "modT"], writes=[m + "AM%d" % w])
        P.c("dve", (lambda e, w=w: e.tensor_tensor(
            out=k.AM[:, l, w], in0=k.AM[:, l, w], in1=nT[:, l, w].unsqueeze(2).to_broadcast([128, 8, 2]),
            op=ALU.mult)), reads=[m + "AM%d" % w, m + "nT"], writes=[m + "AM%d" % w])


def stage_mods(k, l):
    with ExitStack() as es:
        mods_ops(k, Pool_(k.nc, es), l)
        k.P.flush()


def stage_norm(k, l, w, xsrc, csrc, spill=False, nstream=3, ntile=NT):
    nc, P = k.nc, k.P
    with ExitStack() as es:
        pl = Pool_(nc, es)
        ss = pl.sb([128, NT])
        rstd = pl.sb([128, NT])
        bufs = [(pl.sb([128, D]), pl.sb([128, D], BF), pl.sb([128, D], BF), pl.ps([128, 8, 128], BF), pl.sb([128, 8, 128])) for _ in range(nstream)]

        def tile(i):
            s_ = 0 if i < 16 else 1
            src = xsrc[i * 128:(i + 1) * 128, :] if i < 16 else csrc[(i - 16) * 128:(i - 15) * 128, :]
            b = i % nstream
            X, XN, junk, PT, TM = bufs[b]
            P.dma("sp", X[:], src, writes=["xt%d" % b])
            P.c("act", (lambda e: e.activation(out=junk[:], in_=X[:], func=AF.Square, accum_out=ss[:, i:i + 1])),
                reads=["xt%d" % b], writes=["junk%d" % b, "ss%d" % i])
            rsqrt(P, rstd[:, i:i + 1], ss[:, i:i + 1], 1.0 / D, EPS, ["ss%d" % i], ["rs%d" % i])
            P.c("act", (lambda e: e.activation(out=XN[:], in_=X[:], func=AF.Copy, scale=rstd[:, i:i + 1])),
                reads=["xt%d" % b, "rs%d" % i], writes=["xn%d" % b])
            for kc in range(8):
                P.c("pe", (lambda e, kc=kc: e.transpose(PT[:, kc, :], XN[:, kc * 128:(kc + 1) * 128], k.ident[:])),
                    reads=["xn%d" % b, "ident"], writes=["pt%d" % b])
            A = k.AM[:, l, w, :, s_:s_ + 1].to_broadcast([128, 8, 128])
            Bv = k.modT[:, l, 24 * w:24 * w + 8, s_:s_ + 1].to_broadcast([128, 8, 128])
            P.c("dve", (lambda e: e.tensor_tensor(out=TM[:], in0=PT[:], in1=A, op=ALU.mult)),
                reads=["pt%d" % b], writes=["tm%d" % b])
            P.c("dve", (lambda e: e.tensor_tensor(out=k.HT[:, :, i * 128:(i + 1) * 128], in0=TM[:], in1=Bv, op=ALU.add)),
                reads=["tm%d" % b], writes=["HT%d" % i])

        P.record_streams([(lambda st=st: [tile(i) for i in range(st, ntile, nstream)]) for st in range(nstream)])
        if spill:
            hk = ["HT%d" % i for i in range(ntile)]
            for kc in range(8):
                P.dma("sp", k.HTd[kc * 128:(kc + 1) * 128, :], k.HT[:, kc, :], reads=hk, writes=["HTd"])
        P.flush()


def proj_weights(k, pl, l):
    W = pl.sb([128, 8, NCOL], BF, "win")
    wv = k.I["w_in"][l].rearrange("(kc p) n -> p kc n", p=128)
    for c0 in range(0, NCOL, 680):
        k.P.dma("pool", W[:, :, c0:c0 + 680], wv[:, :, c0:c0 + 680], writes=["W%d" % c0], carry=True)
    return W


def stage_proj(k, l, W=None):
    nc, P, I = k.nc, k.P, k.I
    tm_cols = [(0, 672), (1184, 1440), (1696, 2208), (2464, 2720)]
    fm_cols = [(1440, 1696), (672, 1184), (2208, 2464)]
    with ExitStack() as es:
        pl = Pool_(nc, es)
        if W is None:
            W = proj_weights(k, pl, l)
        wkeys = ["W%d" % c0 for c0 in range(0, NCOL, 680)]
        ps = [pl.ps([128, 512]) for _ in range(4)]
        st = [pl.sb([128, 1696]) for _ in range(2)]
        n = 0
        for i in range(NT):
            S = st[i % 2]
            uc = 0
            for (a, b) in tm_cols:
                c = a
                while c < b:
                    w_ = min(512, b - c)
                    pb = ps[n % 4]
                    pk = "ps%d" % (n % 4)
                    n += 1
                    for kc in range(8):
                        P.c("pe", (lambda e, pb=pb, kc=kc, c=c, w_=w_, i=i: e.matmul(
                            pb[:, 0:w_], lhsT=k.HT[:, kc, i * 128:(i + 1) * 128], rhs=W[:, kc, c:c + w_],
                            start=(kc == 0), stop=(kc == 7))), reads=["HT"] + wkeys, writes=[pk])
                    eng = "act" if n % 2 else "dve"
                    if eng == "act":
                        P.c("act", (lambda e, pb=pb, S=S, uc=uc, w_=w_: e.copy(out=S[:, uc:uc + w_], in_=pb[:, 0:w_])),
                            reads=[pk], writes=["st%d_%d" % (i % 2, uc)])
                    else:
                        P.c("dve", (lambda e, pb=pb, S=S, uc=uc, w_=w_: e.tensor_copy(out=S[:, uc:uc + w_], in_=pb[:, 0:w_])),
                            reads=[pk], writes=["st%d_%d" % (i % 2, uc)])
                    uc += w_
                    c += w_
            skeys = [kk for kk in list(P.lastw.keys()) if isinstance(kk, str) and kk.startswith("st%d_" % (i % 2))]
            P.dma("sp", k.U[i * 128:(i + 1) * 128, :], S[:], reads=skeys, writes=["U%d" % i])
        st2 = [pl.sb([128, 512]) for _ in range(2)]
        m = 0
        exchange_alloc(k)
        ukeys = ["U%d" % i for i in range(NT)]
        P.dma("sp", k.EX1in, k.U[0:T, 0:160], reads=ukeys, writes=["EX1in"])
        P.dma("sp", k.EXNin[0:256, :], k.U[0:256, 160:672], reads=ukeys, writes=["EXNin"])
        P.dma("sp", k.EXNin[256:512, :], k.U[T - 256:T, 160:672], reads=ukeys, writes=["EXNin"])
        P.cc(lambda e: e.collective_compute("AllGather", ALU.bypass, replica_groups=k.groups, ins=[k.EXNin], outs=[k.GN]),
             reads=["EXNin"], writes=["GN"])
        P.cc(lambda e: e.collective_compute("AllGather", ALU.bypass, replica_groups=k.groups, ins=[k.EX1in], outs=[k.G1]),
             reads=["EX1in"], writes=["G1"])
        r0 = 0
        for (a, b) in fm_cols:
            for c in range(a, b, 128):
                for t0 in range(0, TA, 512):
                    tw = min(512, TA - t0)
                    pb = ps[n % 4]
                    pk = "ps%d" % (n % 4)
                    n += 1
                    S2 = st2[m % 2]
                    sk = "st2_%d" % (m % 2)
                    m += 1
                    for kc in range(8):
                        P.c("pe", (lambda e, pb=pb, kc=kc, c=c, t0=t0, tw=tw: e.matmul(
                            pb[:, 0:tw], lhsT=W[:, kc, c:c + 128], rhs=k.HT[:, kc, t0:t0 + tw],
                            start=(kc == 0), stop=(kc == 7))), reads=["HT"] + wkeys, writes=[pk])
                    if m % 2:
                        P.c("act", (lambda e, pb=pb, S2=S2, tw=tw: e.copy(out=S2[:, 0:tw], in_=pb[:, 0:tw])), reads=[pk], writes=[sk])
                    else:
                        P.c("dve", (lambda e, pb=pb, S2=S2, tw=tw: e.tensor_copy(out=S2[:, 0:tw], in_=pb[:, 0:tw])), reads=[pk], writes=[sk])
                    P.dma("sp", k.UT[r0:r0 + 128, t0:t0 + tw], S2[:, 0:tw], reads=[sk], writes=["UT%d_%d" % (r0, t0)])
                r0 += 128
                if r0 == 256:
                    pk_ = ["UT%d_%d" % (rr, tt) for rr in (0, 128) for tt in range(0, TA, 512)]
                    P.dma("sp", k.EXPin[:, 0:8], k.UT[0:256, 0:8], reads=pk_, writes=["EXPin"])
                    P.dma("sp", k.EXPin[:, 8:16], k.UT[0:256, T - 8:T], reads=pk_, writes=["EXPin"])
                    P.cc(lambda e: e.collective_compute("AllGather", ALU.bypass, replica_groups=k.groups, ins=[k.EXPin], outs=[k.GP]),
                         reads=["EXPin"], writes=["GP"])
        P.flush()


def exchange_alloc(k):
    nc = k.nc
    if not hasattr(k, "EX1in"):
        k.EX1in = nc.dram_tensor("EX1in", [T, 160], F32, kind="Internal").ap()
        k.G1 = nc.dram_tensor("G1", [2 * T, 160], F32, kind="Internal").ap()
        k.EXNin = nc.dram_tensor("EXNin", [512, 512], F32, kind="Internal").ap()
        k.GN = nc.dram_tensor("GN", [1024, 512], F32, kind="Internal").ap()
        k.EXPin = nc.dram_tensor("EXPin", [256, 16], F32, kind="Internal").ap()
        k.GP = nc.dram_tensor("GP", [512, 16], F32, kind="Internal").ap()


def stage_exchange1(k, l):
    nc, P = k.nc, k.P
    if not hasattr(k, "EX1in"):
        k.EX1in = nc.dram_tensor("EX1in", [T, 160], F32, kind="Internal").ap()
        k.G1 = nc.dram_tensor("G1", [2 * T, 160], F32, kind="Internal").ap()
        k.EXNin = nc.dram_tensor("EXNin", [512, 512], F32, kind="Internal").ap()
        k.GN = nc.dram_tensor("GN", [1024, 512], F32, kind="Internal").ap()
        k.EXPin = nc.dram_tensor("EXPin", [256, 16], F32, kind="Internal").ap()
        k.GP = nc.dram_tensor("GP", [512, 16], F32, kind="Internal").ap()
    P.dma("sp", k.EX1in, k.U[0:T, 0:160], writes=["EX1in"])
    P.dma("sp", k.EXNin[0:256, :], k.U[0:256, 160:672], writes=["EXNin"])
    P.dma("sp", k.EXNin[256:512, :], k.U[T - 256:T, 160:672], writes=["EXNin"])
    P.cc(lambda e: e.collective_compute("AllGather", ALU.bypass, replica_groups=k.groups, ins=[k.EXNin], outs=[k.GN]),
         reads=["EXNin"], writes=["GN"])
    P.dma("sp", k.EXPin[:, 0:8], k.UT[0:256, 0:8], writes=["EXPin"])
    P.dma("sp", k.EXPin[:, 8:16], k.UT[0:256, T - 8:T], writes=["EXPin"])
    P.cc(lambda e: e.collective_compute("AllGather", ALU.bypass, replica_groups=k.groups, ins=[k.EX1in], outs=[k.G1]),
         reads=["EX1in"], writes=["G1"])
    P.cc(lambda e: e.collective_compute("AllGather", ALU.bypass, replica_groups=k.groups, ins=[k.EXPin], outs=[k.GP]),
         reads=["EXPin"], writes=["GP"])
    P.flush()


def stage_pool(k, l):
    nc, P, I = k.nc, k.P, k.I
    with ExitStack() as es:
        pl = Pool_(nc, es)
        Wb = pl.sb([128, 2, 128])
        Wbb = pl.sb([128, 2, 128], BF)
        psc = pl.sb([128, DEPTH, 2])
        P.c("pool", lambda e: e.memset(Wb[:], 0.0), writes=["Wb"])
        for g in range(4):
            t_, o_ = g // 2, (g % 2) * 64
            P.dma("sp", Wb[o_:o_ + 64, t_, o_:o_ + 64], I["pool_w"][l, g], reads=[], writes=["Wb"])
        P.c("dve", lambda e: e.tensor_copy(out=Wbb[:], in_=Wb[:]), reads=["Wb"], writes=["Wbb"])
        P.dma("sp", psc[:], I["pool_scT"], writes=["psc"])
        ps = [pl.ps([128, 512]) for _ in range(2)]
        npsm = [0]
        for (n, c0, invsrc, halo) in ((T, 0, I["pinv"], True), (TC, T, I["pinvc"], False)):
            if n == TC and l == DEPTH - 1:
                continue
            N = n + 16
            sfx = "L" if halo else "C"
            B = pl.sb([128, 2, N], F32, "pB")
            A1 = pl.sb([128, 2, N], F32, "pA1")
            A2 = pl.sb([128, 2, N], F32, "pA2")
            R = pl.sb([128, 2, n], F32, "pR")
            Rb = pl.sb([128, 2, n], BF, "pRb")
            inv = pl.sb([128, 2, n], F32, "pinv")
            kB = "pB" + sfx
            P.c("pool", (lambda e, B=B: e.memset(B[:], 0.0)), writes=[kB])
            for t_ in range(2):
                P.dma("sp", B[:, t_, 8:8 + n], k.UT[t_ * 128:(t_ + 1) * 128, c0:c0 + n], writes=[kB])
                if halo:
                    P.dma("sp", B[:, t_, 0:8], k.GP[t_ * 128:(t_ + 1) * 128, 8:16], writes=[kB])
                    P.dma("sp", B[:, t_, 8 + n:16 + n], k.GP[256 + t_ * 128:256 + (t_ + 1) * 128, 0:8], writes=[kB])
            P.dma("sp", inv[:], invsrc, writes=["inv" + sfx])
            if halo:
                P.c("dve", (lambda e, B=B: e.tensor_scalar_mul(out=B[:, :, 0:8], in0=B[:, :, 0:8], scalar1=k.hm[:, 0:1])),
                    reads=[kB, "hm"], writes=[kB])
                P.c("dve", (lambda e, B=B, n=n: e.tensor_scalar_mul(out=B[:, :, 8 + n:16 + n], in0=B[:, :, 8 + n:16 + n], scalar1=k.hm[:, 1:2])),
                    reads=[kB], writes=[kB])
            kk = [kB, "A1" + sfx, "A2" + sfx, "R" + sfx]
            P.c("dve", (lambda e, B=B, A1=A1, N=N: e.tensor_tensor(out=A1[:, :, 1:N], in0=B[:, :, 0:N - 1], in1=B[:, :, 1:N], op=ALU.add)),
                reads=[kB], writes=[kk[1]])
            P.c("dve", (lambda e, A1=A1, A2=A2, N=N: e.tensor_tensor(out=A2[:, :, 2:N - 1], in0=A1[:, :, 1:N - 2], in1=A1[:, :, 3:N], op=ALU.add)),
                reads=[kk[1]], writes=[kk[2]])
            P.c("dve", (lambda e, A1=A1, R=R, n=n: e.tensor_copy(out=R[0:64, 0, :], in_=A1[0:64, 0, 8:8 + n])), reads=[kk[1]], writes=[kk[3] + "a"])
            P.c("dve", (lambda e, A2=A2, R=R, n=n: e.tensor_copy(out=R[64:128, 0, :], in_=A2[64:128, 0, 8:8 + n])), reads=[kk[2]], writes=[kk[3] + "b"])
            P.c("dve", (lambda e, A1=A1, A2=A2, N=N: e.tensor_tensor(out=A1[:, :, 4:N - 3], in0=A2[:, :, 2:N - 5], in1=A2[:, :, 6:N - 1], op=ALU.add)),
                reads=[kk[2], kk[1]], writes=[kk[1]])
            P.c("dve", (lambda e, A1=A1, R=R, n=n: e.tensor_copy(out=R[0:64, 1, :], in_=A1[0:64, 1, 8:8 + n])), reads=[kk[1]], writes=[kk[3] + "c"])
            P.c("dve", (lambda e, A1=A1, A2=A2, N=N: e.tensor_tensor(out=A2[:, :, 8:N - 7], in0=A1[:, :, 4:N - 11], in1=A1[:, :, 12:N - 3], op=ALU.add)),
                reads=[kk[1], kk[2]], writes=[kk[2]])
            P.c("dve", (lambda e, A2=A2, R=R, n=n: e.tensor_copy(out=R[64:128, 1, :], in_=A2[64:128, 1, 8:8 + n])), reads=[kk[2]], writes=[kk[3] + "d"])
            rk = [kk[3] + x for x in "abcd"]
            P.c("dve", (lambda e, R=R, inv=inv: e.tensor_tensor(out=R[:], in0=R[:], in1=inv[:], op=ALU.mult)), reads=rk + ["inv" + sfx], writes=[kk[3]])
            P.c("dve", (lambda e, R=R, Rb=Rb, B=B, n=n: e.tensor_tensor(out=Rb[:], in0=R[:], in1=B[:, :, 8:8 + n], op=ALU.subtract)),
                reads=[kk[3], kB], writes=["Rb" + sfx])
            for t_ in range(2):
                for t0 in range(0, n, 512):
                    tw = min(512, n - t0)
                    pb = ps[npsm[0] % 2]
                    pk = "pps%d" % (npsm[0] % 2)
                    npsm[0] += 1
                    P.c("pe", (lambda e, pb=pb, t_=t_, t0=t0, tw=tw, Rb=Rb: e.matmul(pb[:, 0:tw], lhsT=Wbb[:, t_, :], rhs=Rb[:, t_, t0:t0 + tw], start=True, stop=True)),
                        reads=["Wbb", "Rb" + sfx], writes=[pk])
                    P.c("act", (lambda e, pb=pb, t_=t_, t0=t0, tw=tw, c0=c0: e.activation(out=k.YT[:, t_, c0 + t0:c0 + t0 + tw], in_=pb[:, 0:tw], func=AF.Copy,
                                                                                     scale=psc[:, l, t_:t_ + 1])),
                        reads=[pk, "psc"], writes=["YT"])
        P.flush()


def head_norm(P, pl, tag, src, H, dh, gain_bc, out, rk, wk, extra_ss=None):
    sq = pl.get(tag + "sq", [128, H, dh], F32)
    ssh = pl.get(tag + "ssh", [128, H], F32)
    P.c("act", lambda e: e.activation(out=sq[:], in_=src, func=AF.Square), reads=rk, writes=[tag + "sq"])
    P.c("dve", lambda e: e.tensor_reduce(out=ssh[:], in_=sq[:], axis=AX.X, op=ALU.add), reads=[tag + "sq"], writes=[tag + "ssh"])
    if extra_ss is not None:
        ap, key, tot = extra_ss
        P.c("dve", lambda e: e.tensor_scalar_add(out=ssh[:], in0=ssh[:], scalar1=ap), reads=[tag + "ssh", key], writes=[tag + "ssh"])
    else:
        tot = dh
    rsqrt(P, ssh[:], ssh[:], 1.0 / tot, EPS, [tag + "ssh"], [tag + "ssh"])
    P.c("dve", lambda e: e.tensor_tensor(out=sq[:], in0=src, in1=ssh[:].unsqueeze(2).to_broadcast([128, H, dh]), op=ALU.mult),
        reads=rk + [tag + "ssh"], writes=[tag + "sq"])
    P.c("dve", lambda e: e.tensor_tensor(out=out, in0=sq[:], in1=gain_bc, op=ALU.mult), reads=[tag + "sq", "gains"], writes=wk)
    return ssh


def rope_apply(P, pl, tag, x, tab, out, rk, wk):
    xv = x.rearrange("p h (a b j) -> p h a b j", a=2, b=2)
    ov = out.rearrange("p h (a b j) -> p h a b j", a=2, b=2)
    x1, x2 = xv[:, :, :, 0, :], xv[:, :, :, 1, :]
    cos = tab[:, 0:16].rearrange("p (a j) -> p a j", a=2).unsqueeze(1).to_broadcast([128, 4, 2, 8])
    sin = tab[:, 16:32].rearrange("p (a j) -> p a j", a=2).unsqueeze(1).to_broadcast([128, 4, 2, 8])
    t = [pl.get(tag + "rt%d" % j, [128, 4, 2, 8], F32) for j in range(4)]
    P.c("dve", lambda e: e.tensor_tensor(out=t[0][:], in0=x1, in1=cos, op=ALU.mult), reads=rk, writes=[tag + "t0"])
    P.c("dve", lambda e: e.tensor_tensor(out=t[1][:], in0=x2, in1=sin, op=ALU.mult), reads=rk, writes=[tag + "t1"])
    P.c("dve", lambda e: e.tensor_tensor(out=ov[:, :, :, 0, :], in0=t[0][:], in1=t[1][:], op=ALU.subtract), reads=[tag + "t0", tag + "t1"], writes=wk)
    P.c("dve", lambda e: e.tensor_tensor(out=t[2][:], in0=x2, in1=cos, op=ALU.mult), reads=rk, writes=[tag + "t2"])
    P.c("dve", lambda e: e.tensor_tensor(out=t[3][:], in0=x1, in1=sin, op=ALU.mult), reads=rk, writes=[tag + "t3"])
    P.c("dve", lambda e: e.tensor_tensor(out=ov[:, :, :, 1, :], in0=t[2][:], in1=t[3][:], op=ALU.add), reads=[tag + "t2", tag + "t3"], writes=wk)


def heads_to_fm(P, pt, fin, dh, dst, rk, ptk, wk, ident, eng="act"):
    for h in range(4):
        P.c("pe", (lambda e, h=h: e.transpose(pt[0:dh, h, :], fin[:, h, :], ident[:])), reads=rk + ["ident"], writes=[ptk])
    if eng == "act":
        P.c("act", lambda e: e.copy(out=dst, in_=pt[0:dh, :, :]), reads=[ptk], writes=wk)
    else:
        P.c("dve", lambda e: e.tensor_copy(out=dst, in_=pt[0:dh, :, :]), reads=[ptk], writes=wk)


class AttnRes:
    pass


def attn_alloc(nc, pl):
    r = AttnRes()
    r.psS = [pl.ps([128, 4, 128], F32, "psS") for _ in range(2)]
    r.PT = [pl.sb([128, 4, 128], BF, "PT") for _ in range(3)]
    r.tb = [pl.sb([128, 4, 128], F32, "tb") for _ in range(2)]
    r.acc = [pl.ps([128, 4, 128], F32, "acc") for _ in range(2)]
    r.pty = pl.ps([128, 2, 128], BF, "pty")
    r.rec = pl.sb([128, 4], F32, "rec")
    r.yb = pl.sb([128, 4, 64], BF, "yb")
    r.ng = 0
    r.na = 0
    return r


def attn_tile(k, r, qT, dq, chunks, scale, ytj, col0, bias=None, nloc=0, bkey="bias"):
    P = k.P
    a = r.na % 2
    r.na += 1
    acc = r.acc[a]
    ak = "acc%d" % a
    groups = []
    for h in range(4):
        ch = chunks(h)
        n = len(ch)
        for g0 in range(0, n, 4):
            groups.append((h, ch, n, g0, min(n, g0 + 4)))

    def qk(g):
        h, ch, n, g0, g1 = g
        q_ap, qkeys = qT(h)
        b = r.ng % 2
        pb3 = r.ng % 3
        r.ng += 1
        psS, PT, tb = r.psS[b], r.PT[pb3], r.tb[b]
        sk, pk, tk = "psS%d" % b, "PT%d" % pb3, "tb%d" % b
        for ci in range(g0, g1):
            kT_ap, vp_ap, ckeys = ch[ci]
            P.c("pe", (lambda e, psS=psS, gi=ci - g0, kT_ap=kT_ap, q_ap=q_ap: e.matmul(psS[:, gi, :], lhsT=kT_ap, rhs=q_ap, start=True, stop=True)),
                reads=qkeys + ckeys, writes=[sk])
        nb = max(0, min(g1, nloc) - g0)
        if nb > 0:
            bv = bias(h)[:, g0:g0 + nb, :]
            P.c("dve", (lambda e, psS=psS, tb=tb, nb=nb, bv=bv: e.scalar_tensor_tensor(
                out=tb[:, 0:nb, :], in0=psS[:, 0:nb, :], scalar=scale, in1=bv, op0=ALU.mult, op1=ALU.add)),
                reads=[sk, bkey], writes=[tk])
            P.c("act", (lambda e, PT=PT, tb=tb, nb=nb: e.activation(out=PT[:, 0:nb, :], in_=tb[:, 0:nb, :], func=AF.Exp)),
                reads=[tk], writes=[pk])
        if g1 - g0 > nb:
            P.c("act", (lambda e, PT=PT, psS=psS, nb=nb, m=g1 - g0: e.activation(out=PT[:, nb:m, :], in_=psS[:, nb:m, :], func=AF.Exp, scale=scale)),
                reads=[sk], writes=[pk])
        return (PT, pk)

    def pv(g, st):
        h, ch, n, g0, g1 = g
        PT, pk = st
        for ci in range(g0, g1):
            kT_ap, vp_ap, ckeys = ch[ci]
            P.c("pe", (lambda e, PT=PT, gi=ci - g0, vp_ap=vp_ap, h=h, ci=ci, n=n: e.matmul(
                acc[:, h, 0:65], lhsT=PT[:, gi, :], rhs=vp_ap, start=(ci == 0), stop=(ci == n - 1))),
                reads=[pk] + ckeys, writes=[ak])

    prev = None
    for g in groups:
        st = qk(g)
        if prev is not None:
            pv(*prev)
        prev = (g, st)
    pv(*prev)
    P.c("dve", lambda e: e.reciprocal(out=r.rec[:], in_=acc[:, :, 64]), reads=[ak], writes=["rec"])
    P.c("dve", lambda e: e.tensor_tensor(out=r.yb[:], in0=acc[:, :, 0:64], in1=r.rec[:].unsqueeze(2).to_broadcast([128, 4, 64]), op=ALU.mult),
        reads=[ak, "rec"], writes=["yb"])
    ybf = r.yb[:].rearrange("p h d -> p (h d)")
    for j in range(2):
        P.c("pe", (lambda e, j=j: e.transpose(r.pty[:, j, :], ybf[:, j * 128:(j + 1) * 128], k.ident[:])), reads=["yb", "ident"], writes=["pty"])
    P.c("act", lambda e: e.copy(out=k.YT[:, ytj:ytj + 2, col0:col0 + 128], in_=r.pty[:]), reads=["pty"], writes=["YT"])


def stage_mla(k, l):
    nc, P, I = k.nc, k.P, k.I
    last = (l == DEPTH - 1)
    G = k.gains
    NS = 3
    with ExitStack() as es:
        pl = Pool_(nc, es)
        KT = pl.sb([96, 4, 34 * 128], BF, "KT")
        VP = pl.sb([128, 34, 4, 65], BF, "VP")
        QT = pl.sb([96, 4, TA], BF, "QT")
        with ExitStack() as es2:
            p2 = Pool_(nc, es2)
            Wkv = p2.sb([128, 512], BF, "wkv")
            Wq = p2.sb([128, 2, 384], BF, "wq")
            P.dma("pool", Wkv[:], I["mla_wkv_b"][l], writes=["Wkv"])
            P.dma("pool", Wq[:], I["mla_wq_b"][l].rearrange("(j p) n -> p j n", p=128), writes=["Wq"])
            P.c("pool", lambda e: e.memset(VP[:, :, :, 64:65], 1.0), writes=["VP1"])
            ss = p2.sb([128, 64], F32, "ss")
            junk = [p2.sb([128, 256], F32, "junk") for _ in range(2)]
            for i in range(34):
                st = i % NS
                tg = "k%d" % st
                src = k.G1[i * 128:(i + 1) * 128, 0:160] if i < 32 else k.U[T + (i - 32) * 128:T + (i - 31) * 128, 0:160]
                kva = p2.get(tg + "kva", [128, 160])
                ckvn = p2.get(tg + "ckvn", [128, 128], BF)
                cT_ = p2.get(tg + "cT", [128, 128], BF)
                sr = p2.get(tg + "sr", [128, 1])
                kfin = p2.get(tg + "kfin", [128, 4, 96], BF)
                t32 = p2.get(tg + "t32", [128, 32])
                kr = p2.get(tg + "kr", [128, 4, 32])
                tab = p2.get(tg + "tab", [128, 32])
                pT = p2.get("pT%d" % (i % 2), [128, 2, 128], BF, psum=True)
                pkv = p2.get("pkv%d" % (i % 2), [128, 512], F32, psum=True)
                pt4 = p2.get("pt4%d" % (i % 2), [128, 4, 128], BF, psum=True)
                pTk, pkvk, pt4k = "pT%d" % (i % 2), "pkv%d" % (i % 2), "pt4%d" % (i % 2)
                jk = junk[i % 2]
                jkk = "junk%d" % (i % 2)
                P.dma("sp", kva[:], src, writes=[tg + "kva"])
                if i < 32:
                    P.dma("sp", tab[:], I["ropek"][i * 128:(i + 1) * 128, :], writes=[tg + "tab"])
                P.c("act", (lambda e, kva=kva, i=i, jk=jk: e.activation(out=jk[:, 0:128], in_=kva[:, 0:128], func=AF.Square, accum_out=ss[:, i:i + 1])),
                    reads=[tg + "kva"], writes=[jkk, "ss%d" % i])
                P.c("act", (lambda e, kva=kva, sr=sr, jk=jk: e.activation(out=jk[:, 128:160], in_=kva[:, 128:160], func=AF.Square, accum_out=sr[:])),
                    reads=[tg + "kva"], writes=[jkk + "b", tg + "sr"])
                rsqrt(P, ss[:, i:i + 1], ss[:, i:i + 1], 1.0 / 128, EPS, ["ss%d" % i], ["ss%d" % i])
                P.c("dve", (lambda e, kva=kva, ckvn=ckvn, i=i: e.scalar_tensor_tensor(out=ckvn[:], in0=kva[:, 0:128], scalar=ss[:, i:i + 1], in1=G[:, l, 256:384],
                                                                                   op0=ALU.mult, op1=ALU.mult)), reads=[tg + "kva", "ss%d" % i, "gains"], writes=[tg + "ckvn"])
                P.c("pe", (lambda e, ckvn=ckvn, pT=pT: e.transpose(pT[:, 0, :], ckvn[:], k.ident[:])), reads=[tg + "ckvn", "ident"], writes=[pTk])
                P.c("act", (lambda e, cT_=cT_, pT=pT: e.copy(out=cT_[:], in_=pT[:, 0, :])), reads=[pTk], writes=[tg + "cT"])
                P.c("pe", (lambda e, cT_=cT_, pkv=pkv: e.matmul(pkv[:], lhsT=cT_[:], rhs=Wkv[:], start=True, stop=True)), reads=[tg + "cT", "Wkv"], writes=[pkvk])
                pv = pkv[:].rearrange("p (h c) -> p h c", h=4)
                P.c("act", (lambda e, i=i, pv=pv: e.copy(out=VP[:, i, :, 0:64], in_=pv[:, :, 64:128])), reads=[pkvk], writes=["VP%d" % i])
                rh = head_norm(P, p2, tg + "hn", pv[:, :, 0:64], 4, 64, G[:, l, 480:544].unsqueeze(1).to_broadcast([128, 4, 64]), kfin[:, :, 0:64],
                               [pkvk], [tg + "kfinA"], extra_ss=(sr[:], tg + "sr", 96))
                P.c("dve", (lambda e, kva=kva, t32=t32: e.tensor_tensor(out=t32[:], in0=kva[:, 128:160], in1=G[:, l, 544:576], op=ALU.mult)),
                    reads=[tg + "kva", "gains"], writes=[tg + "t32"])
                if i < 32:
                    P.c("dve", (lambda e, t32=t32, kr=kr, rh=rh: e.tensor_tensor(out=kr[:], in0=t32[:].unsqueeze(1).to_broadcast([128, 4, 32]),
                                                                             in1=rh[:].unsqueeze(2).to_broadcast([128, 4, 32]), op=ALU.mult)),
                        reads=[tg + "t32", tg + "hnssh"], writes=[tg + "kr"])
                    rope_apply(P, p2, tg + "rp", kr[:], tab, kfin[:, :, 64:96], [tg + "kr", tg + "tab"], [tg + "kfinB"])
                else:
                    P.c("dve", (lambda e, t32=t32, kfin=kfin, rh=rh: e.tensor_tensor(out=kfin[:, :, 64:96], in0=t32[:].unsqueeze(1).to_broadcast([128, 4, 32]),
                                                                                 in1=rh[:].unsqueeze(2).to_broadcast([128, 4, 32]), op=ALU.mult)),
                        reads=[tg + "t32", tg + "hnssh"], writes=[tg + "kfinB"])
                heads_to_fm(P, pt4, kfin, 96, KT[:, :, i * 128:(i + 1) * 128], [tg + "kfinA", tg + "kfinB"], pt4k, ["KT%d" % i], k.ident,
                            eng=("act" if i % 2 else "dve"))
            nq = 16 if last else 18
            for i in range(nq):
                st = i % NS
                tg = "q%d" % st
                qa = p2.get(tg + "qa", [128, 256])
                qan = p2.get(tg + "qan", [128, 256], BF)
                qT_ = p2.get(tg + "qanT", [128, 2, 128], BF)
                qn = p2.get(tg + "qn", [128, 4, 96])
                qfin = p2.get(tg + "qfin", [128, 4, 96], BF)
                tab = p2.get(tg + "tab", [128, 32])
                pT = p2.get("pT%d" % (i % 2), [128, 2, 128], BF, psum=True)
                pkv = p2.get("pkv%d" % (i % 2), [128, 512], F32, psum=True)
                pt4 = p2.get("pt4%d" % (i % 2), [128, 4, 128], BF, psum=True)
                pTk, pkvk, pt4k = "pT%d" % (i % 2), "pkv%d" % (i % 2), "pt4%d" % (i % 2)
                jk = junk[i % 2]
                jkk = "junk%d" % (i % 2)
                P.dma("sp", qa[:], k.U[i * 128:(i + 1) * 128, 928:1184], writes=[tg + "qa"])
                if i < 16:
                    P.dma("sp", tab[:], I["ropeq"][i * 128:(i + 1) * 128, :], writes=[tg + "tab"])
                P.c("act", (lambda e, qa=qa, i=i, jk=jk: e.activation(out=jk[:], in_=qa[:], func=AF.Square, accum_out=ss[:, 40 + i:41 + i])),
                    reads=[tg + "qa"], writes=[jkk, "qs%d" % i])
                rsqrt(P, ss[:, 40 + i:41 + i], ss[:, 40 + i:41 + i], 1.0 / 256, EPS, ["qs%d" % i], ["qs%d" % i])
                P.c("dve", (lambda e, qa=qa, qan=qan, i=i: e.scalar_tensor_tensor(out=qan[:], in0=qa[:], scalar=ss[:, 40 + i:41 + i], in1=G[:, l, 0:256],
                                                                                 op0=ALU.mult, op1=ALU.mult)), reads=[tg + "qa", "qs%d" % i, "gains"], writes=[tg + "qan"])
                for j in range(2):
                    P.c("pe", (lambda e, qan=qan, j=j, pT=pT: e.transpose(pT[:, j, :], qan[:, j * 128:(j + 1) * 128], k.ident[:])), reads=[tg + "qan", "ident"], writes=[pTk])
                P.c("act", (lambda e, qT_=qT_, pT=pT: e.copy(out=qT_[:], in_=pT[:])), reads=[pTk], writes=[tg + "qT"])
                for j in range(2):
                    P.c("pe", (lambda e, qT_=qT_, j=j, pkv=pkv: e.matmul(pkv[:, 0:384], lhsT=qT_[:, j, :], rhs=Wq[:, j, :], start=(j == 0), stop=(j == 1))),
                        reads=[tg + "qT", "Wq"], writes=[pkvk])
                pq = pkv[:, 0:384].rearrange("p (h c) -> p h c", h=4)
                head_norm(P, p2, tg + "hn", pq, 4, 96, G[:, l, 384:480].unsqueeze(1).to_broadcast([128, 4, 96]), qn[:], [pkvk], [tg + "qn"])
                P.c("act", (lambda e, qn=qn, qfin=qfin: e.copy(out=qfin[:, :, 0:64], in_=qn[:, :, 0:64])), reads=[tg + "qn"], writes=[tg + "qfinA"])
                if i < 16:
                    rope_apply(P, p2, tg + "rp", qn[:, :, 64:96], tab, qfin[:, :, 64:96], [tg + "qn", tg + "tab"], [tg + "qfinB"])
                else:
                    P.c("dve", (lambda e, qn=qn, qfin=qfin: e.tensor_copy(out=qfin[:, :, 64:96], in_=qn[:, :, 64:96])), reads=[tg + "qn"], writes=[tg + "qfinB"])
                heads_to_fm(P, pt4, qfin, 96, QT[:, :, i * 128:(i + 1) * 128], [tg + "qfinA", tg + "qfinB"], pt4k, ["QT%d" % i], k.ident,
                            eng=("act" if i % 2 else "dve"))
            P.flush()
        with ExitStack() as es3:
            p3 = Pool_(nc, es3)
            r = attn_alloc(nc, p3)
            sc = 96 ** -0.5
            for qi in range(16 if last else 18):
                cl = list(range(34)) if qi < 16 else [32, 33]
                attn_tile(k, r, (lambda h, qi=qi: (QT[:, h, qi * 128:(qi + 1) * 128], [])), 96,
                          (lambda h, cl=cl: [(KT[:, h, c * 128:(c + 1) * 128], VP[:, c, h, :], []) for c in cl]), sc, 2, qi * 128)
                if qi % 4 == 3:
                    P.flush()
            P.flush()


def stage_na(k, l):
    nc, P, I = k.nc, k.P, k.I
    last = (l == DEPTH - 1)
    G = k.gains
    NS = 3
    with ExitStack() as es:
        pl = Pool_(nc, es)
        KT = pl.sb([64, 4, 22 * 128], BF, "nKT")
        VP = pl.sb([128, 22, 4, 65], BF, "nVP")
        QT = pl.sb([64, 4, TA], BF, "nQT")
        with ExitStack() as es2:
            p2 = Pool_(nc, es2)
            P.c("pool", lambda e: e.memset(VP[:, :, :, 64:65], 1.0), writes=["VP1"])
            for i in range(22):
                if i < 2:
                    src = k.GN[256 + i * 128:256 + (i + 1) * 128, :]
                elif i < 18:
                    src = k.U[(i - 2) * 128:(i - 1) * 128, 160:672]
                elif i < 20:
                    src = k.GN[512 + (i - 18) * 128:512 + (i - 17) * 128, :]
                else:
                    src = k.U[T + (i - 20) * 128:T + (i - 19) * 128, 160:672]
                tg = "nk%d" % (i % NS)
                kv = p2.get(tg + "kv", [128, 512])
                kfin = p2.get(tg + "kfin", [128, 4, 64], BF)
                pt4 = p2.get("npt4%d" % (i % 2), [128, 4, 128], BF, psum=True)
                P.dma("sp", kv[:], src, writes=[tg + "kv"])
                P.c("act", (lambda e, kv=kv, i=i: e.copy(out=VP[:, i, :, 0:64], in_=kv[:, 256:512].rearrange("p (h d) -> p h d", h=4))),
                    reads=[tg + "kv"], writes=["VP%d" % i])
                head_norm(P, p2, tg + "hn", kv[:, 0:256].rearrange("p (h d) -> p h d", h=4), 4, 64,
                          G[:, l, 640:704].unsqueeze(1).to_broadcast([128, 4, 64]), kfin[:], [tg + "kv"], [tg + "kfin"])
                heads_to_fm(P, pt4, kfin, 64, KT[:, :, i * 128:(i + 1) * 128], [tg + "kfin"], "npt4%d" % (i % 2), ["KT%d" % i], k.ident,
                            eng=("act" if i % 2 else "dve"))
            for i in range(16 if last else 18):
                tg = "nq%d" % (i % NS)
                q = p2.get(tg + "q", [128, 256])
                qfin = p2.get(tg + "qfin", [128, 4, 64], BF)
                pt4 = p2.get("npt4%d" % (i % 2), [128, 4, 128], BF, psum=True)
                P.dma("sp", q[:], k.U[i * 128:(i + 1) * 128, 1184:1440], writes=[tg + "q"])
                head_norm(P, p2, tg + "hn", q[:].rearrange("p (h d) -> p h d", h=4), 4, 64,
                          G[:, l, 576:640].unsqueeze(1).to_broadcast([128, 4, 64]), qfin[:], [tg + "q"], [tg + "qfin"])
                heads_to_fm(P, pt4, qfin, 64, QT[:, :, i * 128:(i + 1) * 128], [tg + "qfin"], "npt4%d" % (i % 2), ["QT%d" % i], k.ident,
                            eng=("act" if i % 2 else "dve"))
            P.flush()
        with ExitStack() as es3:
            p3 = Pool_(nc, es3)
            r = attn_alloc(nc, p3)
            bb = [p3.sb([128, 6, 4, 128], F32, "nab") for _ in range(2)]
            sc = 64 ** -0.5
            for qi in range(16 if last else 18):
                if qi < 16:
                    t0 = 0 if qi == 0 else (14 if qi == 15 else qi)
                    nl = 6 if qi in (0, 15) else 5
                    cl = list(range(t0, t0 + nl)) + [20, 21]
                    B = bb[qi % 2]
                    P.dma("sp", B[:], I["nab"][l, qi], writes=["bias"])
                    bias = (lambda h, B=B: B[:, :, h, :])
                else:
                    cl, nl, bias = [20, 21], 0, None
                attn_tile(k, r, (lambda h, qi=qi: (QT[:, h, qi * 128:(qi + 1) * 128], [])), 64,
                          (lambda h, cl=cl: [(KT[:, h, c * 128:(c + 1) * 128], VP[:, c, h, :], []) for c in cl]), sc, 4, qi * 128,
                          bias=bias, nloc=nl)
                if qi % 4 == 3:
                    P.flush()
            P.flush()


def stage_attn(k, l, which):
    mla, na = (which == "mla"), (which == "na")
    nc, P, I = k.nc, k.P, k.I
    last = (l == DEPTH - 1)
    G = k.gains
    NS = 3
    nq = 16 if last else 18
    with ExitStack() as es:
        pl = Pool_(nc, es)
        if mla:
            KT = pl.sb([96, 4, 34 * 128], BF, "KT")
            VP = pl.sb([128, 34, 4, 65], BF, "VP")
            QT = pl.sb([96, 4, TA], BF, "QT")
        else:
            nKT = pl.sb([64, 4, 22 * 128], BF, "nKT")
            nVP = pl.sb([128, 22, 4, 65], BF, "nVP")
            nQT = pl.sb([64, 4, TA], BF, "nQT")
        with ExitStack() as es2:
            p2 = Pool_(nc, es2)
            if mla:
                Wkv = p2.sb([128, 512], BF, "wkv")
                Wq = p2.sb([128, 2, 384], BF, "wq")
                P.dma("pool", Wkv[:], I["mla_wkv_b"][l], writes=["Wkv"])
                P.dma("pool", Wq[:], I["mla_wq_b"][l].rearrange("(j p) n -> p j n", p=128), writes=["Wq"])
                P.c("pool", lambda e: e.memset(VP[:, :, :, 64:65], 1.0), writes=["VP1"])
                ss = p2.sb([128, 64], F32, "ss")
                junkk = [p2.sb([128, 160], F32, "junkk") for _ in range(4)]
                junkq = [p2.sb([128, 256], F32, "junkq") for _ in range(4)]
                pkvs = [p2.ps([128, 512], F32, "pkv") for _ in range(4)]
                pt4s = [p2.ps([128, 4, 128], BF, "pt4") for _ in range(4)]
            else:
                P.c("pool", lambda e: e.memset(nVP[:, :, :, 64:65], 1.0), writes=["nVP1"])
                npt4s = [p2.ps([128, 4, 128], BF, "npt4") for _ in range(4)]

            def ktile(i):
                sid = i % 2
                tg = "k%d_%d" % (sid, (i // 2) % 2)
                src = k.G1[i * 128:(i + 1) * 128, 0:160] if i < 32 else k.U[T + (i - 32) * 128:T + (i - 31) * 128, 0:160]
                kva = p2.get(tg + "kva", [128, 160])
                ckvn = p2.get(tg + "ckvn", [128, 128], BF)
                cT_ = p2.get(tg + "cT", [128, 128], BF)
                sr = p2.get(tg + "sr", [128, 1])
                kfin = p2.get(tg + "kfin", [128, 4, 96], BF)
                t32 = p2.get(tg + "t32", [128, 32])
                kr = p2.get(tg + "kr", [128, 4, 32])
                tab = p2.get(tg + "tab", [128, 32])
                pkv, pt4 = pkvs[sid], pt4s[sid]
                pT = pt4
                pTk, pkvk, pt4k = "pt4%d" % sid, "pkv%d" % sid, "pt4%d" % sid
                jk = junkk[sid * 2 + (i // 2) % 2]
                jkk = "junkk%d" % (sid * 2 + (i // 2) % 2)
                P.dma("sp", kva[:], src, writes=[tg + "kva"])
                if i < 32:
                    P.dma("sp", tab[:], I["ropek"][i * 128:(i + 1) * 128, :], writes=[tg + "tab"])
                P.c("act", (lambda e: e.activation(out=jk[:, 0:128], in_=kva[:, 0:128], func=AF.Square, accum_out=ss[:, i:i + 1])),
                    reads=[tg + "kva"], writes=[jkk, "ss%d" % i])
                P.c("act", (lambda e: e.activation(out=jk[:, 128:160], in_=kva[:, 128:160], func=AF.Square, accum_out=sr[:])),
                    reads=[tg + "kva"], writes=[jkk + "b", tg + "sr"])
                rsqrt(P, ss[:, i:i + 1], ss[:, i:i + 1], 1.0 / 128, EPS, ["ss%d" % i], ["ss%d" % i])
                P.c("dve", (lambda e: e.scalar_tensor_tensor(out=ckvn[:], in0=kva[:, 0:128], scalar=ss[:, i:i + 1], in1=G[:, l, 256:384],
                                                             op0=ALU.mult, op1=ALU.mult)), reads=[tg + "kva", "ss%d" % i, "gains"], writes=[tg + "ckvn"])
                P.c("pe", (lambda e: e.transpose(pT[:, 0, :], ckvn[:], k.ident[:])), reads=[tg + "ckvn", "ident"], writes=[pTk])
                P.c("act", (lambda e: e.copy(out=cT_[:], in_=pT[:, 0, :])), reads=[pTk], writes=[tg + "cT"])
                P.c("pe", (lambda e: e.matmul(pkv[:], lhsT=cT_[:], rhs=Wkv[:], start=True, stop=True)), reads=[tg + "cT", "Wkv"], writes=[pkvk])
                pv = pkv[:].rearrange("p (h c) -> p h c", h=4)
                P.c("act", (lambda e: e.copy(out=VP[:, i, :, 0:64], in_=pv[:, :, 64:128])), reads=[pkvk], writes=["VP%d" % i])
                rh = head_norm(P, p2, tg + "hn", pv[:, :, 0:64], 4, 64, G[:, l, 480:544].unsqueeze(1).to_broadcast([128, 4, 64]), kfin[:, :, 0:64],
                               [pkvk], [tg + "kfinA"], extra_ss=(sr[:], tg + "sr", 96))
                P.c("dve", (lambda e: e.tensor_tensor(out=t32[:], in0=kva[:, 128:160], in1=G[:, l, 544:576], op=ALU.mult)),
                    reads=[tg + "kva", "gains"], writes=[tg + "t32"])
                if i < 32:
                    P.c("dve", (lambda e: e.tensor_tensor(out=kr[:], in0=t32[:].unsqueeze(1).to_broadcast([128, 4, 32]),
                                                          in1=rh[:].unsqueeze(2).to_broadcast([128, 4, 32]), op=ALU.mult)),
                        reads=[tg + "t32", tg + "hnssh"], writes=[tg + "kr"])
                    rope_apply(P, p2, tg + "rp", kr[:], tab, kfin[:, :, 64:96], [tg + "kr", tg + "tab"], [tg + "kfinB"])
                else:
                    P.c("dve", (lambda e: e.tensor_tensor(out=kfin[:, :, 64:96], in0=t32[:].unsqueeze(1).to_broadcast([128, 4, 32]),
                                                          in1=rh[:].unsqueeze(2).to_broadcast([128, 4, 32]), op=ALU.mult)),
                        reads=[tg + "t32", tg + "hnssh"], writes=[tg + "kfinB"])
                heads_to_fm(P, pt4, kfin, 96, KT[:, :, i * 128:(i + 1) * 128], [tg + "kfinA", tg + "kfinB"], pt4k, ["KT%d" % i], k.ident,
                            eng=("act" if i % 2 else "dve"))

            def qtile(i):
                sid = 2 + i % 2
                tg = "q%d_%d" % (sid, (i // 2) % 2)
                qa = p2.get(tg + "qa", [128, 256])
                qan = p2.get(tg + "qan", [128, 256], BF)
                qT_ = p2.get(tg + "qanT", [128, 2, 128], BF)
                qn = p2.get(tg + "qn", [128, 4, 96])
                qfin = p2.get(tg + "qfin", [128, 4, 96], BF)
                tab = p2.get(tg + "tab", [128, 32])
                pkv, pt4 = pkvs[sid], pt4s[sid]
                pT = pt4
                pTk, pkvk, pt4k = "pt4%d" % sid, "pkv%d" % sid, "pt4%d" % sid
                jk = junkq[(sid - 2) * 2 + (i // 2) % 2]
                jkk = "junkq%d" % ((sid - 2) * 2 + (i // 2) % 2)
                P.dma("sp", qa[:], k.U[i * 128:(i + 1) * 128, 928:1184], writes=[tg + "qa"])
                if i < 16:
                    P.dma("sp", tab[:], I["ropeq"][i * 128:(i + 1) * 128, :], writes=[tg + "tab"])
                P.c("act", (lambda e: e.activation(out=jk[:], in_=qa[:], func=AF.Square, accum_out=ss[:, 40 + i:41 + i])),
                    reads=[tg + "qa"], writes=[jkk, "qs%d" % i])
                rsqrt(P, ss[:, 40 + i:41 + i], ss[:, 40 + i:41 + i], 1.0 / 256, EPS, ["qs%d" % i], ["qs%d" % i])
                P.c("dve", (lambda e: e.scalar_tensor_tensor(out=qan[:], in0=qa[:], scalar=ss[:, 40 + i:41 + i], in1=G[:, l, 0:256],
                                                             op0=ALU.mult, op1=ALU.mult)), reads=[tg + "qa", "qs%d" % i, "gains"], writes=[tg + "qan"])
                for j in range(2):
                    P.c("pe", (lambda e, j=j: e.transpose(pT[:, j, :], qan[:, j * 128:(j + 1) * 128], k.ident[:])), reads=[tg + "qan", "ident"], writes=[pTk])
                P.c("act", (lambda e: e.copy(out=qT_[:], in_=pT[:, 0:2, :])), reads=[pTk], writes=[tg + "qT"])
                for j in range(2):
                    P.c("pe", (lambda e, j=j: e.matmul(pkv[:, 0:384], lhsT=qT_[:, j, :], rhs=Wq[:, j, :], start=(j == 0), stop=(j == 1))),
                        reads=[tg + "qT", "Wq"], writes=[pkvk])
                pq = pkv[:, 0:384].rearrange("p (h c) -> p h c", h=4)
                head_norm(P, p2, tg + "hn", pq, 4, 96, G[:, l, 384:480].unsqueeze(1).to_broadcast([128, 4, 96]), qn[:], [pkvk], [tg + "qn"])
                P.c("act", (lambda e: e.copy(out=qfin[:, :, 0:64], in_=qn[:, :, 0:64])), reads=[tg + "qn"], writes=[tg + "qfinA"])
                if i < 16:
                    rope_apply(P, p2, tg + "rp", qn[:, :, 64:96], tab, qfin[:, :, 64:96], [tg + "qn", tg + "tab"], [tg + "qfinB"])
                else:
                    P.c("dve", (lambda e: e.tensor_copy(out=qfin[:, :, 64:96], in_=qn[:, :, 64:96])), reads=[tg + "qn"], writes=[tg + "qfinB"])
                heads_to_fm(P, pt4, qfin, 96, QT[:, :, i * 128:(i + 1) * 128], [tg + "qfinA", tg + "qfinB"], pt4k, ["QT%d" % i], k.ident,
                            eng=("act" if i % 2 else "dve"))

            def nktile(i):
                if i < 2:
                    src = k.GN[256 + i * 128:256 + (i + 1) * 128, :]
                elif i < 18:
                    src = k.U[(i - 2) * 128:(i - 1) * 128, 160:672]
                elif i < 20:
                    src = k.GN[512 + (i - 18) * 128:512 + (i - 17) * 128, :]
                else:
                    src = k.U[T + (i - 20) * 128:T + (i - 19) * 128, 160:672]
                tg = "nk%d_%d" % (i % 2, (i // 2) % 2)
                kv = p2.get(tg + "kv", [128, 512])
                kfin = p2.get(tg + "kfin", [128, 4, 64], BF)
                P.dma("sp", kv[:], src, writes=[tg + "kv"])
                P.c("act", (lambda e: e.copy(out=nVP[:, i, :, 0:64], in_=kv[:, 256:512].rearrange("p (h d) -> p h d", h=4))),
                    reads=[tg + "kv"], writes=["nVP%d" % i])
                head_norm(P, p2, tg + "hn", kv[:, 0:256].rearrange("p (h d) -> p h d", h=4), 4, 64,
                          G[:, l, 640:704].unsqueeze(1).to_broadcast([128, 4, 64]), kfin[:], [tg + "kv"], [tg + "kfin"])
                heads_to_fm(P, npt4s[i % 2], kfin, 64, nKT[:, :, i * 128:(i + 1) * 128], [tg + "kfin"], "npt4%d" % (i % 2), ["nKT%d" % i], k.ident,
                            eng=("act" if i % 2 else "dve"))

            def nqtile(i):
                tg = "nq%d_%d" % (i % 2, (i // 2) % 2)
                q = p2.get(tg + "q", [128, 256])
                qfin = p2.get(tg + "qfin", [128, 4, 64], BF)
                P.dma("sp", q[:], k.U[i * 128:(i + 1) * 128, 1184:1440], writes=[tg + "q"])
                head_norm(P, p2, tg + "hn", q[:].rearrange("p (h d) -> p h d", h=4), 4, 64,
                          G[:, l, 576:640].unsqueeze(1).to_broadcast([128, 4, 64]), qfin[:], [tg + "q"], [tg + "qfin"])
                heads_to_fm(P, npt4s[2 + i % 2], qfin, 64, nQT[:, :, i * 128:(i + 1) * 128], [tg + "qfin"], "npt4%d" % (2 + i % 2), ["nQT%d" % i], k.ident,
                            eng=("dve" if i % 2 else "act"))

            def run(fn, idxs):
                return lambda: [fn(i) for i in idxs]
            if mla:
                P.record_streams([run(ktile, range(0, 34, 2)), run(ktile, range(1, 34, 2)),
                                  run(qtile, range(0, nq, 2)), run(qtile, range(1, nq, 2))])
            else:
                P.record_streams([run(nktile, range(0, 22, 2)), run(nktile, range(1, 22, 2)),
                                  run(nqtile, range(0, nq, 2)), run(nqtile, range(1, nq, 2))])
            P.flush()
        with ExitStack() as es3:
            p3 = Pool_(nc, es3)
            r = attn_alloc(nc, p3)
            if na:
                bb = [p3.sb([128, 6, 4, 128], F32, "nab") for _ in range(2)]
            sc = 96 ** -0.5

            def mla_sweep():
                for qi in range(nq if mla else 0):
                    cl = list(range(34)) if qi < 16 else [32, 33]
                    attn_tile(k, r, (lambda h, qi=qi: (QT[:, h, qi * 128:(qi + 1) * 128], [])), 96,
                              (lambda h, cl=cl: [(KT[:, h, c * 128:(c + 1) * 128], VP[:, c, h, :], []) for c in cl]), sc, 2, qi * 128)
            if mla and l + 1 < DEPTH:
                P.record_streams([mla_sweep, lambda: mods_ops(k, p3, l + 1)])
            else:
                mla_sweep()
            sc = 64 ** -0.5
            for qi in range(nq if na else 0):
                if qi < 16:
                    t0 = 0 if qi == 0 else (14 if qi == 15 else qi)
                    nl = 6 if qi in (0, 15) else 5
                    cl = list(range(t0, t0 + nl)) + [20, 21]
                    B = bb[qi % 2]
                    bk = "bias%d" % (qi % 2)
                    P.dma("sp", B[:], I["nab"][l, qi], writes=[bk])
                    bias = (lambda h, B=B: B[:, :, h, :])
                else:
                    cl, nl, bias, bk = [20, 21], 0, None, "bias0"
                attn_tile(k, r, (lambda h, qi=qi: (nQT[:, h, qi * 128:(qi + 1) * 128], [])), 64,
                          (lambda h, cl=cl: [(nKT[:, h, c * 128:(c + 1) * 128], nVP[:, c, h, :], []) for c in cl]), sc, 4, qi * 128,
                          bias=bias, nloc=nl, bkey=bk)
            P.flush()


def stage_hgrn(k, l, upto=None):
    nc, P, I = k.nc, k.P, k.I
    last = (l == DEPTH - 1)
    NCH = 36
    G = k.gains
    if not hasattr(k, "EXSin"):
        k.EXSin = nc.dram_tensor("EXSin", [256, 256], F32, kind="Internal").ap()
        k.GS = nc.dram_tensor("GS", [512, 256], F32, kind="Internal").ap()
    with ExitStack() as es:
        pl = Pool_(nc, es)
        LB = pl.sb([128, 2, 2], F32, "LB")
        OML = pl.sb([128, 2, 2], F32, "OML")
        lbt = pl.sb([128, 3, 4], F32, "lbt")
        lsum = pl.sb([128, 4], F32, "lsum")
        P.dma("sp", lbt[:], I["hg_lbT"].rearrange("p s d h -> p s (d h)"), writes=["lbt"])
        P.c("act", lambda e: e.activation(out=lbt[:], in_=lbt[:], func=AF.Exp), reads=["lbt"], writes=["lbt"])
        P.c("dve", lambda e: e.tensor_tensor(out=lsum[:], in0=lbt[:, 0, :], in1=lbt[:, 1, :], op=ALU.add), reads=["lbt"], writes=["lsum"])
        P.c("dve", lambda e: e.tensor_tensor(out=lsum[:], in0=lsum[:], in1=lbt[:, 2, :], op=ALU.add), reads=["lsum", "lbt"], writes=["lsum"])
        P.c("dve", lambda e: e.reciprocal(out=lsum[:], in_=lsum[:]), reads=["lsum"], writes=["lsum"])
        LBf = LB[:].rearrange("p d h -> p (d h)")
        OMf = OML[:].rearrange("p d h -> p (d h)")
        if l == 0:
            P.c("dve", lambda e: e.tensor_tensor(out=LBf, in0=lbt[:, 0, :], in1=lsum[:], op=ALU.mult), reads=["lsum", "lbt"], writes=["LB"])
        else:
            P.c("dve", lambda e: e.tensor_tensor(out=LBf, in0=lbt[:, 0, :], in1=lbt[:, 1, :], op=ALU.add), reads=["lbt"], writes=["LB"])
            P.c("dve", lambda e: e.tensor_tensor(out=LBf, in0=LBf, in1=lsum[:], op=ALU.mult), reads=["lsum", "LB"], writes=["LB"])
        P.c("dve", lambda e: e.tensor_scalar(out=LBf, in0=LBf, scalar1=1e-6, scalar2=1.0 - 1e-6, op0=ALU.max, op1=ALU.min), reads=["LB"], writes=["LB"])
        P.c("dve", lambda e: e.tensor_scalar(out=OMf, in0=LBf, scalar1=-1.0, scalar2=1.0, op0=ALU.mult, op1=ALU.add), reads=["LB"], writes=["OML"])
        zt = pl.sb([128, 256], F32, "zt")
        P.c("pool", lambda e: e.memset(zt[:], 0.0), writes=["zt"])
        for d_ in range(2):
            P.dma("sp", k.EXSin[d_ * 128:(d_ + 1) * 128, :], zt[:], reads=["zt"], writes=["EXSin"])
        cmask = pl.sb([128, TA // 2], F32, "cmask")
        P.c("pool", lambda e: e.memset(cmask[:], 1.0), writes=["cmask"])
        P.c("pool", lambda e: e.memset(cmask[:].rearrange("p (c t) -> p c t", t=64)[:, :, 0:1], 0.0), reads=["cmask"], writes=["cmask"])
        mk = pl.sb([64, 2, 64], F32, "trimask")
        P.c("pool", lambda e: e.memset(mk[:], 1.0), writes=["mk"])
        P.c("pool", lambda e: e.affine_select(out=mk[:, 0, :], in_=mk[:, 0, :], compare_op=ALU.is_ge, fill=0.0, base=0, pattern=[[1, 64]], channel_multiplier=-1),
            reads=["mk"], writes=["mk"])
        P.c("pool", lambda e: e.affine_select(out=mk[:, 1, :], in_=mk[:, 1, :], compare_op=ALU.is_ge, fill=0.0, base=0, pattern=[[-1, 64]], channel_multiplier=1),
            reads=["mk"], writes=["mk"])
        mka = pl.sb([64, 2, 64], F32, "mka")
        P.c("pool", lambda e: e.memset(mka[:], 0.0), writes=["mka"])
        P.c("pool", lambda e: e.memset(mka[0:32, 0, 32:64], 1.0), reads=["mka"], writes=["mka"])
        P.c("pool", lambda e: e.memset(mka[32:64, 1, 0:32], 1.0), reads=["mka"], writes=["mka"])
        P.c("pool", lambda e: e.memset(mk[0:32, 0, 32:64], 0.0), reads=["mk"], writes=["mk"])
        P.c("pool", lambda e: e.memset(mk[32:64, 1, 0:32], 0.0), reads=["mk"], writes=["mk"])
        mk4 = pl.sb([64, 4, 64], F32, "mk4")
        P.c("dve", lambda e: e.tensor_copy(out=mk4[:, 0:2, :], in_=mk[:]), reads=["mk"], writes=["mk4"])
        P.c("dve", lambda e: e.tensor_copy(out=mk4[:, 2:4, :], in_=mka[:]), reads=["mka"], writes=["mk4"])
        bm = pl.sb([128, 128], F32, "bm")
        P.c("pool", lambda e: e.memset(bm[:], 0.0), writes=["bm"])
        P.c("pool", lambda e: e.memset(bm[0:64, 0:64], 1.0), reads=["bm"], writes=["bm"])
        P.c("pool", lambda e: e.memset(bm[64:128, 64:128], 1.0), reads=["bm"], writes=["bm"])
        pm = pl.sb([128, 2], F32, "pm")
        P.c("pool", lambda e: e.memset(pm[:], 0.0), writes=["pm"])
        P.c("pool", lambda e: e.memset(pm[0:64, 0:1], 1.0), reads=["pm"], writes=["pm"])
        P.c("pool", lambda e: e.memset(pm[64:128, 1:2], 1.0), reads=["pm"], writes=["pm"])
        P.flush()
        nch_out = 32 if last else 36
        for hp in range(2):
            with ExitStack() as es2:
                p2 = Pool_(nc, es2)
                QT_ = p2.sb([128, 2, TA], BF, "hq")
                KT_ = p2.sb([128, 2, TA], BF, "hk")
                QE = p2.sb([128, 2, TA], BF, "hqe")
                KA = p2.sb([128, 2, TA], BF, "hka")
                EBL = p2.sb([128, 2, NCH], F32, "ebl")
                BLs = p2.sb([128, 2, NCH], F32, "bls")
                KH = p2.sb([64, NCH, 2, 128], BF, "KH")
                IT = p2.sb([64, NCH, 128], BF, "IT")
                SB = p2.sb([128, 2, NCH, 128], BF, "SB")
                SG = p2.sb([64, NCH, 128], F32, "SG")
                for c9 in range(0, NCH, 9):
                    P.dma("pool", IT[:, c9:c9 + 9, :], k.U[c9 * 64:(c9 + 9) * 64, 672 + hp * 128:672 + (hp + 1) * 128].rearrange("(c s) n -> s c n", s=64), writes=["IT%d" % c9])
                    P.dma("sp", SG[:, c9:c9 + 9, :], k.U[c9 * 64:(c9 + 9) * 64, 1440 + hp * 128:1440 + (hp + 1) * 128].rearrange("(c s) n -> s c n", s=64), writes=["SG"])
                P.c("act", lambda e: e.activation(out=SG[:], in_=SG[:], func=AF.Exp, scale=-1.0), reads=["SG"], writes=["SG"])
                P.c("dve", lambda e: e.tensor_scalar_add(out=SG[:], in0=SG[:], scalar1=1.0), reads=["SG"], writes=["SG"])
                P.c("dve", lambda e: e.reciprocal(out=SG[:], in_=SG[:]), reads=["SG"], writes=["SG"])
                NTB = 3
                for tb in range(NTB):
                  with ExitStack() as es3:
                    TB, NB = TA // NTB, NCH // NTB
                    tsl = slice(tb * TB, (tb + 1) * TB)
                    csl = slice(tb * NB, (tb + 1) * NB)
                    p3 = Pool_(nc, es3)
                    q = p3.sb([128, TB], F32, "hqq")
                    P.dma("sp", q[:], k.UT[768 + hp * 128:768 + (hp + 1) * 128, tsl], writes=["q"])

                    def dstream(d):
                        sd = "%d" % d
                        f = p3.sb([128, TB], F32, "hf")
                        lf = p3.sb([128, TB], F32, "hlf")
                        Pc = p3.sb([128, TB], F32, "hP")
                        b_ = p3.sb([128, TB], F32, "hb")
                        kk = p3.sb([128, TB], F32, "hkk")
                        kh = p3.sb([128, TB], BF, "hkh")
                        mref = p3.sb([128, NB, 2], F32, "hmref")
                        blt = p3.sb([128, NB], F32, "hblt")
                        c0t = p3.sb([128, NB], F32, "hc0t")
                        ptk = p3.ps([64, 4, 128], BF, "ptk")
                        lbs, oms = LB[:, d, hp:hp + 1], OML[:, d, hp:hp + 1]
                        kf, klf, kP, kb, kkk = "f" + sd, "lf" + sd, "Pc" + sd, "b" + sd, "kk" + sd
                        P.dma("sp", f[:], k.UT[256 + d * 256 + hp * 128:256 + d * 256 + (hp + 1) * 128, tsl], writes=[kf])
                        P.c("act", lambda e: e.activation(out=f[:], in_=f[:], func=AF.Sigmoid), reads=[kf], writes=[kf])
                        P.c("act", lambda e: e.activation(out=f[:], in_=f[:], func=AF.Identity, scale=oms, bias=lbs), reads=[kf, "LB", "OML"], writes=[kf])
                        P.c("act", lambda e: e.activation(out=lf[:], in_=f[:], func=AF.Ln), reads=[kf], writes=[klf])
                        P.c("act", lambda e: e.activation(out=kk[:], in_=f[:], func=AF.Identity, scale=-1.0, bias=1.0), reads=[kf], writes=[kkk])
                        P.c("dve", lambda e: e.tensor_tensor_scan(out=Pc[:], data0=cmask[:, 0:TB], data1=lf[:], initial=0.0, op0=ALU.mult, op1=ALU.add),
                            reads=[klf, "cmask"], writes=[kP])
                        Pv = Pc[:].rearrange("p (c t) -> p c t", t=64)
                        P.c("dve", lambda e: e.tensor_copy(out=blt[:], in_=Pv[:, :, 63]), reads=[kP], writes=["blt" + sd])
                        P.c("dve", lambda e: e.tensor_copy(out=c0t[:], in_=Pv[:, :, 31]), reads=[kP], writes=["c0t" + sd])
                        blv, c0v = blt[:], c0t[:]
                        blb = blv.unsqueeze(2).to_broadcast([128, NB, 64])
                        bv = b_[:].rearrange("p (c t) -> p c t", t=64)
                        lfv = lf[:].rearrange("p (c t) -> p c t", t=64)
                        rk = [kP, "blt" + sd, "c0t" + sd]
                        P.c("act", lambda e: e.activation(out=EBL[:, d, csl], in_=blv, func=AF.Exp), reads=rk, writes=["EBL" + sd])
                        P.c("dve", lambda e: e.tensor_copy(out=BLs[:, d, csl], in_=blv), reads=["blt" + sd], writes=["BLs" + sd])
                        if d == 0:
                            P.c("act", lambda e: e.copy(out=b_[:], in_=Pc[:]), reads=rk, writes=[kb])
                            P.c("dve", lambda e: e.tensor_scalar_mul(out=mref[:, :, 0], in0=c0v, scalar1=0.5), reads=rk, writes=["mref0" + sd])
                            P.c("dve", lambda e: e.tensor_tensor(out=mref[:, :, 1], in0=c0v, in1=blv, op=ALU.add), reads=rk, writes=["mref1" + sd])
                            P.c("dve", lambda e: e.tensor_scalar_mul(out=mref[:, :, 1], in0=mref[:, :, 1], scalar1=0.5), reads=["mref1" + sd], writes=["mref1" + sd])
                        else:
                            P.c("dve", lambda e: e.tensor_tensor(out=bv, in0=blb, in1=Pv, op=ALU.subtract), reads=rk, writes=[kb])
                            P.c("dve", lambda e: e.tensor_tensor(out=bv, in0=bv, in1=lfv, op=ALU.add), reads=[kb, klf], writes=[kb])
                            P.c("dve", lambda e: e.scalar_tensor_tensor(out=mref[:, :, 0], in0=c0v, scalar=-0.5, in1=blv, op0=ALU.mult, op1=ALU.add),
                                reads=rk, writes=["mref0" + sd])
                            P.c("dve", lambda e: e.tensor_tensor(out=mref[:, :, 1], in0=blv, in1=c0v, op=ALU.subtract), reads=rk, writes=["mref1" + sd])
                            P.c("dve", lambda e: e.tensor_scalar_mul(out=mref[:, :, 1], in0=mref[:, :, 1], scalar1=0.5), reads=["mref1" + sd], writes=["mref1" + sd])
                        P.c("act", lambda e: e.activation(out=f[:], in_=b_[:], func=AF.Exp), reads=[kb], writes=[kf])
                        P.c("dve", lambda e: e.scalar_tensor_tensor(out=QE[:, d, tsl], in0=q[:], scalar=0.125, in1=f[:], op0=ALU.mult, op1=ALU.mult),
                            reads=["q", kf], writes=["QE" + sd])
                        P.c("dve", lambda e: e.tensor_tensor(out=Pv, in0=blb, in1=bv, op=ALU.subtract), reads=[kb, kP, "blt" + sd], writes=[kP])
                        P.c("act", lambda e: e.activation(out=Pc[:], in_=Pc[:], func=AF.Exp), reads=[kP], writes=[kP])
                        P.c("dve", lambda e: e.tensor_tensor(out=kh[:], in0=kk[:], in1=Pc[:], op=ALU.mult), reads=[kkk, kP], writes=["kh" + sd])
                        b4 = b_[:].rearrange("p (c j t) -> p c j t", j=2, t=32)
                        P4 = Pc[:].rearrange("p (c j t) -> p c j t", j=2, t=32)
                        k4 = kk[:].rearrange("p (c j t) -> p c j t", j=2, t=32)
                        KA4 = KA[:, d, tsl].rearrange("p (c j t) -> p c j t", j=2, t=32)
                        P.c("act", lambda e: e.activation(out=P4[:, :, d, :], in_=b4[:, :, d, :], func=AF.Exp, scale=-1.0), reads=[kb, "kh" + sd], writes=[kP])
                        P.c("dve", lambda e: e.scalar_tensor_tensor(out=KA4[:, :, d, :], in0=P4[:, :, d, :], scalar=5.0e34, in1=k4[:, :, d, :], op0=ALU.min, op1=ALU.mult),
                            reads=[kkk, kP], writes=["KA" + sd])
                        P.c("pool", lambda e: e.memset(KA4[:, :, 1 - d, :], 0.0), reads=[], writes=["KAz" + sd])
                        P.c("dve", lambda e: e.tensor_tensor(out=b4, in0=b4, in1=mref[:].unsqueeze(3).to_broadcast([128, NB, 2, 32]), op=ALU.subtract),
                            reads=[kb, "mref0" + sd, "mref1" + sd, kf, kP], writes=[kb])
                        P.c("act", lambda e: e.activation(out=f[:], in_=b_[:], func=AF.Exp), reads=[kb, "QE" + sd], writes=[kf])
                        P.c("act", lambda e: e.activation(out=lf[:], in_=b_[:], func=AF.Exp, scale=-1.0), reads=[kb], writes=[klf])
                        P.c("dve", lambda e: e.scalar_tensor_tensor(out=QT_[:, d, tsl], in0=q[:], scalar=0.125, in1=f[:], op0=ALU.mult, op1=ALU.mult),
                            reads=["q", kf], writes=["QT_" + sd])
                        P.c("dve", lambda e: e.tensor_tensor(out=KT_[:, d, tsl], in0=kk[:], in1=lf[:], op=ALU.mult), reads=[kkk, klf], writes=["KT_" + sd])
                        for c0 in range(0, NB, 4):
                            n4 = min(4, NB - c0)
                            for ci in range(n4):
                                c = c0 + ci
                                P.c("pe", (lambda e, ci=ci, c=c: e.transpose(ptk[:, ci, :], kh[:, c * 64:(c + 1) * 64], k.ident[:])),
                                    reads=["kh" + sd, "ident"], writes=["ptk" + sd])
                            P.c("dve", (lambda e, c0=c0, n4=n4: e.tensor_copy(out=KH[:, tb * NB + c0:tb * NB + c0 + n4, d, :], in_=ptk[:, 0:n4, :])),
                                reads=["ptk" + sd], writes=["KH" + sd])

                    P.record_streams([lambda: dstream(0), lambda: dstream(1)])
                    P.flush()
                if upto == "hgrn1":
                    continue
                with ExitStack() as es4:
                    p4 = Pool_(nc, es4)
                    S = [p4.sb([128, 128], F32, "S%d" % d) for d in range(2)]
                    psG = [[p4.ps([128, 128], F32, "psG") for _ in range(2)] for d in range(2)]
                    tmpG = [[p4.sb([128, 128], F32, "tmpG") for _ in range(2)] for d in range(2)]
                    Sc = [p4.sb([128, 128], F32, "Sc%d" % d) for d in range(2)]
                    E = p4.sb([128, 2, 128], F32, "Eend")
                    Gs = p4.sb([128, 2, 128], F32, "Gs")
                    ng = [0, 0]

                    def step(d, c, Sd):
                        b = ng[d] % 2
                        ng[d] += 1
                        pg, tg_, gk = psG[d][b], tmpG[d][b], "G%d_%d" % (d, b)
                        P.c("pe", (lambda e: e.matmul(pg[:], lhsT=KH[:, c, d, :], rhs=IT[:, c, :], start=True, stop=True)),
                            reads=["KH", "IT"], writes=["ps" + gk])
                        P.c("dve", (lambda e: e.tensor_tensor(out=tg_[:], in0=pg[:], in1=bm[:], op=ALU.mult)), reads=["ps" + gk, "bm"], writes=["tm" + gk])
                        P.c("dve", (lambda e: e.scalar_tensor_tensor(out=Sd[:], in0=Sd[:], scalar=EBL[:, d, c:c + 1], in1=tg_[:],
                                                                     op0=ALU.mult, op1=ALU.add)), reads=["tm" + gk, "S%d" % d, "EBL"], writes=["S%d" % d])

                    def save(d, c, Sd):
                        P.c("act", (lambda e: e.copy(out=SB[:, d, c, :], in_=Sd[:])), reads=["S%d" % d], writes=["SB%d" % d])

                    def chain(d):
                        P.c("pool", (lambda e: e.memset(S[d][:], 0.0)), writes=["S%d" % d])
                        for i in range(4):
                            c = 32 + i if d == 0 else 35 - i
                            save(d, c, S[d])
                            step(d, c, S[d])
                        P.c("dve", (lambda e: e.tensor_copy(out=Sc[d][:], in_=S[d][:])), reads=["S%d" % d], writes=["Sc%d" % d])
                        for i in range(32):
                            c = i if d == 0 else 31 - i
                            save(d, c, S[d])
                            step(d, c, S[d])
                        P.c("dve", (lambda e: e.tensor_copy(out=E[:, d, :], in_=S[d][:])), reads=["S%d" % d], writes=["E%d" % d])
                        P.dma("sp", k.EXSin[d * 128:(d + 1) * 128, hp * 128:(hp + 1) * 128], E[:, d, :], reads=["E%d" % d], writes=["EXSin%d" % d])

                    P.record_streams([lambda: chain(0), lambda: chain(1)])
                    PF = p4.sb([128, 2, 32], F32, "PF")
                    ones32 = p4.sb([128, 32], F32, "ones32")
                    P.c("pool", lambda e: e.memset(ones32[:], 1.0), writes=["ones32"])
                    for d in range(2):
                        P.c("dve", (lambda e, d=d: e.tensor_tensor_scan(out=PF[:, d, :], data0=ones32[:], data1=BLs[:, d, 0:32], initial=0.0, op0=ALU.mult, op1=ALU.add)),
                            reads=["ones32"], writes=["PF%d" % d])
                    Dc = p4.sb([128, 2, 32], F32, "Dc")
                    P.c("dve", lambda e: e.tensor_tensor(out=Dc[:, 0, :], in0=PF[:, 0, :], in1=BLs[:, 0, 0:32], op=ALU.subtract), reads=["PF0"], writes=["Dc0"])
                    P.c("dve", lambda e: e.tensor_tensor(out=Dc[:, 1, :], in0=PF[:, 1, 31:32].to_broadcast([128, 32]), in1=PF[:, 1, :], op=ALU.subtract), reads=["PF1"], writes=["Dc1"])
                    P.c("act", lambda e: e.activation(out=Dc[:], in_=Dc[:], func=AF.Exp), reads=["Dc0", "Dc1"], writes=["Dc"])
                    P.cc(lambda e: e.collective_compute("AllGather", ALU.bypass, replica_groups=k.groups, ins=[k.EXSin], outs=[k.GS]),
                         reads=["EXSin0", "EXSin1"], writes=["GS"])
                    P.dma("sp", Gs[:, 0, :], k.GS[0:128, hp * 128:(hp + 1) * 128], reads=["GS"], writes=["Gs"])
                    P.dma("sp", Gs[:, 1, :], k.GS[384:512, hp * 128:(hp + 1) * 128], reads=["GS"], writes=["Gs"])
                    tmpT = p4.sb([128, 32, 128], F32, "tmpT")
                    for d in range(2):
                        oth = 0 if d == 0 else 1
                        P.c("dve", (lambda e, d=d: e.tensor_tensor(out=S[d][:], in0=Gs[:, d, :], in1=Sc[d][:], op=ALU.subtract)), reads=["Gs", "Sc%d" % d], writes=["S%d" % d])
                        P.c("dve", (lambda e, d=d, oth=oth: e.tensor_scalar_mul(out=S[d][:], in0=S[d][:], scalar1=k.hm[:, oth:oth + 1])), reads=["S%d" % d, "hm"], writes=["S%d" % d])
                        P.c("dve", (lambda e, d=d: e.tensor_tensor(out=tmpT[:], in0=Dc[:, d, :].unsqueeze(2).to_broadcast([128, 32, 128]),
                                                                  in1=S[d][:].unsqueeze(1).to_broadcast([128, 32, 128]), op=ALU.mult)), reads=["S%d" % d, "Dc"], writes=["tmpT"])
                        P.c("dve", (lambda e, d=d: e.tensor_tensor(out=SB[:, d, 0:32, :], in0=SB[:, d, 0:32, :], in1=tmpT[:], op=ALU.add)), reads=["tmpT", "SB%d" % d], writes=["SB%d" % d])
                    P.flush()
                if upto == "hgrn2":
                    continue
                with ExitStack() as es5:
                    p5 = Pool_(nc, es5)
                    psAB = [[p5.ps([64, 4, 2, 64], F32, "psAB") for _ in range(2)] for s_ in range(2)]
                    psOl = [p5.ps([64, 128], F32, "psO") for s_ in range(2)]
                    ptyl = [p5.ps([128, 64], BF, "hpty") for s_ in range(2)]
                    tA = [[p5.sb([64, 4, 2, 64], F32, "htA") for _ in range(2)] for s_ in range(2)]
                    AT = [[p5.sb([64, 2, 2, 64], BF, "hAT") for _ in range(2)] for s_ in range(2)]
                    on = [p5.sb([64, 2, 64], F32, "hon") for s_ in range(2)]
                    yb = [p5.sb([64, 2, 64], BF, "hyb") for s_ in range(2)]
                    KTm = p5.sb([128, 2, 2, TA], BF, "KTm")
                    KAm = p5.sb([128, 2, 2, TA], BF, "KAm")
                    for hl in range(2):
                        P.c("dve", (lambda e, hl=hl: e.tensor_scalar_mul(out=KTm[:, hl], in0=KT_[:], scalar1=pm[:, hl:hl + 1])), reads=["pm"], writes=["KTm%d" % hl])
                        P.c("dve", (lambda e, hl=hl: e.tensor_scalar_mul(out=KAm[:, hl], in0=KA[:], scalar1=pm[:, hl:hl + 1])), reads=["pm"], writes=["KAm%d" % hl])
                    P.c("dve", lambda e: e.tensor_tensor(out=SG[:].rearrange("p c (h v) -> p c h v", h=2), in0=SG[:].rearrange("p c (h v) -> p c h v", h=2),
                                                         in1=G[0:64, l, 704:768].unsqueeze(1).unsqueeze(1).to_broadcast([64, NCH, 2, 64]), op=ALU.mult),
                        reads=["gains"], writes=["SGg"])

                    def A_mm(c):
                        st, sl = c % 2, (c // 2) % 2
                        pa, ta_, at_ = psAB[st][sl], tA[st][sl], AT[st][sl]
                        kk_ = "%d_%d" % (st, sl)
                        for d in range(2):
                            for hl in range(2):
                                P.c("pe", (lambda e, d=d, hl=hl: e.matmul(
                                    pa[:, d, hl, :], lhsT=KTm[:, hl, d, c * 64:(c + 1) * 64], rhs=QT_[:, d, c * 64:(c + 1) * 64],
                                    start=True, stop=True)), reads=["KTm%d" % hl], writes=["psAB" + kk_])
                                P.c("pe", (lambda e, d=d, hl=hl: e.matmul(
                                    pa[:, 2 + d, hl, :], lhsT=KAm[:, hl, d, c * 64:(c + 1) * 64], rhs=QE[:, d, c * 64:(c + 1) * 64],
                                    start=True, stop=True)), reads=["KAm%d" % hl], writes=["psAB" + kk_])
                        P.c("dve", (lambda e: e.tensor_tensor(out=ta_[:], in0=pa[:], in1=mk4[:].unsqueeze(2).to_broadcast([64, 4, 2, 64]), op=ALU.mult)),
                            reads=["psAB" + kk_, "mk4"], writes=["tA" + kk_])
                        P.c("dve", (lambda e: e.tensor_tensor(out=at_[:], in0=ta_[:, 0:2], in1=ta_[:, 2:4], op=ALU.add)),
                            reads=["tA" + kk_], writes=["AT" + kk_])

                    def O_mm(c):
                        st, sl = c % 2, (c // 2) % 2
                        at_ = AT[st][sl]
                        kk_ = "%d_%d" % (st, sl)
                        psO = psOl[st][:]
                        ok_ = "psO%d" % st
                        for d in range(2):
                            P.c("pe", (lambda e, d=d: e.matmul(
                                psO, lhsT=QE[:, d, c * 64:(c + 1) * 64], rhs=SB[:, d, c, :], start=(d == 0), stop=False)),
                                reads=[], writes=[ok_])
                        for hl in range(2):
                            for d in range(2):
                                P.c("pe", (lambda e, d=d, hl=hl: e.matmul(
                                    psOl[st][:, hl * 64:(hl + 1) * 64], lhsT=at_[:, d, hl, :], rhs=IT[:, c, hl * 64:(hl + 1) * 64],
                                    start=False, stop=(hl == 1 and d == 1))), reads=["AT" + kk_], writes=[ok_])
                        tg = "ho%d" % st
                        hn_small(P, p5, tg, psO.rearrange("p (h v) -> p h v", h=2), None, on[st], c, [ok_], [tg + "on"])
                        P.c("dve", (lambda e: e.tensor_tensor(out=yb[st][:], in0=on[st][:], in1=SG[:, c, :].rearrange("p (h v) -> p h v", h=2), op=ALU.mult)),
                            reads=[tg + "on", "SGg"], writes=[tg + "yb"])
                        P.c("pe", (lambda e: e.transpose(ptyl[st][:], yb[st][:].rearrange("p h v -> p (h v)"), k.ident[0:64, 0:64])),
                            reads=[tg + "yb", "ident"], writes=["hpty%d" % st])
                        P.c("act", (lambda e: e.copy(out=k.YT[:, 6 + hp, c * 64:(c + 1) * 64], in_=ptyl[st][:])), reads=["hpty%d" % st], writes=["YT%d" % st])

                    def ostream(st):
                        cs = list(range(st, nch_out, 2))
                        for j, c in enumerate(cs):
                            A_mm(c)
                            if j > 0:
                                O_mm(cs[j - 1])
                        O_mm(cs[-1])

                    P.record_streams([lambda: ostream(0), lambda: ostream(1)])
                    P.flush()


def psO_dbg(k, pl, ps, key, c):
    t = pl.sb([64, 128], F32, "dbgO")
    k.P.c("dve", lambda e: e.tensor_copy(out=t[:], in_=ps[:]), reads=[key], writes=["dbgO%d" % c])
    return t[:]


_HN = {}


def hn_small(P, pl, tag, src, gain_bc, out, c, rk, wk):
    cache = pl.__dict__.setdefault("_hn", {})
    if tag not in cache:
        cache[tag] = (pl.sb([64, 2, 64], F32, tag + "sq"), pl.sb([64, 2], F32, tag + "ssh"))
    sq, ssh = cache[tag]
    P.c("act", lambda e: e.activation(out=sq[:], in_=src, func=AF.Square), reads=rk, writes=[tag + "sq"])
    P.c("dve", lambda e: e.tensor_reduce(out=ssh[:], in_=sq[:], axis=AX.X, op=ALU.add), reads=[tag + "sq"], writes=[tag + "ssh"])
    rsqrt(P, ssh[:], ssh[:], 1.0 / 64, EPS, [tag + "ssh"], [tag + "ssh"])
    if gain_bc is None:
        P.c("dve", lambda e: e.tensor_tensor(out=out[:], in0=src, in1=ssh[:].unsqueeze(2).to_broadcast([64, 2, 64]), op=ALU.mult),
            reads=rk + [tag + "ssh"], writes=wk)
        return
    P.c("dve", lambda e: e.tensor_tensor(out=sq[:], in0=src, in1=ssh[:].unsqueeze(2).to_broadcast([64, 2, 64]), op=ALU.mult),
        reads=rk + [tag + "ssh"], writes=[tag + "sq"])
    P.c("dve", lambda e: e.tensor_tensor(out=out[:], in0=sq[:], in1=gain_bc, op=ALU.mult), reads=[tag + "sq", "gains"], writes=wk)


def stage_merge(k, l, xsrc, csrc):
    nc, P, I = k.nc, k.P, k.I
    last = (l == DEPTH - 1)
    ntok = T if last else TA
    with ExitStack() as es:
        pl = Pool_(nc, es)
        Wbr = pl.sb([128, 8, D], BF, "wbr")
        Wo = pl.sb([128, 8, D], BF, "wo")
        Wg = [pl.sb([128, 8, 512], BF, "wg") for _ in range(2)]
        wv = I["w_in"][l].rearrange("(kc p) n -> p kc n", p=128)
        HTa = pl.sb([128, 8, ntok], BF, "HTa")
        hv = k.HTd.rearrange("(kc p) t -> p kc t", p=128)
        hkeys = []
        for t0 in range(0, ntok, 512):
            tw = min(512, ntok - t0)
            P.dma("sp", HTa[:, :, t0:t0 + tw], hv[:, :, t0:t0 + tw], writes=["HTa%d" % t0])
            hkeys.append("HTa%d" % t0)

        def load_wg(fc):
            wg = Wg[fc % 2]
            for n in range(4):
                c = NCOL + n * D + fc * 128
                P.dma("pool", wg[:, :, n * 128:(n + 1) * 128], wv[:, :, c:c + 128], writes=["wg%d_%d" % (fc % 2, n)])
        load_wg(0)
        P.dma("pool", Wbr[:], I["w_branch"][l].rearrange("n (j p) d -> p (n j) d", p=128), writes=["Wbr"])
        load_wg(1)
        P.dma("pool", Wo[:], I["w_out"][l].rearrange("(kc p) d -> p kc d", p=128), writes=["Wo"])
        grow = pl.sb([128, 2, D], F32, "grow")
        for s_ in range(2):
            src = bass.AP(k.MODROW.tensor, k.MODROW[l, 32 + s_, :].offset, [[0, 128], [256, 8], [1, 128]])
            P.dma("sp", grow[:, s_, :].rearrange("p (j f) -> p j f", f=128), src, writes=["grow"])
        ST = pl.sb([128, 8, ntok], BF, "ST")
        acc = pl.sb([128, 512], F32, "acc")
        sg = [pl.sb([128, 512], F32, "sg") for _ in range(2)]
        tmp = [pl.sb([128, 512], F32, "mt") for _ in range(2)]
        psg = [pl.ps([128, 512]) for _ in range(2)]
        psp = [pl.ps([128, 512]) for _ in range(2)]
        psy = [pl.ps([128, 512]) for _ in range(2)]
        xt = [pl.sb([128, D], F32, "mx") for _ in range(2)]
        cnt = 0
        nx = 0
        for fc in range(8):
            wg = Wg[fc % 2]
            if fc >= 2:
                load_wg(fc)
            for t0 in range(0, ntok, 512):
                tw = min(512, ntok - t0)
                hk = "HTa%d" % t0
                for n in range(4):
                    b = cnt % 2
                    cnt += 1
                    wk = "wg%d_%d" % (fc % 2, n)
                    for kc in range(8):
                        P.c("pe", (lambda e, b=b, kc=kc, n=n, wg=wg, t0=t0, tw=tw: e.matmul(
                            psg[b][:, 0:tw], lhsT=wg[:, kc, n * 128:(n + 1) * 128], rhs=HTa[:, kc, t0:t0 + tw],
                            start=(kc == 0), stop=(kc == 7))), reads=[wk, hk], writes=["psg%d" % b])
                    for j in range(2):
                        P.c("pe", (lambda e, b=b, j=j, n=n, fc=fc, t0=t0, tw=tw: e.matmul(
                            psp[b][:, 0:tw], lhsT=Wbr[:, n * 2 + j, fc * 128:(fc + 1) * 128], rhs=k.YT[:, n * 2 + j, t0:t0 + tw],
                            start=(j == 0), stop=(j == 1))), reads=["Wbr", "YT"], writes=["psp%d" % b])
                    P.c("act", (lambda e, b=b, tw=tw: e.activation(out=sg[b][:, 0:tw], in_=psg[b][:, 0:tw], func=AF.Sigmoid)),
                        reads=["psg%d" % b], writes=["sg%d" % b])
                    if n == 0:
                        P.c("dve", (lambda e, b=b, tw=tw: e.tensor_tensor(out=acc[:, 0:tw], in0=sg[b][:, 0:tw], in1=psp[b][:, 0:tw], op=ALU.mult)),
                            reads=["sg%d" % b, "psp%d" % b], writes=["acc"])
                    else:
                        P.c("dve", (lambda e, b=b, tw=tw: e.tensor_tensor(out=tmp[b][:, 0:tw], in0=sg[b][:, 0:tw], in1=psp[b][:, 0:tw], op=ALU.mult)),
                            reads=["sg%d" % b, "psp%d" % b], writes=["mt%d" % b])
                        if n < 3:
                            P.c("dve", (lambda e, b=b, tw=tw: e.tensor_tensor(out=acc[:, 0:tw], in0=acc[:, 0:tw], in1=tmp[b][:, 0:tw], op=ALU.add)),
                                reads=["acc", "mt%d" % b], writes=["acc"])
                        else:
                            P.c("dve", (lambda e, b=b, tw=tw, fc=fc, t0=t0: e.tensor_tensor(out=ST[:, fc, t0:t0 + tw], in0=acc[:, 0:tw], in1=tmp[b][:, 0:tw], op=ALU.add)),
                                reads=["acc", "mt%d" % b], writes=["ST%d_%d" % (fc, t0)])
        for tok0 in range(0, ntok, 128):
            t0 = (tok0 // 512) * 512
            s_ = 0 if tok0 < T else 1
            X = xt[nx % 2]
            xk = "mx%d" % (nx % 2)
            nx += 1
            src = xsrc[tok0:tok0 + 128, :] if tok0 < T else csrc[tok0 - T:tok0 - T + 128, :]
            P.dma("sp", X[:], src, writes=[xk])
            for hf in range(2):
                pb = psy[hf]
                for fc in range(8):
                    P.c("pe", (lambda e, pb=pb, fc=fc, tok0=tok0, hf=hf: e.matmul(
                        pb[:], lhsT=ST[:, fc, tok0:tok0 + 128], rhs=Wo[:, fc, hf * 512:(hf + 1) * 512],
                        start=(fc == 0), stop=(fc == 7))), reads=["ST%d_%d" % (fc, t0), "Wo"], writes=["psy%d" % hf])
                P.c("dve", (lambda e, pb=pb, hf=hf, s_=s_: e.tensor_tensor(out=tmp[hf][:], in0=pb[:], in1=grow[:, s_, hf * 512:(hf + 1) * 512], op=ALU.mult)),
                    reads=["psy%d" % hf, "grow"], writes=["mt%d" % hf])
                P.c("dve", (lambda e, X=X, hf=hf: e.tensor_tensor(out=X[:, hf * 512:(hf + 1) * 512], in0=X[:, hf * 512:(hf + 1) * 512], in1=tmp[hf][:], op=ALU.add)),
                    reads=["mt%d" % hf, xk], writes=[xk])
            P.dma("sp", k.XM[l][tok0:tok0 + 128, :], X[:], reads=[xk], writes=["XM"])
        P.flush()


def ffn_weights(k, pl, l):
    W1 = pl.sb([128, 8, 4 * D], BF, "w1")
    W2 = pl.sb([128, 32, D], BF, "w2")
    w1v = k.I["w_ff1"][l].rearrange("(kc p) n -> p kc n", p=128)
    w2v = k.I["w_ff2"][l].rearrange("(kc p) n -> p kc n", p=128)
    for q_ in range(8):
        k.P.dma("pool", W1[:, :, q_ * 512:(q_ + 1) * 512], w1v[:, :, q_ * 512:(q_ + 1) * 512], writes=["W1_%d" % q_], carry=True)
        k.P.dma("pool", W2[:, q_ * 4:(q_ + 1) * 4, :], w2v[:, q_ * 4:(q_ + 1) * 4, :], writes=["W2_%d" % q_], carry=True)
    return W1, W2


def stage_ffn(k, l, Wff=None):
    nc, P, I = k.nc, k.P, k.I
    last = (l == DEPTH - 1)
    ntok = T if last else TA
    with ExitStack() as es:
        pl = Pool_(nc, es)
        W1, W2 = Wff if Wff is not None else ffn_weights(k, pl, l)
        grow = pl.sb([128, 2, D], F32, "grow2")
        for s_ in range(2):
            src = bass.AP(k.MODROW.tensor, k.MODROW[l, 80 + s_, :].offset, [[0, 128], [256, 8], [1, 128]])
            P.dma("sp", grow[:, s_, :].rearrange("p (j f) -> p j f", f=128), src, writes=["grow2"])
        psh = [pl.ps([128, 256]) for _ in range(2)]
        pso = [pl.ps([128, 512]) for _ in range(4)]
        r1 = [pl.sb([128, 256], F32, "r1") for _ in range(2)]
        hid = [pl.sb([128, 256], BF, "hid") for _ in range(2)]
        xt = [pl.sb([128, D], F32, "fx") for _ in range(2)]
        tmp = [pl.sb([128, 512], F32, "ft") for _ in range(2)]
        nx = 0
        nh = 0
        for t0 in range(0, ntok, 256):
            def w2_mm(hc, b):
                for ti in range(2):
                    for hf in range(2):
                        P.c("pe", (lambda e, b=b, ti=ti, hf=hf, hc=hc: e.matmul(
                            pso[ti * 2 + hf][:], lhsT=hid[b][:, ti * 128:(ti + 1) * 128], rhs=W2[:, hc, hf * 512:(hf + 1) * 512],
                            start=(hc == 0), stop=(hc == 31))), reads=["hid%d" % b, "W2_%d" % (hc // 4)], writes=["pso%d" % (ti * 2 + hf)])
            prev = None
            for hc in range(32):
                b = nh % 2
                nh += 1
                for kc in range(8):
                    P.c("pe", (lambda e, b=b, kc=kc, hc=hc, t0=t0: e.matmul(
                        psh[b][:], lhsT=W1[:, kc, hc * 128:(hc + 1) * 128], rhs=k.HT[:, kc, t0:t0 + 256],
                        start=(kc == 0), stop=(kc == 7))), reads=["W1_%d" % (hc // 4), "HT"], writes=["psh%d" % b])
                if prev is not None:
                    w2_mm(*prev)
                P.c("act", (lambda e, b=b: e.activation(out=r1[b][:], in_=psh[b][:], func=AF.Relu)), reads=["psh%d" % b], writes=["r1%d" % b])
                P.c("dve", (lambda e, b=b: e.tensor_tensor(out=hid[b][:], in0=r1[b][:], in1=r1[b][:], op=ALU.mult)), reads=["r1%d" % b], writes=["hid%d" % b])
                prev = (hc, b)
            w2_mm(*prev)
            for ti in range(2):
                tok0 = t0 + ti * 128
                s_ = 0 if tok0 < T else 1
                X = xt[nx % 2]
                xk = "fx%d" % (nx % 2)
                nx += 1
                P.dma("sp", X[:], k.XM[l][tok0:tok0 + 128, :], writes=[xk])
                for hf in range(2):
                    pb = pso[ti * 2 + hf]
                    P.c("dve", (lambda e, pb=pb, hf=hf, s_=s_: e.tensor_tensor(out=tmp[hf][:], in0=pb[:], in1=grow[:, s_, hf * 512:(hf + 1) * 512], op=ALU.mult)),
                        reads=["pso%d" % (ti * 2 + hf), "grow2"], writes=["ft%d" % hf])
                    P.c("dve", (lambda e, X=X, hf=hf: e.tensor_tensor(out=X[:, hf * 512:(hf + 1) * 512], in0=X[:, hf * 512:(hf + 1) * 512], in1=tmp[hf][:], op=ALU.add)),
                        reads=["ft%d" % hf, xk], writes=[xk])
                dst = k.out[tok0:tok0 + 128, :] if last else k.XL[0][tok0:tok0 + 128, :]
                P.dma("sp", dst, X[:], reads=[xk], writes=["XLout"])
        P.flush()


def _rope_tab(pos):
    r, c = pos // 64, pos % 64
    inv = (10000.0 ** (-np.arange(8, dtype=np.float32) / 8)).astype(np.float32)
    ar = r[:, None].astype(np.float32) * inv
    ac = c[:, None].astype(np.float32) * inv
    return np.concatenate([np.cos(ar), np.cos(ac), np.sin(ar), np.sin(ac)], axis=1).astype(np.float32)


def _na_index(half):
    idx = -np.ones((16, 128, 6, 128), np.int64)
    kp = np.arange(128)
    q = np.arange(128)
    for p in range(16):
        t0 = 0 if p == 0 else (14 if p == 15 else p)
        nl = 6 if p in (0, 15) else 5
        for c in range(nl):
            kl = 2 * (t0 + c) + kp // 64
            kc = kp % 64
            kr = 32 * half - 4 + kl
            r = 32 * half + 2 * p + q // 64
            cq = q % 64
            rs = np.clip(r - 4, 0, 56)
            cs = np.clip(cq - 8, 0, 48)
            okr = (kr[:, None] >= rs[None, :]) & (kr[:, None] < rs[None, :] + 8) & (kr[:, None] >= 0) & (kr[:, None] < 64)
            okc = (kc[:, None] >= cs[None, :]) & (kc[:, None] < cs[None, :] + 16)
            dr = kr[:, None] - r[None, :] + 7
            dc = kc[:, None] - cq[None, :] + 15
            ok = okr & okc
            v = np.where(ok, dr * 31 + dc, -1)
            idx[p, :, c, :] = v
    return idx


def prep(inp, ncores=8):
    f = np.float32
    g = {}
    for n_ in ("ada_w", "w_in", "w_branch", "w_out", "w_ff1", "w_ff2", "pool_w", "mla_wq_b", "mla_wkv_b"):
        g[n_] = np.ascontiguousarray(inp[n_], f)
    g["ada_bT"] = np.ascontiguousarray(inp["ada_b"].reshape(DEPTH, 48, 128).transpose(2, 0, 1), f)
    nrm = np.stack([inp["norm1"], inp["norm2"]], axis=1)
    g["normT"] = np.ascontiguousarray(nrm.reshape(DEPTH, 2, 8, 128).transpose(3, 0, 1, 2), f)
    g["pool_scT"] = np.ascontiguousarray(inp["pool_scale"].reshape(DEPTH, 2, 128).transpose(2, 0, 1), f)
    gn = np.zeros((DEPTH, 800), f)
    for l in range(DEPTH):
        gn[l, 0:256] = inp["mla_q_norm"][l]
        gn[l, 256:384] = inp["mla_kv_norm"][l]
        gn[l, 384:480] = inp["mla_gq"][l]
        gn[l, 480:576] = inp["mla_gk"][l]
        gn[l, 576:640] = inp["na_gq"][l]
        gn[l, 640:704] = inp["na_gk"][l]
        gn[l, 704:768] = inp["hg_norm"][l]
    g["gains"] = np.ascontiguousarray(np.broadcast_to(gn[None], (128, DEPTH, 800)), f)
    g["ropek"] = _rope_tab(np.arange(L))
    lb = inp["hg_lb"].reshape(DEPTH + 1, 2, 2, 2, 64)
    g["hg_lbT"] = np.ascontiguousarray(lb.transpose(3, 4, 0, 1, 2).reshape(128, DEPTH + 1, 2, 2), f)
    def invcnt(t, n):
        out = np.zeros((4, len(t)), f)
        for gi, w in enumerate((2, 4, 8, 16)):
            lo = np.clip(t - w // 2, 0, n)
            hi = np.clip(t - w // 2 + w, 0, n)
            out[gi] = 1.0 / (hi - lo)
        return out
    def to_pt(a):
        return np.ascontiguousarray(np.repeat(a.reshape(2, 2, 1, -1), 64, axis=2).reshape(2, 128, -1).transpose(1, 0, 2), f)
    g["pinvc"] = to_pt(invcnt(np.arange(TC), TC))
    rpb_ext = [np.concatenate([inp["na_rpb"][l].reshape(4, -1), np.full((4, 1), NEG, f)], axis=1) for l in range(DEPTH)]
    maps = []
    for core in range(ncores):
        b, half = core // 2, core % 2
        m = dict(g)
        m["x"] = np.ascontiguousarray(inp["x"][b, half * T:(half + 1) * T], f)
        m["ctx"] = np.ascontiguousarray(inp["ctx"][b], f)
        cc = np.stack([inp["c"][b], inp["c_ctx"]], axis=-1)
        m["cT"] = np.ascontiguousarray(cc.reshape(8, 128, 2).transpose(1, 0, 2), f)
        m["pinv"] = to_pt(invcnt(np.arange(half * T, (half + 1) * T), L))
        m["hm"] = np.ascontiguousarray(np.broadcast_to(np.array([half, 1 - half], f), (128, 2)))
        m["ropeq"] = np.ascontiguousarray(g["ropek"][half * T:(half + 1) * T])
        idx = _na_index(half)
        nab = np.stack([rpb_ext[l][:, idx] for l in range(DEPTH)])
        m["nab"] = np.ascontiguousarray(nab.transpose(0, 2, 3, 4, 1, 5), f)
        maps.append(m)
    return maps


def kernel(**inputs):
    inp = {k_: np.asarray(v) for k_, v in inputs.items()}
    k = build(ncores=8)
    maps = prep(inp, 8)
    names = list(k.I.keys())
    maps = [{n: m[n] for n in names} for m in maps]
    res = run_bass_kernel_spmd(k.nc, maps, core_ids=list(range(8)))
    out = np.zeros((4, L, D), np.float32)
    for core in range(8):
        b, half = core // 2, core % 2
        out[b, half * T:(half + 1) * T] = np.asarray(res.results[core]["out"], np.float32)
    return out
```
